# Optimizing a Trainium2 kernel written in Bass

```python
import math
import jax, jax.numpy as jnp
from jax import lax
import numpy as np

D_MODEL = 1024
BATCH = 4
SEQ = 4096
DEPTH = 2

GRID_W = 64
CTX_LEN = 256
N_GROUPS = 4
BRANCH_W = D_MODEL // N_GROUPS
D_MIX = N_GROUPS * BRANCH_W
HEAD_DIM = 64
EPS = 1e-6
ROPE_BASE = 10000.0
GLA_HEADS = BRANCH_W // HEAD_DIM
GLA_DV = HEAD_DIM
GLA_DK = HEAD_DIM // 2
GLA_RANK = 16
GLA_TAU = 16.0
GLA_CHUNK = 64
FNET_GROUPS = 4
FNET_GW = BRANCH_W // FNET_GROUPS
SWA_HEADS = BRANCH_W // HEAD_DIM
SWA_KV_HEADS = 2
SWA_GROUP = SWA_HEADS // SWA_KV_HEADS
SWA_WINDOW = 128
SWA_BLOCK = 128
NA_HEADS = BRANCH_W // HEAD_DIM
NA_KH_MAX = 8
NA_KW = 16

PROJ_SPLITS = (
    ("a_q", GLA_HEADS * GLA_DK), ("a_k", GLA_HEADS * GLA_DK), ("a_v", GLA_HEADS * GLA_DV),
    ("a_g", BRANCH_W), ("a_lr", 2 * GLA_RANK),
    ("b_v", BRANCH_W), ("b_g", BRANCH_W),
    ("c_q", SWA_HEADS * HEAD_DIM), ("c_k", SWA_KV_HEADS * HEAD_DIM), ("c_v", SWA_KV_HEADS * HEAD_DIM),
    ("c_g", BRANCH_W),
    ("d_q", BRANCH_W), ("d_k", BRANCH_W), ("d_v", BRANCH_W), ("d_g", BRANCH_W),
)
PROJ_WIDTH = (4 * GLA_HEADS * GLA_DK + 2 * GLA_RANK + 2 * BRANCH_W
              + 2 * BRANCH_W
              + 2 * BRANCH_W + 2 * SWA_KV_HEADS * HEAD_DIM
              + 4 * BRANCH_W)

kernel_name = "hybrid_parallel_group_flow_trunk"


def rms_norm(x, w):
    xf = x.astype(jnp.float32)
    y = xf * lax.rsqrt(jnp.mean(xf * xf, axis=-1, keepdims=True) + EPS)
    return (y * w).astype(x.dtype)


def split_proj(p):
    out, off = {}, 0
    for name, w in PROJ_SPLITS:
        out[name] = p[..., off:off + w]
        off += w
    return out


def to_heads(t, n_heads):
    b_, L, w = t.shape
    return t.reshape(b_, L, n_heads, w // n_heads).transpose(0, 2, 1, 3)


def from_heads(t):
    b_, h, L, d = t.shape
    return t.transpose(0, 2, 1, 3).reshape(b_, L, h * d)


def axial_rope_tables(n):
    t = jnp.arange(n)
    row = (t // GRID_W).astype(jnp.float32)
    col = (t % GRID_W).astype(jnp.float32)
    axis_dim = HEAD_DIM // 2
    inv = ROPE_BASE ** (-jnp.arange(0, axis_dim, 2, dtype=jnp.float32) / axis_dim)
    ang = jnp.stack([row[:, None] * inv, col[:, None] * inv], axis=1)
    return jnp.cos(ang), jnp.sin(ang)


def apply_rope(x, cos, sin):
    shp = x.shape
    xr = x.astype(jnp.float32).reshape(shp[:-1] + (2, 2, HEAD_DIM // 4))
    x1, x2 = xr[..., 0, :], xr[..., 1, :]
    o1 = x1 * cos - x2 * sin
    o2 = x2 * cos + x1 * sin
    return jnp.stack([o1, o2], axis=-2).reshape(shp).astype(x.dtype)


def gla_chunk_scan(q, k, v, log_a, s0):
    b_, h_, L, _ = q.shape
    nc = L // GLA_CHUNK

    def to_chunks(t):
        t = t.astype(jnp.float32).reshape(b_, h_, nc, GLA_CHUNK, t.shape[-1])
        return jnp.moveaxis(t, 2, 0)

    causal = jnp.tril(jnp.ones((GLA_CHUNK, GLA_CHUNK), bool))

    def step(s, inp):
        qc, kc, vc, ac = inp
        cum = jnp.cumsum(ac, axis=2)
        inter = jnp.einsum('bhtd,bhde->bhte', qc * jnp.exp(cum), s)
        diff = cum[:, :, :, None, :] - cum[:, :, None, :, :]
        decay = jnp.exp(jnp.where(causal[:, :, None], diff, -jnp.inf))
        scores = jnp.einsum('bhtd,bhsd,bhtsd->bhts', qc, kc, decay)
        intra = jnp.einsum('bhts,bhse->bhte', scores, vc)
        last = cum[:, :, -1:, :]
        s_new = (jnp.exp(last[:, :, 0, :])[..., None] * s
                 + jnp.einsum('bhsd,bhse->bhde', kc * jnp.exp(last - cum), vc))
        return s_new, inter + intra

    s_fin, out = lax.scan(step, s0, (to_chunks(q), to_chunks(k), to_chunks(v), to_chunks(log_a)))
    out = jnp.moveaxis(out, 0, 2).reshape(b_, h_, L, v.shape[-1])
    return out, s_fin


def gla_mixer(pl, pc, dec_w, dec_b, out_norm, need_ctx):
    def prep(p):
        q = to_heads(p['a_q'], GLA_HEADS) * (GLA_DK ** -0.5)
        k = to_heads(p['a_k'], GLA_HEADS)
        v = to_heads(p['a_v'], GLA_HEADS)
        lr = p['a_lr']
        log_a = [to_heads(jax.nn.log_sigmoid(
            (lr[..., d * GLA_RANK:(d + 1) * GLA_RANK] @ dec_w[d] + dec_b[d]).astype(jnp.float32)) / GLA_TAU,
            GLA_HEADS) for d in range(2)]
        return q, k, v, log_a

    ql, kl, vl, al = prep(pl)
    qc, kc, vc, ac = prep(pc)
    b_ = ql.shape[0]
    s0 = jnp.zeros((b_, GLA_HEADS, GLA_DK, GLA_DV), jnp.float32)
    out_l = jnp.zeros(vl.shape, jnp.float32)
    out_c = jnp.zeros(vc.shape, jnp.float32)
    for d in range(2):
        flip = (lambda t: t) if d == 0 else (lambda t: jnp.flip(t, axis=2))
        o_c, s_c = gla_chunk_scan(flip(qc), flip(kc), flip(vc), flip(ac[d]), s0)
        o_l, _ = gla_chunk_scan(flip(ql), flip(kl), flip(vl), flip(al[d]), s_c)
        out_l = out_l + flip(o_l)
        if need_ctx:
            out_c = out_c + flip(o_c)
    y_l = from_heads(rms_norm(out_l, out_norm)).astype(pl['a_v'].dtype)
    y_c = from_heads(rms_norm(out_c, out_norm)).astype(pc['a_v'].dtype) if need_ctx else None
    return y_l, y_c


def fourier_mix(v, w_f):
    b_, L, _ = v.shape
    vg = v.astype(jnp.float32).reshape(b_, L, FNET_GROUPS, FNET_GW)
    f = jnp.fft.fft2(vg, axes=(1, 3), norm='ortho').real
    return (f.reshape(b_, L, BRANCH_W) @ w_f).astype(v.dtype)


def swa_mixer(pl, pc, q_norm, k_norm, sink, cos, sin, need_ctx):
    b_, n, _ = pl['c_q'].shape
    scale = HEAD_DIM ** -0.5
    nb = n // SWA_BLOCK
    q = apply_rope(rms_norm(to_heads(pl['c_q'], SWA_HEADS), q_norm), cos, sin)
    k = apply_rope(rms_norm(to_heads(pl['c_k'], SWA_KV_HEADS), k_norm), cos, sin)
    v = to_heads(pl['c_v'], SWA_KV_HEADS)
    qcx = rms_norm(to_heads(pc['c_q'], SWA_HEADS), q_norm)
    kcx = rms_norm(to_heads(pc['c_k'], SWA_KV_HEADS), k_norm)
    vcx = to_heads(pc['c_v'], SWA_KV_HEADS)

    def band(t):
        tp = jnp.pad(t, ((0, 0), (0, 0), (SWA_BLOCK, SWA_BLOCK), (0, 0)))
        tb = tp.reshape(b_, SWA_KV_HEADS, nb + 2, SWA_BLOCK, HEAD_DIM)
        return jnp.concatenate([tb[:, :, :-2], tb[:, :, 1:-1], tb[:, :, 2:]], axis=3)

    k_win, v_win = band(k), band(v)
    qb = q.reshape(b_, SWA_KV_HEADS, SWA_GROUP, nb, SWA_BLOCK, HEAD_DIM)
    s_loc = jnp.einsum('bkgnqd,bknjd->bkgnqj', qb, k_win).astype(jnp.float32) * scale
    qpos = jnp.arange(nb)[:, None] * SWA_BLOCK + jnp.arange(SWA_BLOCK)[None, :]
    kpos = (jnp.arange(nb)[:, None] - 1) * SWA_BLOCK + jnp.arange(3 * SWA_BLOCK)[None, :]
    valid = ((jnp.abs(qpos[:, :, None] - kpos[:, None, :]) <= SWA_WINDOW)
             & (kpos[:, None, :] >= 0) & (kpos[:, None, :] < n))
    s_loc = jnp.where(valid, s_loc, -jnp.inf)
    s_ctx = jnp.einsum('bkgnqd,bkcd->bkgnqc', qb, kcx).astype(jnp.float32) * scale
    sink_l = jnp.broadcast_to(sink.astype(jnp.float32).reshape(1, SWA_KV_HEADS, SWA_GROUP, 1, 1, 1),
                              s_loc.shape[:-1] + (1,))
    p = jax.nn.softmax(jnp.concatenate([s_loc, s_ctx, sink_l], axis=-1), axis=-1).astype(v.dtype)
    nw = 3 * SWA_BLOCK
    o = (jnp.einsum('bkgnqj,bknjd->bkgnqd', p[..., :nw], v_win)
         + jnp.einsum('bkgnqc,bkcd->bkgnqd', p[..., nw:nw + CTX_LEN], vcx))
    y_l = from_heads(o.reshape(b_, SWA_HEADS, n, HEAD_DIM))
    y_c = None
    if need_ctx:
        qcb = qcx.reshape(b_, SWA_KV_HEADS, SWA_GROUP, CTX_LEN, HEAD_DIM)
        s_cc = jnp.einsum('bkgqd,bkcd->bkgqc', qcb, kcx).astype(jnp.float32) * scale
        sink_c = jnp.broadcast_to(sink.astype(jnp.float32).reshape(1, SWA_KV_HEADS, SWA_GROUP, 1, 1),
                                  s_cc.shape[:-1] + (1,))
        pc_ = jax.nn.softmax(jnp.concatenate([s_cc, sink_c], axis=-1), axis=-1).astype(v.dtype)
        o_c = jnp.einsum('bkgqc,bkcd->bkgqd', pc_[..., :CTX_LEN], vcx)
        y_c = from_heads(o_c.reshape(b_, SWA_HEADS, CTX_LEN, HEAD_DIM))
    return y_l, y_c


def na_mixer(pl, pc, q_norm, k_norm, rel_bias, need_ctx):
    b_, n, _ = pl['d_q'].shape
    rows = n // GRID_W
    kh = min(NA_KH_MAX, rows)
    scale = HEAD_DIM ** -0.5
    q = rms_norm(to_heads(pl['d_q'], NA_HEADS), q_norm)
    k = rms_norm(to_heads(pl['d_k'], NA_HEADS), k_norm)
    v = to_heads(pl['d_v'], NA_HEADS)
    qcx = rms_norm(to_heads(pc['d_q'], NA_HEADS), q_norm)
    kcx = rms_norm(to_heads(pc['d_k'], NA_HEADS), k_norm)
    vcx = to_heads(pc['d_v'], NA_HEADS)

    grid = lambda t: t.reshape(b_, NA_HEADS, rows, GRID_W, HEAD_DIM)
    qg, kg, vg = grid(q), grid(k), grid(v)
    r = jnp.arange(rows)
    row_start = jnp.clip(r - kh // 2, 0, rows - kh)
    row_idx = row_start[:, None] + jnp.arange(kh)[None, :]
    k_rows = kg[:, :, row_idx]
    v_rows = vg[:, :, row_idx]
    s_nb = jnp.einsum('bhrqd,bhrkwd->bhrqkw', qg, k_rows).astype(jnp.float32) * scale
    cq = jnp.arange(GRID_W)
    col_start = jnp.clip(cq - NA_KW // 2, 0, GRID_W - NA_KW)
    col_ok = (cq[None, :] >= col_start[:, None]) & (cq[None, :] < col_start[:, None] + NA_KW)
    dy = row_idx - r[:, None] + (NA_KH_MAX - 1)
    dx = jnp.clip(cq[None, :] - cq[:, None], -(NA_KW - 1), NA_KW - 1) + (NA_KW - 1)
    bias = rel_bias[:, dy[:, None, :, None], dx[None, :, None, :]].astype(jnp.float32)
    s_nb = jnp.where(col_ok[:, None, :], s_nb + bias, -jnp.inf)
    s_nb = s_nb.reshape(b_, NA_HEADS, rows, GRID_W, kh * GRID_W)
    s_ctx = jnp.einsum('bhrqd,bhcd->bhrqc', qg, kcx).astype(jnp.float32) * scale
    p = jax.nn.softmax(jnp.concatenate([s_nb, s_ctx], axis=-1), axis=-1).astype(v.dtype)
    p_nb = p[..., :kh * GRID_W].reshape(b_, NA_HEADS, rows, GRID_W, kh, GRID_W)
    o = (jnp.einsum('bhrqkw,bhrkwd->bhrqd', p_nb, v_rows)
         + jnp.einsum('bhrqc,bhcd->bhrqd', p[..., kh * GRID_W:], vcx))
    y_l = from_heads(o.reshape(b_, NA_HEADS, n, HEAD_DIM))
    y_c = None
    if need_ctx:
        s_cc = jnp.einsum('bhqd,bhcd->bhqc', qcx, kcx).astype(jnp.float32) * scale
        o_c = jnp.einsum('bhqc,bhcd->bhqd', jax.nn.softmax(s_cc, axis=-1).astype(v.dtype), vcx)
        y_c = from_heads(o_c)
    return y_l, y_c


def hybrid_layer(x, ctx, c, c_ctx, norm_w, ada_w, ada_b, w_in, w_out, gla_dec_w, gla_dec_b, gla_out_norm,
                 fnet_w, swa_q_norm, swa_k_norm, swa_sink, na_q_norm, na_k_norm, na_rel_bias,
                 cos, sin, need_ctx):
    shift_l, scale_l, gate_l = jnp.split(jax.nn.silu(c) @ ada_w + ada_b, 3, axis=-1)
    shift_c, scale_c, gate_c = jnp.split(jax.nn.silu(c_ctx) @ ada_w + ada_b, 3, axis=-1)
    h_l = rms_norm(x, norm_w) * (1.0 + scale_l[:, None, :]) + shift_l[:, None, :]
    h_c = rms_norm(ctx, norm_w) * (1.0 + scale_c) + shift_c
    pl = split_proj(h_l @ w_in)
    pc = split_proj(h_c @ w_in)

    a_l, a_c = gla_mixer(pl, pc, gla_dec_w, gla_dec_b, gla_out_norm, need_ctx)
    b_l = fourier_mix(pl['b_v'], fnet_w)
    c_l, c_c = swa_mixer(pl, pc, swa_q_norm, swa_k_norm, swa_sink, cos, sin, need_ctx)
    d_l, d_c = na_mixer(pl, pc, na_q_norm, na_k_norm, na_rel_bias, need_ctx)

    y_l = jnp.concatenate([a_l * jax.nn.silu(pl['a_g']), b_l * jax.nn.silu(pl['b_g']),
                           c_l * jax.nn.silu(pl['c_g']), d_l * jax.nn.silu(pl['d_g'])], axis=-1) @ w_out
    x = x + gate_l[:, None, :] * y_l
    if need_ctx:
        b_c = fourier_mix(pc['b_v'], fnet_w)
        y_c = jnp.concatenate([a_c * jax.nn.silu(pc['a_g']), b_c * jax.nn.silu(pc['b_g']),
                               c_c * jax.nn.silu(pc['c_g']), d_c * jax.nn.silu(pc['d_g'])], axis=-1) @ w_out
        ctx = ctx + gate_c * y_c
    return x, ctx


def setup_inputs(seed: int = 0) -> dict:
    key = jax.random.key(seed)
    ks = jax.random.split(key, 20)

    def nrm(k, shape, s):
        return jax.random.normal(k, shape, jnp.float32) * s

    return {
        "x": nrm(ks[0], (BATCH, SEQ, D_MODEL), 1.0),
        "c": nrm(ks[1], (BATCH, D_MODEL), 1.0),
        "ctx": nrm(ks[2], (BATCH, CTX_LEN, D_MODEL), 1.0),
        "c_ctx": nrm(ks[3], (D_MODEL,), 1.0),
        "norm_w": 1.0 + nrm(ks[4], (DEPTH, D_MODEL), 0.02),
        "ada_w": nrm(ks[5], (DEPTH, D_MODEL, 3 * D_MODEL), 0.5 * D_MODEL ** -0.5),
        "ada_b": nrm(ks[6], (DEPTH, 3 * D_MODEL), 0.02),
        "w_in": nrm(ks[7], (DEPTH, D_MODEL, PROJ_WIDTH), D_MODEL ** -0.5),
        "w_out": nrm(ks[8], (DEPTH, D_MIX, D_MODEL), D_MIX ** -0.5),
        "gla_dec_w": nrm(ks[9], (DEPTH, 2, GLA_RANK, GLA_HEADS * GLA_DK), GLA_RANK ** -0.5),
        "gla_dec_b": nrm(ks[10], (DEPTH, 2, GLA_HEADS * GLA_DK), 0.5),
        "gla_out_norm": 1.0 + nrm(ks[11], (DEPTH, GLA_DV), 0.02),
        "fnet_w": nrm(ks[12], (DEPTH, BRANCH_W, BRANCH_W), BRANCH_W ** -0.5),
        "swa_q_norm": 1.0 + nrm(ks[13], (DEPTH, HEAD_DIM), 0.02),
        "swa_k_norm": 1.0 + nrm(ks[14], (DEPTH, HEAD_DIM), 0.02),
        "swa_sink": nrm(ks[15], (DEPTH, SWA_HEADS), 0.5),
        "na_q_norm": 1.0 + nrm(ks[16], (DEPTH, HEAD_DIM), 0.02),
        "na_k_norm": 1.0 + nrm(ks[17], (DEPTH, HEAD_DIM), 0.02),
        "na_rel_bias": nrm(ks[18], (DEPTH, NA_HEADS, 2 * NA_KH_MAX - 1, 2 * NA_KW - 1), 0.1),
    }


def reference(x, c, ctx, c_ctx, norm_w, ada_w, ada_b, w_in, w_out, gla_dec_w, gla_dec_b, gla_out_norm,
              fnet_w, swa_q_norm, swa_k_norm, swa_sink, na_q_norm, na_k_norm, na_rel_bias):
    cos, sin = axial_rope_tables(x.shape[1])
    for i in range(DEPTH):
        x, ctx = hybrid_layer(x, ctx, c, c_ctx, norm_w[i], ada_w[i], ada_b[i], w_in[i], w_out[i],
                              gla_dec_w[i], gla_dec_b[i], gla_out_norm[i], fnet_w[i],
                              swa_q_norm[i], swa_k_norm[i], swa_sink[i],
                              na_q_norm[i], na_k_norm[i], na_rel_bias[i],
                              cos, sin, need_ctx=(i < DEPTH - 1))
    return x
```

```python
import os
import numpy as np
import ml_dtypes
import concourse.bass as bass
import concourse.mybir as mybir
from concourse.bass_utils import run_bass_kernel_spmd

F32 = mybir.dt.float32
BF16 = mybir.dt.bfloat16
AF = mybir.ActivationFunctionType
ALU = mybir.AluOpType
AX = mybir.AxisListType

D = 1024
NL = 4096
NC_ = 256
NT = NL + NC_
NTILE = NT // 128
DEPTH = 2
EPS = 1e-6
PW = 3104
GROUPS = [(0, 2)] + [(2 + 4 * i, 4) for i in range(8)]
NEG = -30000.0

ENGS = ['pe', 'act', 'dve', 'pool', 'sp']
NDMA = 40


class Buf:
    __slots__ = ('w', 'wa', 'r', 'excl')

    def __init__(self, excl=False):
        self.w = {}
        self.wa = {}
        self.r = {}
        self.excl = excl


MAXOUT = int(os.environ.get('KS_MAXOUT', '6'))
NOADDS = os.environ.get('KS_NOADDS', '0') == '1'


class Sched:
    def __init__(self):
        self.prog = {e: [] for e in ENGS}
        self.cnt = {e: 0 for e in ENGS}
        self.waited = {e: {} for e in ENGS}
        self.dma_val = [0] * NDMA
        self.dma_rr = 0
        self.dma_rr2 = 0
        self.outst = {e: [] for e in ENGS}

    def _waits(self, eng, deps):
        for key, val in deps:
            if key == 'pe' and eng == 'pe':
                continue
            if self.waited[eng].get(key, 0) >= val:
                continue
            self.waited[eng][key] = val
            self.prog[eng].append(('wait', key, val))

    @staticmethod
    def _deps(reads, writes, adds=()):
        deps = []
        for b in reads:
            deps.extend(b.w.items())
            deps.extend(b.wa.items())
        for b in writes:
            deps.extend(b.w.items())
            deps.extend(b.wa.items())
            deps.extend(b.r.items())
        for b in adds:
            deps.extend(b.w.items())
            deps.extend(b.r.items())
        return deps

    @staticmethod
    def _mark(tok, reads, writes, adds=()):
        for b in reads:
            if b.r.get(tok[0], 0) < tok[1]:
                b.r[tok[0]] = tok[1]
        for b in writes:
            b.w = {tok[0]: tok[1]}
            b.wa = {}
            b.r = {}
        for b in adds:
            if b.wa.get(tok[0], 0) < tok[1]:
                b.wa[tok[0]] = tok[1]

    def op(self, eng, fn, reads=(), writes=(), adds=()):
        if NOADDS:
            writes, adds = list(writes) + list(adds), ()
        ex = [b for b in reads if b.excl]
        if ex:
            reads = [b for b in reads if not b.excl]
            writes = list(writes) + ex
        self._waits(eng, self._deps(reads, writes, adds))
        self.cnt[eng] += 1
        tok = (eng, self.cnt[eng])
        self.prog[eng].append(('op', fn, eng))
        self._mark(tok, reads, writes, adds)
        return tok

    def dma(self, q, out_ap, in_ap, reads=(), writes=(), adds=()):
        if NOADDS:
            writes, adds = list(writes) + list(adds), ()
        half = NDMA // 2
        if q == 'sp':
            i = self.dma_rr % half
            self.dma_rr += 1
        else:
            i = half + self.dma_rr2 % half
            self.dma_rr2 += 1
        key = 'd%d' % i
        deps = self._deps(reads, writes, adds)
        if self.dma_val[i] > 0:
            deps.append((key, self.dma_val[i]))
        if len(self.outst[q]) >= (MAXOUT if q == 'sp' else 10):
            deps.append(self.outst[q].pop(0))
        self._waits(q, deps)
        self.dma_val[i] += 16
        tok = (key, self.dma_val[i])
        self.outst[q].append(tok)
        self.prog[q].append(('dma', out_ap, in_ap, key))
        self._mark(tok, reads, writes, adds)
        return tok

    def barrier(self):
        for e in ENGS:
            deps = [(e2, self.cnt[e2]) for e2 in ENGS if e2 != e and self.cnt[e2] > 0]
            deps += [('d%d' % i, v) for i, v in enumerate(self.dma_val) if v > 0]
            self._waits(e, deps)

    def emit(self, nc, sems):
        if os.environ.get("K_MULTIBLOCK", "0") == "1":
            return self.flush(nc, sems)
        self.barrier()

    def flush(self, nc, sems):
        self.barrier()
        prog = self.prog
        self.prog = {e: [] for e in ENGS}

        def mk(e):
            def f(engobj):
                for it in prog[e]:
                    if it[0] == 'wait':
                        engobj.wait_ge(sems[it[1]], it[2])
                    elif it[0] == 'op':
                        it[1](engobj).then_inc(sems[it[2]], 1)
                    else:
                        engobj.dma_start(out=it[1], in_=it[2]).then_inc(sems[it[3]], 16)
            return f

        with nc.Block() as block:
            block.tensor(mk('pe'))
            block.scalar(mk('act'))
            block.vector(mk('dve'))
            block.gpsimd(mk('pool'))
            block.sync(mk('sp'))


class Arena:
    _bases = {}

    def __init__(self, nc, base, limit):
        self.nc = nc
        self.base = base
        self.limit = limit
        self.off = base
        key = (id(nc), base, limit)
        if key not in Arena._bases:
            t = nc.alloc_sbuf_tensor_at("arena_%d" % base, [128, (limit - base) // 2], BF16, offset=base)
            Arena._bases[key] = t.ap()
        self.ap = Arena._bases[key]

    def reset(self):
        self.off = self.base

    def view(self, shape, dtype, off):
        esz = 4 if dtype == F32 else 2
        n = int(np.prod(shape[1:]))
        o2 = (off - self.base) // 2
        v = self.ap[0:shape[0], o2:o2 + n * esz // 2]
        if dtype != BF16:
            v = v.bitcast(dtype)
        if len(shape) > 2:
            names = " ".join("d%d" % i for i in range(len(shape) - 1))
            kw = {"d%d" % i: shape[i + 1] for i in range(len(shape) - 1)}
            v = v.rearrange("p (%s) -> p %s" % (names, names), **kw)
        return v

    def alloc(self, shape, dtype, name=None):
        esz = 4 if dtype == F32 else 2
        nbytes = int(np.prod(shape[1:])) * esz
        nbytes = (nbytes + 63) // 64 * 64
        assert self.off + nbytes <= self.limit, ("SBUF arena overflow", self.off, nbytes, self.limit)
        v = self.view(shape, dtype, self.off)
        self.off += nbytes
        return v


class Rot:
    def __init__(self, items):
        self.items = items
        self.i = 0

    def next(self):
        it = self.items[self.i % len(self.items)]
        self.i += 1
        return it


def pipeline2(gens):
    prev = None
    for g in gens:
        next(g)
        if prev is not None:
            for _ in prev:
                pass
        prev = g
    if prev is not None:
        for _ in prev:
            pass


def pipelineN(gens, nstage):
    n = len(gens)
    for t in range(n + nstage - 1):
        for k in range(nstage):
            i = t - k
            if 0 <= i < n:
                try:
                    next(gens[i])
                except StopIteration:
                    assert k == nstage - 1, (k, nstage)
                else:
                    assert k < nstage - 1, (k, nstage)


def store_item(fn):
    def g(nstage):
        for _ in range(nstage - 1):
            yield
        fn()
    return g


def bc(ap, shape):
    return ap.broadcast_to(list(shape))


class Builder:
    def __init__(self, debug=False, depth=DEPTH, stop_after=None):
        self.debug = debug
        self.depth = depth
        self.stop_after = stop_after
        nc = bass.Bass("TRN2", target_bir_lowering=False)
        self.nc = nc
        self.s = Sched()
        self.dbg_outs = []
        self.dram_bufs = {}

    def dma_tm(self, q, sb, dr, t0, t1, load, reads, writes, step=4):
        for a in range(t0, t1, step):
            b = min(a + step, t1)
            d = dr[a * 128:b * 128, :].rearrange("(i p) c -> p i c", p=128)
            sview = sb[:, a:b, :]
            if load:
                self.s.dma(q, sview, d, reads=reads, adds=writes)
            else:
                self.s.dma(q, d, sview, reads=reads, adds=writes)

    def din(self, name, shape, dtype=F32):
        return self.nc.dram_tensor(name, list(shape), dtype, kind="ExternalInput").ap()

    def dscr(self, name, shape, dtype):
        kind = "ExternalOutput" if self.debug else "Internal"
        t = self.nc.dram_tensor(name, list(shape), dtype, kind=kind).ap()
        if self.debug:
            self.dbg_outs.append(name)
        self.dram_bufs[name] = Buf()
        return t

    def build(self):
        nc = self.nc
        s = self.s
        self.xin = self.din("xin", [NT, D])
        self.cvec = self.din("cvec", [128, 8, 2])
        self.ident_f_d = self.din("ident_f", [128, 128])
        self.ident_b_d = self.din("ident_b", [128, 128], BF16)
        self.cs_d = self.din("cs_tab", [NL, 64])
        self.L = []
        for l in range(self.depth):
            Lw = dict(
                ada_w=self.din("ada_w%d" % l, [D, 3 * D]),
                ada_brow=self.din("ada_brow%d" % l, [2, 3 * D]),
                norm_wT=self.din("norm_wT%d" % l, [128, 8]),
                wp=self.din("wp%d" % l, [D, PW]),
                wc=self.din("wc%d" % l, [128, 384]),
                wd=self.din("wd%d" % l, [128, 512]),
            )
            self.L.append(Lw)
        self.out = self.nc.dram_tensor("out", [NL, D], F32, kind="ExternalOutput").ap()
        self.Gs = self.dscr("Gs", [NT, 1024], BF16)
        self.Aqkv = self.dscr("Aqkv", [NT, 512], BF16)
        self.LrT = self.dscr("LrT", [32, NT], F32)
        self.CT = self.dscr("CT", [4, 128, NT], BF16)
        self.CV1 = self.dscr("CV1", [NT, 132], BF16)
        self.DT = self.dscr("DT", [4, 128, NT], BF16)
        self.DV1 = self.dscr("DV1", [NT, 264], BF16)
        self.Bv = self.dscr("Bv", [NT, 256], BF16)
        self.NAG = na_geometry()
        if self.debug:
            self.modT_d = self.dscr("modT_dbg", [128, 48], F32)
        if self.stop_after == 'p12':
            return self._build_rest()
        self.Ymix = self.dscr("Ymix", [NT, 1024], BF16)
        self.X1 = self.dscr("X1", [NT, D], F32)
        self.cmask_d = self.din("cmask", [128, 2, 128], BF16)
        self.tri_d = self.din("tri", [128, 4, 128])
        self.fD1_d = self.din("fD1", [64, 128], BF16)
        self.fM3_d = self.din("fM3", [128, 64, 2, 128], BF16)
        self.fCH_d = self.din("fCH", [128, 8, 128], BF16)
        self.fDC_d = self.din("fDC", [128, 2, 512], BF16)
        self.fCHc_d = self.din("fCHc", [128, 2, 128], BF16)
        self.blk_d = self.din("blkmask", [128, 260])
        for l in range(self.depth):
            self.L[l]['sinkb'] = self.din("sinkb%d" % l, [128, 4])
            self.L[l]['wdec'] = self.din("wdec%d" % l, [33, 256])
            self.L[l]['onb'] = self.din("onb%d" % l, [128, 64])
            self.L[l]['fw'] = self.din("fw%d" % l, [256, 256])
            self.L[l]['wo'] = self.din("wo%d" % l, [D, D])
            self.L[l]['nab'] = self.din("nab%d" % l, [128, 4 * self.NAG['ntypes'], 128], BF16)
        return self._build_rest()

    def _build_rest(self):
        nc = self.nc
        s = self.s
        from contextlib import ExitStack
        with ExitStack() as st:
            sems = {}
            for e in ENGS:
                sems[e] = st.enter_context(nc.semaphore("sem_" + e))
            for i in range(NDMA):
                sems['d%d' % i] = st.enter_context(nc.semaphore("semd%d" % i))
            self.sems = sems
            self.ps = []
            for i in range(8):
                t = nc.alloc_psum_tensor("psb%d" % i, [128, 512], F32)
                self.ps.append((t.ap(), Buf(excl=True)))
            pa = Arena(nc, 16512, 16512 + 12 * 1024)
            self.ident_f = pa.alloc([128, 128], F32, "idf")
            self.ident_b = pa.alloc([128, 128], BF16, "idb")
            self.scT = pa.alloc([128, 8, 2], F32, "scT")
            self.modT = [pa.alloc([128, 24, 2], F32, "modT%d" % l) for l in range(self.depth)]
            self.Amod = [pa.alloc([128, 8, 2], F32, "Amod%d" % l) for l in range(self.depth)]
            self.cvs = pa.alloc([128, 8, 2], F32, "cvs")
            self.b_const = Buf()
            self.b_out = Buf()
            self.b_mod = [Buf() for _ in range(self.depth)]
            self.arena = Arena(nc, 16512 + 12 * 1024, 229312)

            s.dma('sp', self.ident_f, self.ident_f_d, adds=[self.b_const])
            s.dma('sp', self.ident_b, self.ident_b_d, adds=[self.b_const])
            s.dma('sp', self.cvs, self.cvec, adds=[self.b_const])
            s.op('act', lambda e: e.activation(out=self.scT, in_=self.cvs, func=AF.Silu),
                 reads=[self.b_const], writes=[self.b_const])
            for l in range(self.depth):
                self.phase_mod(l)
                s.emit(nc, sems)
            for l in range(self.depth):
                self.phase_weights(l)
                s.emit(nc, sems)
                self.x_src = self.xin if l == 0 else self.X1
                self.b_xsrc = Buf() if l == 0 else self.dram_bufs['X1']
                self.phase_p12(l)
                s.emit(nc, sems)
                if self.stop_after == 'p12':
                    break
                self.phase_c(l)
                s.emit(nc, sems)
                if self.stop_after == 'c':
                    break
                self.phase_d(l)
                s.emit(nc, sems)
                if self.stop_after == 'cd':
                    break
                self.phase_a(l)
                s.emit(nc, sems)
                if self.stop_after == 'a':
                    break
                self.phase_b(l)
                s.emit(nc, sems)
                if self.stop_after == 'b':
                    break
                self.phase_o(l)
                s.emit(nc, sems)
            s.flush(nc, sems)
        return nc

    def phase_mod(self, l):
        nc, s = self.nc, self.s
        ar = self.arena
        ar.reset()
        Lw = self.L[l]
        wb = Rot([(ar.alloc([128, 3 * D], F32, "adaw"), Buf()) for _ in range(3)])
        row = ar.alloc([2, 3 * D], F32, "modrow")
        brow = ar.alloc([2, 3 * D], F32, "brow")
        nwT = ar.alloc([128, 8], F32, "nwT")
        tmp = ar.alloc([128, 8, 2], F32, "tmpm")
        b_small, b_row = Buf(), Buf()
        s.dma('sp', brow, Lw['ada_brow'], adds=[b_small])
        s.dma('sp', nwT, Lw['norm_wT'], adds=[b_small])
        for k in range(8):
            w, bw = wb.next()
            s.dma('sp', w, Lw['ada_w'][k * 128:(k + 1) * 128, :], writes=[bw])
            for cc in range(6):
                pk, bpk = self.ps[cc]
                s.op('pe', (lambda w=w, k=k, cc=cc, pk=pk: lambda e: e.matmul(
                    pk[0:2, 0:512], lhsT=self.scT[:, k, :], rhs=w[:, cc * 512:(cc + 1) * 512],
                    start=(k == 0), stop=(k == 7)))(), reads=[bw, self.b_const], writes=[bpk])
        for cc in range(6):
            pk, bpk = self.ps[cc]
            s.op('dve', (lambda cc=cc, pk=pk: lambda e: e.tensor_tensor(
                out=row[:, cc * 512:(cc + 1) * 512], in0=pk[0:2, 0:512], in1=brow[:, cc * 512:(cc + 1) * 512],
                op=ALU.add))(), reads=[bpk, b_small], adds=[b_row])
        psT, bpT = self.ps[6]
        for j in range(24):
            s.op('pe', (lambda j=j: lambda e: e.matmul(
                psT[:, 2 * j:2 * j + 2], lhsT=row[0:2, j * 128:(j + 1) * 128], rhs=self.ident_f[0:2, 0:2],
                start=True, stop=True))(), reads=[b_row, self.b_const], writes=[bpT])
        modT = self.modT[l]
        bmod = self.b_mod[l]
        s.op('dve', lambda e: e.tensor_copy(out=modT, in_=psT[:, 0:48].rearrange("p (j i) -> p j i", i=2)),
             reads=[bpT], writes=[bmod])
        s.op('dve', lambda e: e.tensor_scalar(out=tmp, in0=modT[:, 8:16, :], scalar1=1.0, scalar2=None,
                                              op0=ALU.add), reads=[bmod], writes=[b_small])
        Am = self.Amod[l]
        s.op('dve', lambda e: e.tensor_tensor(out=Am, in0=tmp, in1=bc(nwT.unsqueeze(2), [128, 8, 2]),
                                              op=ALU.mult), reads=[b_small], writes=[bmod])
        if self.debug and l == 0:
            s.dma('pool', self.modT_d, modT.rearrange("p j i -> p (j i)"), reads=[bmod],
                  writes=[self.dram_bufs["modT_dbg"]])

    def phase_weights(self, l):
        nc, s = self.nc, self.s
        ar = self.arena
        ar.reset()
        self.Wb = ar.alloc([128, 8, PW], BF16, "Wb")
        self.b_W = Buf()
        self.p12_base = ar.off
        stg = Rot([(ar.alloc([128, PW], F32, "stgW"), Buf()) for _ in range(2)])
        wp = self.L[l]['wp']
        for k in range(8):
            w, bw = stg.next()
            s.dma('sp', w, wp[k * 128:(k + 1) * 128, :], writes=[bw])
            s.op('act', (lambda w=w, k=k: lambda e: e.activation(
                out=self.Wb[:, k, 0:1024], in_=w[:, 0:1024], func=AF.Copy))(), reads=[bw], adds=[self.b_W])
            s.op('dve', (lambda w=w, k=k: lambda e: e.tensor_copy(
                out=self.Wb[:, k, 1024:2048], in_=w[:, 1024:2048]))(), reads=[bw], adds=[self.b_W])
            s.op('pool', (lambda w=w, k=k: lambda e: e.tensor_copy(
                out=self.Wb[:, k, 2048:PW], in_=w[:, 2048:PW]))(), reads=[bw], adds=[self.b_W])

    def phase_p12(self, l):
        nc, s = self.nc, self.s
        ar = self.arena
        ar.off = self.p12_base
        Lw = self.L[l]
        Wb, bW = self.Wb, self.b_W
        Am, modT, bmod = self.Amod[l], self.modT[l], self.b_mod[l]
        wc = ar.alloc([128, 384], F32, "wc")
        wd = ar.alloc([128, 512], F32, "wd")
        b_nw = Buf()
        s.dma('sp', wc, Lw['wc'], adds=[b_nw])
        s.dma('sp', wd, Lw['wd'], adds=[b_nw])

        def rot(shape, dt, name, n):
            return Rot([(ar.alloc(shape, dt, name), Buf()) for _ in range(n)])
        xt = rot([128, D], F32, "xt", 3)
        junk = ar.alloc([128, D], BF16, "junk")
        b_junk = Buf()
        xn = rot([128, D], F32, "xn", 2)
        hT = rot([128, 8, 128], BF16, "hT", 3)
        st = rot([128, 8], F32, "stat", 4)
        raw = rot([128, 896], F32, "raw", 3)
        nt = rot([128, 896], F32, "nt", 3)
        nt2 = rot([128, 384], F32, "nt2", 3)
        st8 = rot([128, 64], F32, "st8", 4)
        rt = rot([128, 4, 192], F32, "rt", 2)
        qkr = rot([128, 384], BF16, "qkr", 3)
        kd = rot([128, 256], BF16, "kd", 3)
        dqk = rot([128, 512], BF16, "dqk", 3)
        cs = rot([128, 4, 64], F32, "cs", 3)

        def stage_set():
            d = dict(
                G=ar.alloc([128, 4, 1024], BF16, "sG"), A=ar.alloc([128, 4, 512], BF16, "sA"),
                CT=ar.alloc([128, 4, 512], BF16, "sCT"), CV=ar.alloc([128, 4, 2, 66], BF16, "sCV"),
                DT=ar.alloc([128, 4, 512], BF16, "sDT"), DV=ar.alloc([128, 4, 4, 66], BF16, "sDV"),
                BV=ar.alloc([128, 4, 256], BF16, "sBV"), LR=ar.alloc([32, 512], F32, "sLR"))
            d['buf'] = Buf()
            d['bufB'] = Buf()
            return d
        stg = Rot([stage_set(), stage_set()])
        for sset in stg.items:
            for nm in ('CV', 'DV'):
                s.op('pool', (lambda a=sset[nm]: lambda e: e.memset(a, 1.0))(), writes=[sset['buf']])

        TP = [self.ps[0], self.ps[1]]
        MM = Rot([self.ps[2], self.ps[3], self.ps[4], self.ps[5]])
        TQ, bTQ = self.ps[6]
        TL, bTL = self.ps[7]
        TQb = TQ.bitcast(BF16)
        db = self.dram_bufs

        def stores(sset, bS, t0, n4, batch):
            u0 = t0 * 128
            n = n4 * 128
            tm = lambda dr: dr[u0:u0 + n, :].rearrange("(i p) c -> p i c", p=128)
            if batch == 1:
                bSB = sset['bufB']
                s.dma('pool', self.CT[:, :, u0:u0 + n].rearrange("a p t -> p a t"), sset['CT'][:, :, 0:n],
                      reads=[bSB], adds=[db['CT']])
                s.dma('pool', self.DT[:, :, u0:u0 + n].rearrange("a p t -> p a t"), sset['DT'][:, :, 0:n],
                      reads=[bSB], adds=[db['DT']])
                return
            s.dma('pool', tm(self.Gs), sset['G'][:, 0:n4, :], reads=[bS], adds=[db['Gs']])
            s.dma('pool', tm(self.Aqkv), sset['A'][:, 0:n4, :], reads=[bS], adds=[db['Aqkv']])
            s.dma('pool', tm(self.CV1), sset['CV'][:, 0:n4].rearrange("p i g d -> p i (g d)"), reads=[bS],
                  adds=[db['CV1']])
            s.dma('pool', tm(self.DV1), sset['DV'][:, 0:n4].rearrange("p i g d -> p i (g d)"), reads=[bS],
                  adds=[db['DV1']])
            s.dma('pool', tm(self.Bv), sset['BV'][:, 0:n4, :], reads=[bS], adds=[db['Bv']])
            s.dma('pool', self.LrT[:, u0:u0 + n], sset['LR'][0:32, 0:n], reads=[bS], adds=[db['LrT']])

        def tile_body(t0, n4, i4, sset, cs_slot):
            bS = sset['buf']
            latent = t0 >= 2
            mi = 0 if latent else 1
            ti = t0 + i4
            u0 = ti * 128
            cst, bcs = cs_slot if latent else (None, None)
            if latent and i4 == 0:
                tl0 = (t0 - 2) * 128
                s.dma('sp', cst[:, 0:n4, :], self.cs_d[tl0:tl0 + n4 * 128, :].rearrange("(i p) c -> p i c", p=128),
                      writes=[bcs])
            x_t, bx = xt.next()
            s.dma('sp', x_t, self.x_src[u0:u0 + 128, :], reads=[self.b_xsrc], writes=[bx])
            stt, bst = st.next()
            s.op('act', lambda e: e.activation(out=junk, in_=x_t, func=AF.Square, accum_out=stt[:, 0:1]),
                 reads=[bx], writes=[b_junk, bst])
            yield
            s.op('dve', lambda e: e.tensor_scalar(out=stt[:, 1:2], in0=stt[:, 0:1], scalar1=1.0 / D, scalar2=EPS,
                                                  op0=ALU.mult, op1=ALU.add), reads=[bst], writes=[bst])
            self.rsqrt1(stt[:, 1:2], stt[:, 2:3], stt[:, 3:4], bst)
            yield
            x_n, bxn = xn.next()
            s.op('pool', lambda e: e.tensor_tensor(out=x_n, in0=x_t, in1=bc(stt[:, 2:3], [128, D]), op=ALU.mult),
                 reads=[bx, bst], writes=[bxn])
            yield
            h_t, bh = hT.next()
            for half in range(2):
                tp, btp = TP[half]
                for kk in range(4):
                    k = half * 4 + kk
                    s.op('pe', (lambda tp=tp, kk=kk, k=k: lambda e: e.transpose(
                        out=tp[:, kk * 128:(kk + 1) * 128], in_=x_n[:, k * 128:(k + 1) * 128],
                        identity=self.ident_f))(), reads=[bxn, self.b_const], writes=[btp])
                for kk in range(4):
                    k = half * 4 + kk
                    if False:
                        pass
                    else:
                        s.op('act', (lambda tp=tp, kk=kk, k=k: lambda e: e.activation(
                            out=h_t[:, k, :], in_=tp[:, kk * 128:(kk + 1) * 128], func=AF.Identity,
                            scale=Am[:, k, mi:mi + 1], bias=modT[:, k, mi:mi + 1]))(),
                            reads=[btp, bmod], adds=[bh])
            yield
            def mm_chunk(c):
                pm, bpm = MM.next()
                for k in range(8):
                    s.op('pe', (lambda pm=pm, k=k, c=c: lambda e: e.matmul(
                        pm[:, 0:512], lhsT=h_t[:, k, :], rhs=Wb[:, k, c * 512:(c + 1) * 512],
                        start=(k == 0), stop=(k == 7)))(), reads=[bh, bW], writes=[bpm])
                return pm, bpm
            r_w, brw = raw.next()
            pm, bpm = mm_chunk(0)
            s.op('act', (lambda pm=pm: lambda e: e.activation(out=sset['A'][:, i4, :], in_=pm, func=AF.Copy))(),
                 reads=[bpm], adds=[bS])
            for c in (1, 2):
                pm, bpm = mm_chunk(c)
                s.op('act', (lambda pm=pm, c=c: lambda e: e.activation(
                    out=sset['G'][:, i4, (c - 1) * 512:c * 512], in_=pm, func=AF.Silu))(), reads=[bpm], adds=[bS])
            pm, bpm = mm_chunk(3)
            s.op('dve', (lambda pm=pm: lambda e: e.tensor_copy(out=r_w[:, 0:384], in_=pm[:, 0:384]))(),
                 reads=[bpm], adds=[brw])
            s.op('dve', (lambda pm=pm: lambda e: e.tensor_copy(
                out=sset['CV'][:, i4, :, 0:64], in_=pm[:, 384:512].rearrange("p (g d) -> p g d", g=2)))(),
                reads=[bpm], adds=[bS])
            pm, bpm = mm_chunk(4)
            s.op('act', (lambda pm=pm: lambda e: e.activation(out=r_w[:, 384:896], in_=pm[:, 0:512], func=AF.Copy))(),
                 reads=[bpm], adds=[brw])
            pm, bpm = mm_chunk(5)
            s.op('dve', (lambda pm=pm: lambda e: e.tensor_copy(
                out=sset['DV'][:, i4, :, 0:64], in_=pm[:, 0:256].rearrange("p (g d) -> p g d", g=4)))(),
                reads=[bpm], adds=[bS])
            s.op('act', (lambda pm=pm: lambda e: e.activation(out=sset['BV'][:, i4, :], in_=pm[:, 256:512],
                                                              func=AF.Copy))(), reads=[bpm], adds=[bS])
            for k in range(8):
                s.op('pe', (lambda k=k: lambda e: e.matmul(
                    TL[0:32, 0:128], lhsT=Wb[:, k, 3072:3104], rhs=h_t[:, k, :],
                    start=(k == 0), stop=(k == 7)))(), reads=[bh, bW], writes=[bTL])
            s.op('act', lambda e: e.activation(out=sset['LR'][0:32, i4 * 128:(i4 + 1) * 128], in_=TL[0:32, 0:128],
                                               func=AF.Copy), reads=[bTL], adds=[bS])
            n_t, bnt = nt.next()
            s.op('pool', lambda e: e.tensor_tensor(out=n_t, in0=r_w, in1=r_w, op=ALU.mult), reads=[brw], writes=[bnt])
            if i4 == n4 - 1:
                stores(sset, bS, t0, n4, 0)
            yield
            s8, bs8 = st8.next()
            s.op('dve', lambda e: e.tensor_reduce(out=s8[:, 0:14], in_=n_t.rearrange("p (h d) -> p h d", d=64),
                                                  axis=AX.X, op=ALU.add), reads=[bnt], writes=[bs8])
            s.op('dve', lambda e: e.tensor_scalar(out=s8[:, 0:14], in0=s8[:, 0:14], scalar1=1.0 / 64, scalar2=EPS,
                                                  op0=ALU.mult, op1=ALU.add), reads=[bs8], writes=[bs8])
            self.rsqrt(s8[:, 0:14], s8[:, 16:30], s8[:, 32:46], s8[:, 48:62], bs8)
            s.op('dve', lambda e: e.tensor_scalar(out=s8[:, 16:20], in0=s8[:, 16:20], scalar1=0.125, scalar2=None,
                                                  op0=ALU.mult), reads=[bs8], writes=[bs8])
            s.op('dve', lambda e: e.tensor_scalar(out=s8[:, 22:26], in0=s8[:, 22:26], scalar1=0.125, scalar2=None,
                                                  op0=ALU.mult), reads=[bs8], writes=[bs8])
            yield
            s.op('dve', lambda e: e.tensor_tensor(
                out=n_t.rearrange("p (h d) -> p h d", d=64), in0=r_w.rearrange("p (h d) -> p h d", d=64),
                in1=bc(s8[:, 16:30].unsqueeze(2), [128, 14, 64]), op=ALU.mult), reads=[brw, bs8], writes=[bnt])
            q2, bq2 = nt2.next()
            s.op('pool', lambda e: e.tensor_tensor(out=q2, in0=n_t[:, 0:384], in1=wc, op=ALU.mult),
                 reads=[bnt, b_nw], writes=[bq2])
            d_t, bdt = dqk.next()
            s.op('pool', lambda e: e.tensor_tensor(out=d_t, in0=n_t[:, 384:896], in1=wd, op=ALU.mult),
                 reads=[bnt, b_nw], writes=[bdt])
            yield
            q_r, bqr = qkr.next()
            if latent:
                r_t, brt = rt.next()
                cst_i = cst[:, i4, :]
                xv = q2.rearrange("p (h a f) -> p h a f", a=2, f=16)
                ov = q_r.rearrange("p (h a f) -> p h a f", a=2, f=16)

                def csb(off):
                    v = cst_i[:, off:off + 32].rearrange("p (a f) -> p a f", a=2)
                    return bc(v.unsqueeze(1), [128, 6, 2, 16])
                x1h = xv[:, :, 0, :].rearrange("p (h a) f -> p h a f", a=2)
                x2h = xv[:, :, 1, :].rearrange("p (h a) f -> p h a f", a=2)
                o1h = ov[:, :, 0, :].rearrange("p (h a) f -> p h a f", a=2)
                o2h = ov[:, :, 1, :].rearrange("p (h a) f -> p h a f", a=2)
                tv = [r_t[:, j, :].rearrange("p (h a f) -> p h a f", a=2, f=16) for j in range(4)]
                cosb, sinb = csb(0), csb(32)
                s.op('pool', lambda e: e.tensor_tensor(out=tv[0], in0=x1h, in1=cosb, op=ALU.mult),
                     reads=[bq2, bcs], adds=[brt])
                s.op('pool', lambda e: e.tensor_tensor(out=tv[1], in0=x2h, in1=sinb, op=ALU.mult),
                     reads=[bq2, bcs], adds=[brt])
                s.op('dve', lambda e: e.tensor_tensor(out=tv[2], in0=x2h, in1=cosb, op=ALU.mult),
                     reads=[bq2, bcs], adds=[brt])
                s.op('dve', lambda e: e.tensor_tensor(out=tv[3], in0=x1h, in1=sinb, op=ALU.mult),
                     reads=[bq2, bcs], adds=[brt])
                s.op('pool', lambda e: e.tensor_tensor(out=o1h, in0=tv[0], in1=tv[1], op=ALU.subtract),
                     reads=[brt], writes=[bqr])
                s.op('dve', lambda e: e.tensor_tensor(out=o2h, in0=tv[2], in1=tv[3], op=ALU.add),
                     reads=[brt], writes=[bqr])
            else:
                s.op('pool', lambda e: e.tensor_copy(out=q_r, in_=q2), reads=[bq2], writes=[bqr])
            k_d, bkd = kd.next()
            s.op('pool', lambda e: e.tensor_copy(
                out=k_d.rearrange("p (g j d) -> p g j d", g=2, j=2),
                in_=bc(q_r[:, 256:384].rearrange("p (g d) -> p g d", g=2).unsqueeze(2), [128, 2, 2, 64])),
                reads=[bqr], writes=[bkd])
            yield
            for j in range(4):
                src = q_r[:, j * 128:(j + 1) * 128] if j < 2 else k_d[:, (j - 2) * 128:(j - 1) * 128]
                s.op('pe', (lambda src=src, j=j: lambda e: e.transpose(
                    out=TQb[:, j * 128:(j + 1) * 128], in_=src, identity=self.ident_b))(),
                    reads=[bqr, bkd, self.b_const], writes=[bTQ])
            for j in range(4):
                s.op('pe', (lambda j=j: lambda e: e.transpose(
                    out=TQb[:, 512 + j * 128:512 + (j + 1) * 128], in_=d_t[:, j * 128:(j + 1) * 128],
                    identity=self.ident_b))(), reads=[bdt, self.b_const], writes=[bTQ])
            s.op('act', lambda e: e.activation(
                out=sset['CT'][:, :, i4 * 128:(i4 + 1) * 128],
                in_=TQb[:, 0:512].rearrange("p (a t) -> p a t", a=4), func=AF.Copy), reads=[bTQ], adds=[sset['bufB']])
            s.op('act', lambda e: e.activation(
                out=sset['DT'][:, :, i4 * 128:(i4 + 1) * 128],
                in_=TQb[:, 512:1024].rearrange("p (a t) -> p a t", a=4), func=AF.Copy), reads=[bTQ], adds=[sset['bufB']])
            if i4 == n4 - 1:
                stores(sset, bS, t0, n4, 1)

        gens = []
        for (t0, n4) in GROUPS:
            sset = stg.next()
            cs_slot = cs.next() if t0 >= 2 else None
            for i4 in range(n4):
                gens.append(tile_body(t0, n4, i4, sset, cs_slot))
        pipelineN(gens, 9)

    def phase_c(self, l):
        return self.attn_phase(l, 'c')

    def phase_d(self, l):
        return self.attn_phase(l, 'd')

    def attn_phase(self, l, kind):
        nc, s = self.nc, self.s
        ar = self.arena
        ar.reset()
        need_ctx = (l < DEPTH - 1)
        db = self.dram_bufs
        isC = (kind == 'c')
        G = self.NAG
        ntp = G['ntypes']
        TT, V1d, vw, ycol = (self.CT, self.CV1, 132, 512) if isC else (self.DT, self.DV1, 264, 768)
        QT = [ar.alloc([128, NT], BF16, "QT%d" % g) for g in range(2)]
        KT = [ar.alloc([128, NT], BF16, "KT%d" % g) for g in range(2)]
        V1 = ar.alloc([128, NTILE, vw], BF16, "V1")
        Gg = ar.alloc([128, NTILE, 256], BF16, "Gg")
        Ys = ar.alloc([128, NTILE, 256], BF16, "Ys")
        bQK = [Buf(), Buf()]
        bV, bG, bM = Buf(), Buf(), Buf()
        bYg = [Buf() for _ in range(NTILE)]
        for g in range(2):
            s.dma('sp', QT[g], TT[g], reads=[db['CT' if isC else 'DT']], adds=[bQK[g]])
            s.dma('sp', KT[g], TT[2 + g], reads=[db['CT' if isC else 'DT']], adds=[bQK[g]])
        self.dma_tm('sp', V1, V1d, 0, NTILE, True, [db['CV1' if isC else 'DV1']], [bV])
        self.dma_tm('sp', Gg, self.Gs[:, ycol:ycol + 256], 0, NTILE, True, [db['Gs']], [bG])
        if isC:
            msk = ar.alloc([128, 2, 128], BF16, "cmsk")
            es = ar.alloc([128, 4], F32, "ces")
            s.dma('sp', msk, self.cmask_d, adds=[bM])
            s.dma('sp', es, self.L[l]['sinkb'], adds=[bM])
            s.op('act', lambda e: e.activation(out=es, in_=es, func=AF.Exp), reads=[bM], writes=[bM])
            mtile = lambda h, typ: msk[:, typ, :]
        else:
            NB = ar.alloc([128, 4 * ntp, 128], BF16, "dNB")
            s.dma('sp', NB, self.L[l]['nab'], writes=[bM])
            for h4 in range(4):
                s.op('act', (lambda h4=h4: lambda e: e.activation(
                    out=NB[:, h4 * ntp:(h4 + 1) * ntp, :], in_=NB[:, h4 * ntp:(h4 + 1) * ntp, :], func=AF.Exp))(),
                    reads=[bM], writes=[bM])
            mtile = lambda h, typ: NB[:, h * ntp + typ, :]
        PT = Rot([(ar.alloc([128, 512], BF16, "PT"), Buf()) for _ in range(8)])
        ST = Rot([self.ps[i] for i in range(6)])
        OP = Rot([self.ps[6], self.ps[7]])
        dn = Rot([(ar.alloc([128, 8], F32, "den"), Buf()) for _ in range(4)])

        def keys_for(qt):
            kts = [(0, None), (1, None)]
            if qt >= 2:
                if isC:
                    if qt - 1 >= 2:
                        kts.append((qt - 1, 0))
                    kts.append((qt, None))
                    if qt + 1 < NTILE:
                        kts.append((qt + 1, 1))
                else:
                    n = qt - 2
                    for m in G['nbrs'][n]:
                        kts.append((m + 2, G['table'][(n, m)]))
            return kts

        def body(h, qts):
            g, j = h // 2, h % 2
            jsl = slice(j * 64, (j + 1) * 64)
            hv = g if isC else h
            nq = len(qts)
            W = 128 * nq
            per_q = [keys_for(q) for q in qts]
            union = sorted({kt for kl in per_q for kt, _ in kl})
            upos = {kt: u for u, kt in enumerate(union)}
            per_bank = 512 // W
            nbank = (len(union) + per_bank - 1) // per_bank
            banks = [ST.next() for _ in range(nbank)]
            pts = [PT.next() for _ in range(nbank)]
            q0 = qts[0] * 128
            for u, kt in enumerate(union):
                bk, bbk = banks[u // per_bank]
                c0 = (u % per_bank) * W
                s.op('pe', (lambda bk=bk, c0=c0, kt=kt: lambda e: e.matmul(
                    bk[:, c0:c0 + W], lhsT=KT[g][jsl, kt * 128:(kt + 1) * 128], rhs=QT[g][jsl, q0:q0 + W],
                    start=True, stop=True))(), reads=[bQK[g]], writes=[bbk])
            for bi in range(nbank):
                ncol = min(per_bank, len(union) - bi * per_bank) * W
                bk, bbk = banks[bi]
                pt, bpt = pts[bi]
                s.op('act', (lambda bk=bk, pt=pt, ncol=ncol: lambda e: e.activation(
                    out=pt[:, 0:ncol], in_=bk[:, 0:ncol], func=AF.Exp))(), reads=[bbk], writes=[bpt])
            nm = 0
            for qi, kl in enumerate(per_q):
                for kt, typ in kl:
                    if typ is None:
                        continue
                    u = upos[kt]
                    pt, bpt = pts[u // per_bank]
                    c0 = (u % per_bank) * W + qi * 128
                    eng = 'dve' if (isC or nm % 5 != 4) else 'pool'
                    nm += 1
                    s.op(eng, (lambda pt=pt, c0=c0, typ=typ: lambda e: e.tensor_tensor(
                        out=pt[:, c0:c0 + 128], in0=pt[:, c0:c0 + 128], in1=mtile(h, typ), op=ALU.mult))(),
                        reads=[bM, bpt], writes=[bpt])
            yield
            o, bo = OP.next()
            for qi, kl in enumerate(per_q):
                for ki, (kt, typ) in enumerate(kl):
                    u = upos[kt]
                    pt, bpt = pts[u // per_bank]
                    c0 = (u % per_bank) * W + qi * 128
                    s.op('pe', (lambda pt=pt, c0=c0, kt=kt, ki=ki, qi=qi, nk=len(kl): lambda e: e.matmul(
                        o[:, qi * 66:qi * 66 + 65], lhsT=pt[:, c0:c0 + 128], rhs=V1[:, kt, hv * 66:hv * 66 + 65],
                        start=(ki == 0), stop=(ki == nk - 1)))(), reads=[bpt, bV], writes=[bo])
            d, bd = dn.next()
            ov = o[:, 0:66 * nq].rearrange("p (q c) -> p q c", c=66)
            if isC:
                s.op('dve', lambda e: e.tensor_tensor(out=d[:, 0:nq], in0=ov[:, :, 64],
                                                      in1=bc(es[:, h:h + 1], [128, nq]), op=ALU.add),
                     reads=[bo, bM], writes=[bd])
                s.op('dve', lambda e: e.reciprocal(out=d[:, 4:4 + nq], in_=d[:, 0:nq]), reads=[bd], writes=[bd])
            else:
                s.op('dve', lambda e: e.reciprocal(out=d[:, 4:4 + nq], in_=ov[:, :, 64]), reads=[bo], writes=[bd])
            for qi, qt in enumerate(qts):
                s.op('dve', (lambda qi=qi, qt=qt: lambda e: e.scalar_tensor_tensor(
                    out=Ys[:, qt, h * 64:(h + 1) * 64], in0=o[:, qi * 66:qi * 66 + 64], scalar=d[:, 4 + qi:5 + qi],
                    in1=Gg[:, qt, h * 64:(h + 1) * 64], op0=ALU.mult, op1=ALU.mult))(),
                    reads=[bo, bd, bG], adds=[bYg[qt // 4]])

        pairs = ([[0, 1]] if need_ctx else []) + [[q, q + 1] for q in range(2, NTILE, 2)]
        gens = []
        for pr in pairs:
            for h in range(4):
                gens.append(body(h, pr))
            qt = pr[-1]
            if qt % 4 == 3 or qt == NTILE - 1:
                g0 = max(pairs[0][0], (qt // 4) * 4)
                gens.append(store_item((lambda g0=g0, qt=qt: lambda: self.dma_tm(
                    'sp', Ys, self.Ymix[:, ycol:ycol + 256], g0, qt + 1, False, [bYg[qt // 4]], [db['Ymix']]))())(2))
        pipeline2(gens)

    def phase_a(self, l):
        nc, s = self.nc, self.s
        ar = self.arena
        ar.reset()
        need_ctx = (l < DEPTH - 1)
        db = self.dram_bufs
        A = ar.alloc([128, NTILE, 512], BF16, "aQKV")
        LRf = ar.alloc([96, NT], F32, "aLRf")
        LRx = ar.alloc([96, NT], BF16, "aLRx")
        Ga = ar.alloc([128, NTILE, 256], BF16, "aG")
        Ost = ar.alloc([128, NTILE, 256], F32, "aOst")
        Ys = ar.alloc([128, NTILE, 256], BF16, "aY")
        tri = ar.alloc([128, 4, 128], F32, "tri")
        blk = ar.alloc([128, 260], F32, "blk")
        Wf = ar.alloc([96, 256], F32, "wdecf")
        Wx = ar.alloc([96, 256], BF16, "wdecx")
        bdec = ar.alloc([1, 256], F32, "bdec")
        bhl = ar.alloc([1, 2, 256], BF16, "bhl")
        ones1 = ar.alloc([1, 128], BF16, "ones1")
        trib = ar.alloc([128, 4, 128], BF16, "trib")
        onb = ar.alloc([128, 64], F32, "onb")
        bA, bLR, bG, bK = Buf(), Buf(), Buf(), Buf()
        bYc = [Buf() for _ in range(NTILE)]
        bOst = [Buf() for _ in range(NTILE)]
        self.dma_tm('sp', A, self.Aqkv, 0, NTILE, True, [db['Aqkv']], [bA])
        for r3 in range(3):
            s.dma('sp', LRf[r3 * 32:(r3 + 1) * 32, :], self.LrT, reads=[db['LrT']], adds=[bLR])
            s.dma('sp', Wf[r3 * 32:(r3 + 1) * 32, :], self.L[l]['wdec'][0:32, :], adds=[bK])
        self.dma_tm('sp', Ga, self.Gs[:, 0:256], 0, NTILE, True, [db['Gs']], [bG])
        s.dma('sp', tri, self.tri_d, adds=[bK])
        s.dma('sp', blk, self.blk_d, adds=[bK])
        s.dma('sp', bdec, self.L[l]['wdec'][32:33, :], adds=[bK])
        s.dma('sp', onb, self.L[l]['onb'], adds=[bK])
        bLX, bWX = Buf(), Buf()
        s.op('pool', lambda e: e.memset(ones1, 1.0), adds=[bWX])
        s.op('act', lambda e: e.activation(out=LRx[0:32, :], in_=LRf[0:32, :], func=AF.Copy), reads=[bLR], adds=[bLX])
        s.op('act', lambda e: e.activation(out=LRx[64:96, :], in_=LRf[64:96, :], func=AF.Copy), reads=[bLR], adds=[bLX])
        s.op('dve', lambda e: e.tensor_copy(out=LRx[32:64, :], in_=LRf[32:64, :]), reads=[bLR], adds=[bLX])
        s.op('dve', lambda e: e.tensor_tensor(out=LRx[32:64, :], in0=LRf[32:64, :], in1=LRx[32:64, :], op=ALU.subtract),
             reads=[bLR, bLX], adds=[bLX])
        s.op('pool', lambda e: e.tensor_copy(out=Wx[0:64, :], in_=Wf[0:64, :]), reads=[bK], adds=[bWX])
        s.op('pool', lambda e: e.tensor_copy(out=Wx[64:96, :], in_=Wf[64:96, :]), reads=[bK], adds=[bWX])
        s.op('pool', lambda e: e.tensor_tensor(out=Wx[64:96, :], in0=Wf[64:96, :], in1=Wx[64:96, :], op=ALU.subtract),
             reads=[bK, bWX], adds=[bWX])
        s.op('pool', lambda e: e.tensor_copy(out=bhl[:, 0, :], in_=bdec), reads=[bK], adds=[bWX])
        s.op('pool', lambda e: e.tensor_tensor(out=bhl[:, 1, :], in0=bdec, in1=bhl[:, 0, :], op=ALU.subtract),
             reads=[bK, bWX], adds=[bWX])
        s.op('pool', lambda e: e.tensor_copy(out=trib, in_=tri), reads=[bK], adds=[bWX])
        blkm = blk[:, 0:256]
        hm = blk[:, 256:260]
        Sf = [ar.alloc([128, 256], F32, "Sf%d" % d) for d in range(2)]
        Sb = [ar.alloc([128, 256], BF16, "Sb%d" % d) for d in range(2)]
        bS = [Buf(), Buf()]
        for d in range(2):
            s.op('pool', (lambda d=d: lambda e: e.memset(Sf[d], 0.0))(), writes=[bS[d]])
            s.op('pool', (lambda d=d: lambda e: e.memset(Sb[d], 0.0))(), adds=[bS[d]])

        def rot(shape, dt, name, n=4):
            return [Rot([(ar.alloc(shape, dt, name), Buf()) for _ in range(n)]) for _ in range(2)]
        gS = rot([128, 128], F32, "gS")
        gH = rot([128, 2, 128], BF16, "gH")
        eT = rot([128, 128], F32, "eT")
        Eq = rot([128, 128], F32, "Eq")
        Ek = rot([128, 128], F32, "Ek")
        Eh = rot([128, 128], F32, "Eh")
        qt_ = rot([128, 128], BF16, "qt")
        kt_ = rot([128, 128], BF16, "kt")
        kh_ = rot([128, 128], BF16, "kh")
        Q4 = rot([128, 512], BF16, "Q4")
        Pm = rot([128, 512], BF16, "Pm")
        fin = Rot([(ar.alloc([128, 3, 256], F32, "fin"), Buf()) for _ in range(3)])
        fst = Rot([(ar.alloc([128, 64], F32, "fst"), Buf()) for _ in range(3)])
        ZG = [self.ps[0], self.ps[1]]
        TRs = Rot([self.ps[2], self.ps[7]])
        AT = [self.ps[3], self.ps[4]]
        OK = [self.ps[5], self.ps[6]]

        order = [list(range(NTILE)), [1, 0] + list(range(NTILE - 1, 1, -1))]
        pos = [{c: i for i, c in enumerate(order[d])} for d in range(2)]

        def step(c, d):
            zg, bzg = ZG[d]
            at, bat = AT[d]
            ok, bok = OK[d]
            first = pos[d][c] < pos[1 - d][c] or (pos[d][c] == pos[1 - d][c] and d == 0)
            cs_ = slice(c * 128, (c + 1) * 128)
            mi_incl, mi_tail = (0, 2) if d == 0 else (1, 3)
            s.op('pe', lambda e: e.matmul(zg[:, 0:128], lhsT=LRx[:, cs_], rhs=Wx[:, d * 128:(d + 1) * 128],
                                          start=True, stop=False), reads=[bLX, bWX], writes=[bzg])
            for hl in range(2):
                s.op('pe', (lambda hl=hl: lambda e: e.matmul(
                    zg[:, 0:128], lhsT=ones1[0:1, :], rhs=bhl[0:1, hl, d * 128:(d + 1) * 128],
                    start=False, stop=(hl == 1)))(), reads=[bWX], writes=[bzg])
            e_t, bet = eT[d].next()
            g_s, bgs = gS[d].next()
            s.op('act', lambda e: e.activation(out=e_t, in_=zg[:, 0:128], func=AF.Exp, scale=-1.0),
                 reads=[bzg], writes=[bet])
            s.op('act', lambda e: e.activation(out=g_s, in_=e_t, func=AF.Ln, bias=1.0), reads=[bet], writes=[bgs])
            g_h, bgh = gH[d].next()
            s.op('act', lambda e: e.activation(out=g_h[:, 0, :], in_=g_s, func=AF.Copy), reads=[bgs], writes=[bgh])
            s.op('pool', lambda e: e.tensor_tensor(out=g_h[:, 1, :], in0=g_s, in1=g_h[:, 0, :], op=ALU.subtract),
                 reads=[bgs, bgh], writes=[bgh])
            yield
            for hl in range(2):
                s.op('pe', (lambda hl=hl: lambda e: e.matmul(
                    zg[:, 128:256], lhsT=g_h[:, hl, :], rhs=trib[:, mi_incl, :], start=(hl == 0), stop=(hl == 1)))(),
                    reads=[bgh, bWX], writes=[bzg])
            for hl in range(2):
                s.op('pe', (lambda hl=hl: lambda e: e.matmul(
                    zg[:, 256:384], lhsT=trib[:, mi_tail, :], rhs=g_h[:, hl, :], start=(hl == 0), stop=(hl == 1)))(),
                    reads=[bgh, bWX], writes=[bzg])
            eq, beq = Eq[d].next()
            ek, bek = Ek[d].next()
            eh, beh = Eh[d].next()
            s.op('act', lambda e: e.activation(out=eq, in_=zg[:, 128:256], func=AF.Exp, scale=-1.0 / 16),
                 reads=[bzg], writes=[beq])
            s.op('act', lambda e: e.activation(out=ek, in_=zg[:, 128:256], func=AF.Exp, scale=1.0 / 16),
                 reads=[bzg], writes=[bek])
            s.op('act', lambda e: e.activation(out=eh, in_=zg[:, 256:384], func=AF.Exp, scale=-1.0 / 16),
                 reads=[bzg], writes=[beh])
            yield
            tr, btr = TRs.next()
            trb = tr.bitcast(BF16)
            s.op('pe', lambda e: e.transpose(out=trb[:, 0:128], in_=A[:, c, 0:128], identity=self.ident_b),
                 reads=[bA, self.b_const], writes=[btr])
            s.op('pe', lambda e: e.transpose(out=trb[:, 128:256], in_=A[:, c, 128:256], identity=self.ident_b),
                 reads=[bA, self.b_const], writes=[btr])
            q_t, bqt = qt_[d].next()
            k_t, bkt = kt_[d].next()
            k_h, bkh = kh_[d].next()
            s.op('pool', lambda e: e.tensor_tensor(out=k_h, in0=A[:, c, 128:256], in1=eh, op=ALU.mult),
                 reads=[bA, beh], writes=[bkh])
            s.op('dve', lambda e: e.scalar_tensor_tensor(out=q_t, in0=trb[:, 0:128], scalar=32.0 ** -0.5, in1=eq,
                                                         op0=ALU.mult, op1=ALU.mult), reads=[btr, beq], writes=[bqt])
            s.op('dve', lambda e: e.tensor_tensor(out=k_t, in0=trb[:, 128:256], in1=ek, op=ALU.mult),
                 reads=[btr, bek], writes=[bkt])
            yield
            q4, bq4 = Q4[d].next()
            s.op('pool', lambda e: e.tensor_tensor(
                out=q4.rearrange("p (h t) -> p h t", h=4), in0=bc(q_t.unsqueeze(1), [128, 4, 128]),
                in1=bc(hm.unsqueeze(2), [128, 4, 128]), op=ALU.mult), reads=[bqt, bK], writes=[bq4])
            yield
            s.op('pe', lambda e: e.matmul(at[:, 0:512], lhsT=k_t, rhs=q4, start=True, stop=True),
                 reads=[bkt, bq4], writes=[bat])
            p_m, bpm = Pm[d].next()
            s.op('dve', lambda e: e.tensor_tensor(
                out=p_m.rearrange("p (h t) -> p h t", h=4), in0=at[:, 0:512].rearrange("p (h t) -> p h t", h=4),
                in1=bc(tri[:, mi_incl, :].unsqueeze(1), [128, 4, 128]), op=ALU.mult), reads=[bat, bK], writes=[bpm])
            yield
            s.op('pe', lambda e: e.matmul(ok[:, 0:256], lhsT=q_t, rhs=Sb[d], start=True, stop=False),
                 reads=[bqt, bS[d]], writes=[bok])
            for h in range(4):
                s.op('pe', (lambda h=h: lambda e: e.matmul(
                    ok[:, h * 64:(h + 1) * 64], lhsT=p_m[:, h * 128:(h + 1) * 128],
                    rhs=A[:, c, 256 + h * 64:256 + (h + 1) * 64], start=False, stop=(h == 3)))(),
                    reads=[bpm, bA], writes=[bok])
            s.op('pe', lambda e: e.matmul(ok[:, 256:512], lhsT=k_h, rhs=A[:, c, 256:512], start=True, stop=True),
                 reads=[bkh, bA], writes=[bok])
            last = 127 if d == 0 else 0
            s.op('dve', lambda e: e.scalar_tensor_tensor(
                out=Sf[d], in0=Sf[d], scalar=eq[:, last:last + 1], in1=ok[:, 256:512], op0=ALU.mult, op1=ALU.add),
                reads=[bok, beq, bS[d]], writes=[bS[d]])
            s.op('dve', lambda e: e.tensor_tensor(out=Sb[d], in0=Sf[d], in1=blkm, op=ALU.mult),
                 reads=[bK, bS[d]], writes=[bS[d]])
            if first:
                s.op('act', lambda e: e.activation(out=Ost[:, c, :], in_=ok[:, 0:256], func=AF.Copy),
                     reads=[bok], writes=[bOst[c]])
                yield
                return
            f, bf = fin.next()
            st_, bst = fst.next()
            s.op('dve', lambda e: e.tensor_tensor(out=f[:, 0, :], in0=ok[:, 0:256], in1=Ost[:, c, :], op=ALU.add),
                 reads=[bok, bOst[c]], writes=[bf])
            yield
            if need_ctx or c >= 2:
                s.op('act', lambda e: e.activation(out=f[:, 1, :], in_=f[:, 0, :], func=AF.Square),
                     reads=[bf], writes=[bf])
                s.op('dve', lambda e: e.tensor_reduce(
                    out=st_[:, 0:4], in_=f[:, 1, :].rearrange("p (h d) -> p h d", d=64), axis=AX.X, op=ALU.add),
                    reads=[bf], writes=[bst])
                s.op('dve', lambda e: e.tensor_scalar(out=st_[:, 0:4], in0=st_[:, 0:4], scalar1=1.0 / 64, scalar2=EPS,
                                                      op0=ALU.mult, op1=ALU.add), reads=[bst], writes=[bst])
                s.op('act', lambda e: e.activation(out=st_[:, 32:36], in_=st_[:, 0:4], func=AF.Ln),
                     reads=[bst], writes=[bst])
                s.op('act', lambda e: e.activation(out=st_[:, 16:20], in_=st_[:, 32:36], func=AF.Exp, scale=-0.5),
                     reads=[bst], writes=[bst])
                s.op('pool', lambda e: e.tensor_tensor(
                    out=f[:, 2, :].rearrange("p (h d) -> p h d", d=64),
                    in0=Ga[:, c, :].rearrange("p (h d) -> p h d", d=64),
                    in1=bc(onb.unsqueeze(1), [128, 4, 64]), op=ALU.mult), reads=[bG, bK], writes=[bf])
                s.op('pool', lambda e: e.tensor_tensor(
                    out=f[:, 1, :].rearrange("p (h d) -> p h d", d=64),
                    in0=f[:, 0, :].rearrange("p (h d) -> p h d", d=64),
                    in1=bc(st_[:, 16:20].unsqueeze(2), [128, 4, 64]), op=ALU.mult), reads=[bf, bst], writes=[bf])
                s.op('pool', lambda e: e.tensor_tensor(out=Ys[:, c, :], in0=f[:, 1, :], in1=f[:, 2, :], op=ALU.mult),
                     reads=[bf], writes=[bYc[c]])

        def ystore(c0, c1):
            return store_item(lambda: self.dma_tm('sp', Ys, self.Ymix[:, 0:256], c0, c1, False,
                                                  [bYc[c] for c in range(c0, c1)], [db['Ymix']]))(7)
        gens = []
        for i in range(NTILE):
            gens.append(step(order[0][i], 0))
            gens.append(step(order[1][i], 1))
            if i == 1 and need_ctx:
                gens.append(ystore(0, 2))
            if i >= 19 and i % 2 == 1:
                gens.append(ystore(i - 1, i + 1))
                gens.append(ystore(35 - i, 37 - i))
        pipelineN(gens, 7)

    def phase_b(self, l):
        nc, s = self.nc, self.s
        ar = self.arena
        ar.reset()
        need_ctx = (l < DEPTH - 1)
        db = self.dram_bufs
        X0 = ar.alloc([64, 64, 256], BF16, "fX0")
        x3_off = ar.off
        X0p = ar.alloc([64, 128, 128], BF16, "fX0p")
        X1 = ar.alloc([128, 128, 128], BF16, "fX1")
        M3 = ar.alloc([128, 64, 2, 128], BF16, "fM3")
        D1 = ar.alloc([64, 128], BF16, "fD1")
        CH = ar.alloc([128, 8, 128], BF16, "fCH")
        RT = ar.alloc([128, 2, NT], BF16, "fRT")
        Gb = ar.alloc([128, NTILE, 256], BF16, "fG")
        Ys = ar.alloc([128, NTILE, 256], BF16, "fY")
        fwf = ar.alloc([128, 2, 256], F32, "fwf")
        fwb = ar.alloc([128, 2, 256], BF16, "fwb")
        bX0, bX1, bX3, bK, bRT, bG, bY, bFW, bX0p = Buf(), Buf(), Buf(), Buf(), Buf(), Buf(), Buf(), Buf(), Buf()
        bv_l = self.Bv[NC_:NT, :].rearrange("(r c) k -> r c k", c=64)
        for q in range(4):
            s.dma('sp', X0[:, q * 16:(q + 1) * 16, :], bv_l[:, q * 16:(q + 1) * 16, :], reads=[db['Bv']], adds=[bX0])
        s.dma('sp', D1, self.fD1_d, adds=[bK])
        for q in range(4):
            s.dma('sp', M3[:, q * 16:(q + 1) * 16], self.fM3_d[:, q * 16:(q + 1) * 16], adds=[bK])
        s.dma('sp', CH, self.fCH_d, adds=[bK])
        s.dma('sp', fwf, self.L[l]['fw'].rearrange("(a p) n -> p a n", p=128), adds=[bFW])
        s.op('pool', lambda e: e.tensor_copy(out=fwb, in_=fwf), reads=[bFW], writes=[bFW])
        self.dma_tm('sp', Gb, self.Gs[:, 256:512], 0, NTILE, True, [db['Gs']], [bG])
        PS = Rot([self.ps[i] for i in range(8)])
        bYg = [Buf() for _ in range(NTILE)]
        evr = [0]

        def evac(out_ap, in_ap, rb, wb, add=False, extra=()):
            eng = 'act' if evr[0] % 2 == 0 else 'dve'
            evr[0] += 1
            kw = dict(adds=[wb] + list(extra)) if add else dict(writes=[wb])
            if eng == 'act':
                s.op('act', lambda e: e.activation(out=out_ap, in_=in_ap, func=AF.Copy), reads=[rb], **kw)
            else:
                s.op('dve', lambda e: e.tensor_copy(out=out_ap, in_=in_ap), reads=[rb], **kw)

        X0v = X0.rearrange("r c (q t) -> r q t c", t=2)
        X0pv = X0p.rearrange("r q (t c) -> r q t c", t=2)
        for qi, eng in enumerate(('dve', 'act', 'dve', 'act')):
            sl = slice(qi * 32, (qi + 1) * 32)
            if eng == 'act':
                s.op('act', (lambda sl=sl: lambda e: e.activation(out=X0pv[:, sl], in_=X0v[:, sl], func=AF.Copy))(),
                     reads=[bX0], adds=[bX0p])
            else:
                s.op(eng, (lambda sl=sl: lambda e: e.tensor_copy(out=X0pv[:, sl], in_=X0v[:, sl]))(),
                     reads=[bX0], adds=[bX0p])
        for q4 in range(32):
            bk, bbk = PS.next()
            for i in range(4):
                chp = q4 * 4 + i
                s.op('pe', (lambda chp=chp, i=i, bk=bk: lambda e: e.matmul(
                    bk[:, i * 128:(i + 1) * 128], lhsT=X0p[:, chp, :], rhs=D1, start=True, stop=True))(),
                    reads=[bX0p, bK], writes=[bbk])
            evac(X1[:, q4 * 4:(q4 + 1) * 4, :], bk[:, 0:512].rearrange("p (a n) -> p a n", a=4), bbk, bX1, add=True)
        X3 = ar.view([128, 64, 2, 128], BF16, x3_off - 64 * 256 * 2)
        assert x3_off - 64 * 256 * 2 == self.arena.base
        X1v = X1.rearrange("p q (k z) -> p k z q", z=2)
        for k4 in range(16):
            bkA, bbA = PS.next()
            bkB, bbB = PS.next()
            for i in range(4):
                k1 = k4 * 4 + i
                for z in range(2):
                    for ch2, (bk, bbk) in enumerate(((bkA, bbA), (bkB, bbB))):
                        ps_ = slice(ch2 * 64, (ch2 + 1) * 64)
                        s.op('pe', (lambda k1=k1, z=z, ps_=ps_, bk=bk, i=i: lambda e: e.matmul(
                            bk[:, i * 128:(i + 1) * 128], lhsT=X1v[ps_, k1, z, :], rhs=M3[ps_, k1, z, :],
                            start=(z == 0), stop=(z == 1)))(), reads=[bX1, bK], writes=[bbk])
            for ch2, (bk, bbk) in enumerate(((bkA, bbA), (bkB, bbB))):
                evac(X3[:, k4 * 4:(k4 + 1) * 4, ch2, :], bk[:, 0:512].rearrange("p (a n) -> p a n", a=4),
                     bbk, bX3, add=True, extra=[bX0])
        X3v = X3.rearrange("p k t (j z) -> p t z k j", z=2)
        RTl = RT[:, :, NC_:NT].rearrange("p a (j k) -> p a k j", k=64)
        for mc in range(2):
            for kb in range(8):
                bk, bbk = PS.next()
                n = 0
                for ch2 in range(2):
                    for z in range(2):
                        s.op('pe', (lambda mc=mc, kb=kb, ch2=ch2, z=z, n=n, bk=bk: lambda e: e.matmul(
                            bk[:, 0:512], lhsT=CH[:, mc * 4 + ch2 * 2 + z, :],
                            rhs=X3v[:, ch2, z, kb * 8:(kb + 1) * 8, :], start=(n == 0), stop=(n == 3)))(),
                            reads=[bX3, bK], writes=[bbk])
                        n += 1
                evac(RTl[:, mc, kb * 8:(kb + 1) * 8, :], bk[:, 0:512].rearrange("p (k j) -> p k j", k=8),
                     bbk, bRT, add=True)
        if need_ctx:
            Vc = ar.alloc([128, 2, 256], BF16, "fVc")
            DC = ar.alloc([128, 2, 512], BF16, "fDC")
            CHc = ar.alloc([128, 2, 128], BF16, "fCHc")
            X3c = ar.alloc([128, 2, 512], BF16, "fX3c")
            bC, bX3c = Buf(), Buf()
            s.dma('sp', Vc, self.Bv[0:NC_, :].rearrange("(i p) k -> p i k", p=128), reads=[db['Bv']], adds=[bC])
            s.dma('sp', DC, self.fDC_d, adds=[bC])
            s.dma('sp', CHc, self.fCHc_d, adds=[bC])
            for cc in range(2):
                bk, bbk = PS.next()
                for i in range(2):
                    s.op('pe', (lambda cc=cc, i=i, bk=bk: lambda e: e.matmul(
                        bk[:, 0:512], lhsT=Vc[:, i, cc * 128:(cc + 1) * 128], rhs=DC[:, i, :],
                        start=(i == 0), stop=(i == 1)))(), reads=[bC], writes=[bbk])
                evac(X3c[:, cc, :], bk[:, 0:512], bbk, bX3c, add=True)
            X3cv = X3c.rearrange("p a (k z) -> p a z k", z=2)
            for cc in range(2):
                bk, bbk = PS.next()
                for z in range(2):
                    s.op('pe', (lambda cc=cc, z=z, bk=bk: lambda e: e.matmul(
                        bk[:, 0:256], lhsT=CHc[:, z, :], rhs=X3cv[:, cc, z, :], start=(z == 0), stop=(z == 1)))(),
                        reads=[bX3c, bC], writes=[bbk])
                evac(RT[:, cc, 0:NC_], bk[:, 0:256], bbk, bRT, add=True)
        t0 = 0 if need_ctx else 2
        for ti in range(t0, NTILE):
            bk, bbk = PS.next()
            for mc in range(2):
                s.op('pe', (lambda ti=ti, mc=mc, bk=bk: lambda e: e.matmul(
                    bk[:, 0:256], lhsT=RT[:, mc, ti * 128:(ti + 1) * 128], rhs=fwb[:, mc, :],
                    start=(mc == 0), stop=(mc == 1)))(), reads=[bRT, bFW], writes=[bbk])
            s.op('dve', (lambda ti=ti, bk=bk: lambda e: e.tensor_tensor(
                out=Ys[:, ti, :], in0=bk[:, 0:256], in1=Gb[:, ti, :], op=ALU.mult))(), reads=[bbk, bG],
                adds=[bYg[ti // 4]])
            if ti % 4 == 3 or ti == NTILE - 1:
                self.dma_tm('sp', Ys, self.Ymix[:, 256:512], max(t0, (ti // 4) * 4), ti + 1, False,
                            [bYg[ti // 4]], [db['Ymix']])

    def phase_o(self, l):
        nc, s = self.nc, self.s
        ar = self.arena
        ar.reset()
        last = (l == DEPTH - 1)
        db = self.dram_bufs
        modT, bmod = self.modT[l], self.b_mod[l]
        nmod = 1 if last else 2
        gcol = ar.alloc([128, 8, 2, 128], F32, "gcol")
        GB = ar.alloc([128, 2, D], F32, "GB")
        Wo = [ar.alloc([128, 8, D], BF16, "Wo%d" % i) for i in range(nmod)]
        bgc, bGB, bWo = Buf(), Buf(), Buf()
        for i in range(nmod):
            s.op('dve', (lambda i=i: lambda e: e.tensor_copy(
                out=gcol[:, :, i, :], in_=bc(modT[:, 16:24, i:i + 1], [128, 8, 128])))(), reads=[bmod], adds=[bgc])
        for i in range(nmod):
            for hf in range(2):
                bk, bbk = self.ps[i * 2 + hf]
                for kk in range(4):
                    k = hf * 4 + kk
                    s.op('pe', (lambda i=i, k=k, kk=kk, bk=bk: lambda e: e.matmul(
                        bk[:, kk * 128:(kk + 1) * 128], lhsT=gcol[:, k, i, :], rhs=self.ident_f,
                        start=True, stop=True))(), reads=[bgc, self.b_const], writes=[bbk])
                s.op('act', (lambda i=i, hf=hf, bk=bk: lambda e: e.activation(
                    out=GB[:, i, hf * 512:(hf + 1) * 512], in_=bk[:, 0:512], func=AF.Copy))(),
                    reads=[bbk], adds=[bGB])
        wst = Rot([(ar.alloc([128, D], F32, "wst"), Buf()) for _ in range(2)])
        for mk in range(8):
            w, bw = wst.next()
            s.dma('sp', w, self.L[l]['wo'][mk * 128:(mk + 1) * 128, :], writes=[bw])
            for i in range(nmod):
                eng = 'dve' if i == 0 else 'pool'
                s.op(eng, (lambda i=i, mk=mk, w=w: lambda e: e.tensor_tensor(
                    out=Wo[i][:, mk, :], in0=w, in1=GB[:, i, :], op=ALU.mult))(), reads=[bw, bGB], adds=[bWo])
        yt = Rot([(ar.alloc([128, 2, D], BF16, "yt"), Buf()) for _ in range(3)])
        xt = Rot([(ar.alloc([128, 2, D], F32, "xt"), Buf()) for _ in range(3)])
        yT = Rot([(ar.alloc([128, 8, 128], BF16, "yT"), Buf()) for _ in range(4)])
        xo = Rot([(ar.alloc([128, 2, D], F32, "xo"), Buf()) for _ in range(2)])
        TPs = Rot([self.ps[4], self.ps[5]])
        MMs = Rot([self.ps[0], self.ps[1], self.ps[2], self.ps[3], self.ps[6], self.ps[7]])
        src = self.xin if l == 0 else self.X1
        bsrc = Buf() if l == 0 else db['X1']
        t0 = 2 if last else 0

        def tile2(ti):
            u0 = ti * 128
            wi = 0 if ti >= 2 else 1
            y_t, byt = yt.next()
            x_t, bxt = xt.next()
            tm = lambda dr: dr[u0:u0 + 256, :].rearrange("(i p) c -> p i c", p=128)
            s.dma('sp', y_t, tm(self.Ymix), reads=[db['Ymix']], writes=[byt])
            s.dma('sp', x_t, tm(src), reads=[bsrc], writes=[bxt])
            yTs = []
            for i in range(2):
                tp, btp = TPs.next()
                tpb = tp.bitcast(BF16)
                for k in range(8):
                    s.op('pe', (lambda k=k, i=i, tpb=tpb: lambda e: e.transpose(
                        out=tpb[:, k * 128:(k + 1) * 128], in_=y_t[:, i, k * 128:(k + 1) * 128],
                        identity=self.ident_b))(), reads=[byt, self.b_const], writes=[btp])
                y_T, byT = yT.next()
                s.op('act', (lambda y_T=y_T, tpb=tpb: lambda e: e.activation(
                    out=y_T.rearrange("p k t -> p (k t)"), in_=tpb[:, 0:1024], func=AF.Copy))(),
                    reads=[btp], writes=[byT])
                yTs.append((y_T, byT))
            yield
            x_o, bxo = xo.next()
            for i in range(2):
                y_T, byT = yTs[i]
                for nc_ in range(2):
                    bk, bbk = MMs.next()
                    for k in range(8):
                        s.op('pe', (lambda k=k, nc_=nc_, bk=bk, y_T=y_T: lambda e: e.matmul(
                            bk[:, 0:512], lhsT=y_T[:, k, :], rhs=Wo[wi][:, k, nc_ * 512:(nc_ + 1) * 512],
                            start=(k == 0), stop=(k == 7)))(), reads=[byT, bWo], writes=[bbk])
                    s.op('dve', (lambda nc_=nc_, bk=bk, i=i: lambda e: e.tensor_tensor(
                        out=x_o[:, i, nc_ * 512:(nc_ + 1) * 512], in0=bk[:, 0:512],
                        in1=x_t[:, i, nc_ * 512:(nc_ + 1) * 512], op=ALU.add))(), reads=[bbk, bxt], adds=[bxo])
            if last:
                s.dma('pool', self.out[u0 - NC_:u0 - NC_ + 256, :].rearrange("(i p) c -> p i c", p=128), x_o,
                      reads=[bxo], adds=[self.b_out])
            else:
                s.dma('pool', self.X1[u0:u0 + 256, :].rearrange("(i p) c -> p i c", p=128), x_o,
                      reads=[bxo], adds=[db['X1']])

        pipeline2([tile2(ti) for ti in range(t0, NTILE, 2)])

    def rsqrt1(self, v, y, t1, buf):
        s = self.s
        I32 = mybir.dt.int32
        vi, yi, t1i = v.bitcast(I32), y.bitcast(I32), t1.bitcast(I32)
        s.op('dve', lambda e: e.tensor_single_scalar(out=t1i, in_=vi, scalar=1, op=ALU.arith_shift_right),
             reads=[buf], writes=[buf])
        s.op('dve', lambda e: e.tensor_scalar(out=yi, in0=t1i, scalar1=-1.0, scalar2=1597463007.0,
                                              op0=ALU.mult, op1=ALU.add), reads=[buf], writes=[buf])
        for _ in range(2):
            s.op('dve', lambda e: e.scalar_tensor_tensor(out=t1, in0=y, scalar=v, in1=y, op0=ALU.mult, op1=ALU.mult),
                 reads=[buf], writes=[buf])
            s.op('dve', lambda e: e.tensor_scalar(out=t1, in0=t1, scalar1=-0.5, scalar2=1.5,
                                                  op0=ALU.mult, op1=ALU.add), reads=[buf], writes=[buf])
            s.op('dve', lambda e: e.tensor_tensor(out=y, in0=y, in1=t1, op=ALU.mult), reads=[buf], writes=[buf])

    def rsqrt(self, v, y, t1, t2, buf):
        s = self.s
        I32 = mybir.dt.int32
        vi, yi, t1i = v.bitcast(I32), y.bitcast(I32), t1.bitcast(I32)
        s.op('dve', lambda e: e.tensor_single_scalar(out=t1i, in_=vi, scalar=1, op=ALU.arith_shift_right),
             reads=[buf], writes=[buf])
        s.op('dve', lambda e: e.tensor_scalar(out=yi, in0=t1i, scalar1=-1.0, scalar2=1597463007.0,
                                              op0=ALU.mult, op1=ALU.add), reads=[buf], writes=[buf])
        for _ in range(2):
            s.op('dve', lambda e: e.tensor_tensor(out=t1, in0=v, in1=y, op=ALU.mult), reads=[buf], writes=[buf])
            s.op('dve', lambda e: e.tensor_tensor(out=t2, in0=t1, in1=y, op=ALU.mult), reads=[buf], writes=[buf])
            s.op('dve', lambda e: e.tensor_scalar(out=t2, in0=t2, scalar1=-0.5, scalar2=1.5,
                                                  op0=ALU.mult, op1=ALU.add), reads=[buf], writes=[buf])
            s.op('dve', lambda e: e.tensor_tensor(out=y, in0=y, in1=t2, op=ALU.mult), reads=[buf], writes=[buf])

    def _norm_heads(self, pm, bpm, nh, wt, b_w, sq, nt, st8, out_f32, rot_out=None, out=None):
        s = self.s
        w = nh * 64
        sq_t, bsq = sq.next()
        s.op('act', lambda e: e.activation(out=sq_t[:, 0:w], in_=pm[:, 0:w], func=AF.Square),
             reads=[bpm], writes=[bsq])
        s8, bs8 = st8.next()
        s.op('dve', lambda e: e.tensor_reduce(out=s8[:, 0:nh], in_=sq_t[:, 0:w].rearrange("p (h d) -> p h d", d=64),
                                              axis=AX.X, op=ALU.add), reads=[bsq], writes=[bs8])
        s.op('dve', lambda e: e.tensor_scalar(out=s8[:, 0:nh], in0=s8[:, 0:nh], scalar1=1.0 / 64, scalar2=EPS,
                                              op0=ALU.mult, op1=ALU.add), reads=[bs8], writes=[bs8])
        s.op('dve', lambda e: e.tensor_scalar(out=s8[:, 8:8 + nh], in0=s8[:, 0:nh], scalar1=-0.5, scalar2=None,
                                              op0=ALU.pow), reads=[bs8], writes=[bs8])
        n_t, bnt = nt.next()
        s.op('dve', lambda e: e.tensor_tensor(
            out=n_t[:, 0:w].rearrange("p (h d) -> p h d", d=64), in0=pm[:, 0:w].rearrange("p (h d) -> p h d", d=64),
            in1=bc(s8[:, 8:8 + nh].unsqueeze(2), [128, nh, 64]), op=ALU.mult), reads=[bpm, bs8], writes=[bnt])
        if out_f32:
            o, bo = rot_out.next()
        else:
            o, bo = out
        s.op('pool', lambda e: e.tensor_tensor(out=o[:, 0:w], in0=n_t[:, 0:w], in1=wt[:, 0:w], op=ALU.mult),
             reads=[bnt, b_w], writes=[bo])
        self._last_norm = (o, bo)

    @property
    def xin_l(self):
        return self.xin


def _col_perm():
    off = {}
    o = 0
    for name, w in (("a_q", 128), ("a_k", 128), ("a_v", 256), ("a_g", 256), ("a_lr", 32), ("b_v", 256),
                    ("b_g", 256), ("c_q", 256), ("c_k", 128), ("c_v", 128), ("c_g", 256), ("d_q", 256),
                    ("d_k", 256), ("d_v", 256), ("d_g", 256)):
        off[name] = (o, w)
        o += w
    order = ["a_q", "a_k", "a_v", "a_g", "b_g", "c_g", "d_g", "c_q", "c_k", "c_v", "d_q", "d_k", "d_v", "b_v", "a_lr"]
    perm = []
    for nm in order:
        a, w = off[nm]
        perm.extend(range(a, a + w))
    return np.array(perm)


def _rope_table():
    t = np.arange(NL)
    row = (t // 64).astype(np.float32)
    col = (t % 64).astype(np.float32)
    inv = (10000.0 ** (-np.arange(0, 32, 2, dtype=np.float32) / 32)).astype(np.float32)
    ang = np.stack([row[:, None] * inv, col[:, None] * inv], axis=1)
    cs = np.concatenate([np.cos(ang).reshape(NL, 32), np.sin(ang).reshape(NL, 32)], axis=1)
    return cs.astype(np.float32)


def na_geometry():
    r = np.arange(64)
    row_start = np.clip(r - 4, 0, 56)
    col_start = np.clip(r - 8, 0, 48)
    kk = np.arange(128)
    ka, kc = (kk // 64)[:, None], (kk % 64)[:, None]
    qa, qc = (kk // 64)[None, :], (kk % 64)[None, :]
    vcol = (kc >= col_start[qc]) & (kc < col_start[qc] + 16)
    dx = np.clip(kc - qc, -15, 15) + 15
    sigs, table, nbrs, mats = {}, {}, [], []
    for n in range(32):
        nb = []
        rr = 2 * n + qa
        rs = row_start[rr]
        for m in range(32):
            a = 2 * m + ka
            valid = (a >= rs) & (a < rs + 8) & vcol
            if not valid.any():
                continue
            dy = np.where(valid, a - rr + 7, 0)
            sig = (valid.tobytes(), dy.tobytes())
            if sig not in sigs:
                sigs[sig] = len(mats)
                mats.append((valid, dy))
            table[(n, m)] = sigs[sig]
            nb.append(m)
        nbrs.append(nb)
    return dict(ntypes=len(mats), table=table, nbrs=nbrs, mats=mats, dx=dx)


def na_bias_mats(rel_bias, G):
    out = np.empty((128, 4 * G['ntypes'], 128), dtype=np.float32)
    for h in range(4):
        for t, (valid, dy) in enumerate(G['mats']):
            out[:, h * G['ntypes'] + t, :] = np.where(valid, rel_bias[h][dy, G['dx']], NEG)
    return out.astype(ml_dtypes.bfloat16)


def fnet_consts():
    bf = ml_dtypes.bfloat16
    r = np.arange(64, dtype=np.float64)
    k1 = np.arange(64, dtype=np.float64)
    a = 2 * np.pi * np.outer(r, k1) / 64.0
    D1 = np.stack([np.cos(a), -np.sin(a)], axis=2).reshape(64, 128)
    c = np.arange(64, dtype=np.float64)[:, None, None]
    kk1 = np.arange(64, dtype=np.float64)[None, :, None]
    k2 = np.arange(64, dtype=np.float64)[None, None, :]
    th = 2 * np.pi * c * (kk1 + 64 * k2) / 4096.0
    m3r, m3i = np.cos(th) / 512.0, -np.sin(th) / 512.0
    ra = np.stack([m3r, m3i], axis=3)
    rb = np.stack([-m3i, m3r], axis=3)
    M3 = np.stack([ra, rb], axis=2).reshape(64, 64, 2, 128)
    M3 = np.concatenate([M3, M3], axis=0)
    j = np.arange(64, dtype=np.float64)
    ph = 2 * np.pi * np.outer(j, j) / 64.0
    C, S = np.cos(ph), np.sin(ph)
    CH = np.zeros((128, 2, 2, 2, 2, 64))
    for chp in range(128):
        g, jj = chp // 32, chp % 32
        for ch2 in range(2):
            CH[chp, g // 2, ch2, 0, g % 2, :] = C[2 * jj + ch2]
            CH[chp, g // 2, ch2, 1, g % 2, :] = S[2 * jj + ch2]
    CH = CH.reshape(128, 8, 128)
    t = np.arange(256, dtype=np.float64)
    ac = 2 * np.pi * np.outer(t, t) / 256.0
    DC = np.stack([np.cos(ac), -np.sin(ac)], axis=2).reshape(2, 128, 512).transpose(1, 0, 2) / 128.0
    CHc = np.zeros((128, 2, 2, 64))
    for p_ in range(128):
        CHc[p_, 0, p_ // 64, :] = C[p_ % 64]
        CHc[p_, 1, p_ // 64, :] = S[p_ % 64]
    CHc = CHc.reshape(128, 2, 128)
    return dict(fD1=D1.astype(bf), fM3=M3.astype(bf), fCH=CH.astype(bf), fDC=np.ascontiguousarray(DC).astype(bf),
                fCHc=CHc.astype(bf))


_FC = {}


def make_core_inputs(b, inp, depth=DEPTH):
    f = lambda a: np.ascontiguousarray(np.asarray(a, dtype=np.float32))
    m = {}
    m["xin"] = f(np.concatenate([np.asarray(inp["ctx"][b]), np.asarray(inp["x"][b])], axis=0))
    cv = np.stack([np.asarray(inp["c"][b]).reshape(8, 128).T, np.asarray(inp["c_ctx"]).reshape(8, 128).T], axis=2)
    m["cvec"] = f(cv)
    m["ident_f"] = np.eye(128, dtype=np.float32)
    m["ident_b"] = np.eye(128).astype(ml_dtypes.bfloat16)
    m["cs_tab"] = _rope_table()
    jj, ii = np.arange(128)[:, None], np.arange(128)[None, :]
    m["cmask"] = np.stack([(ii <= jj), (jj <= ii)], axis=1).astype(ml_dtypes.bfloat16)
    G = na_geometry()
    if not _FC:
        _FC.update(fnet_consts())
    m.update(_FC)
    ss, tt = np.arange(128)[:, None], np.arange(128)[None, :]
    m["tri"] = np.stack([ss <= tt, ss >= tt, ss > tt, ss < tt], axis=1).astype(np.float32)
    hd = np.arange(128)[:, None] // 32
    m["blkmask"] = np.concatenate([(hd == (np.arange(256)[None, :] // 64)), (hd == np.arange(4)[None, :])],
                                  axis=1).astype(np.float32)
    perm = _col_perm()
    for l in range(depth):
        m["ada_w%d" % l] = f(inp["ada_w"][l])
        m["ada_brow%d" % l] = f(np.broadcast_to(np.asarray(inp["ada_b"][l])[None, :], (2, 3 * D)))
        m["norm_wT%d" % l] = f(np.asarray(inp["norm_w"][l]).reshape(8, 128).T)
        m["wp%d" % l] = f(np.asarray(inp["w_in"][l])[:, perm])
        qn, kn = np.asarray(inp["swa_q_norm"][l]), np.asarray(inp["swa_k_norm"][l])
        m["wc%d" % l] = f(np.broadcast_to(np.concatenate([np.tile(qn, 4), np.tile(kn, 2)])[None, :], (128, 384)))
        qn, kn = np.asarray(inp["na_q_norm"][l]), np.asarray(inp["na_k_norm"][l])
        m["wd%d" % l] = f(np.broadcast_to(np.concatenate([np.tile(qn, 4), np.tile(kn, 4)])[None, :], (128, 512)))
        wdec = np.zeros((33, 256), np.float32)
        wdec[0:16, 0:128] = np.asarray(inp["gla_dec_w"][l][0])
        wdec[16:32, 128:256] = np.asarray(inp["gla_dec_w"][l][1])
        wdec[32, :] = np.asarray(inp["gla_dec_b"][l]).reshape(256)
        m["wdec%d" % l] = wdec
        m["fw%d" % l] = f(inp["fnet_w"][l])
        m["wo%d" % l] = f(inp["w_out"][l])
        m["onb%d" % l] = f(np.broadcast_to(np.asarray(inp["gla_out_norm"][l])[None, :], (128, 64)))
        m["sinkb%d" % l] = f(np.broadcast_to(np.asarray(inp["swa_sink"][l])[None, :], (128, 4)))
        m["nab%d" % l] = na_bias_mats(np.asarray(inp["na_rel_bias"][l], dtype=np.float32), G)
    return m


_CACHE = {}


def kernel(**inputs):
    if "nc" not in _CACHE:
        _CACHE["nc"] = Builder().build()
    nc = _CACHE["nc"]
    ncores = int(os.environ.get("K_NCORES", "8"))
    in_maps = [make_core_inputs(i % 4, inputs) for i in range(ncores)]
    res = run_bass_kernel_spmd(nc, in_maps, core_ids=list(range(ncores)))
    out = np.stack([np.asarray(res.results[b]["out"], dtype=np.float32) for b in range(4)], axis=0)
    return out
```

```python
import os
import numpy as np
import ml_dtypes
import concourse.bass as bass
import concourse.mybir as mybir
from concourse.bass_utils import run_bass_kernel_spmd

F32 = mybir.dt.float32
BF16 = mybir.dt.bfloat16
AF = mybir.ActivationFunctionType
ALU = mybir.AluOpType
AX = mybir.AxisListType

D = 1024
NL = 4096
NC_ = 256
NT = NL + NC_
NTILE = NT // 128
DEPTH = 2
EPS = 1e-6
PW = 3104
GROUPS = [(0, 2)] + [(2 + 4 * i, 4) for i in range(8)]
NEG = -30000.0

ENGS = ['pe', 'act', 'dve', 'pool', 'sp']
NDMA = 40


class Buf:
    __slots__ = ('w', 'wa', 'r', 'excl')

    def __init__(self, excl=False):
        self.w = {}
        self.wa = {}
        self.r = {}
        self.excl = excl


MAXOUT = int(os.environ.get('KS_MAXOUT', '6'))
NOADDS = os.environ.get('KS_NOADDS', '0') == '1'


class Sched:
    def __init__(self):
        self.prog = {e: [] for e in ENGS}
        self.cnt = {e: 0 for e in ENGS}
        self.waited = {e: {} for e in ENGS}
        self.dma_val = [0] * NDMA
        self.dma_rr = 0
        self.dma_rr2 = 0
        self.outst = {e: [] for e in ENGS}

    def _waits(self, eng, deps):
        for key, val in deps:
            if key == 'pe' and eng == 'pe':
                continue
            if self.waited[eng].get(key, 0) >= val:
                continue
            self.waited[eng][key] = val
            self.prog[eng].append(('wait', key, val))

    @staticmethod
    def _deps(reads, writes, adds=()):
        deps = []
        for b in reads:
            deps.extend(b.w.items())
            deps.extend(b.wa.items())
        for b in writes:
            deps.extend(b.w.items())
            deps.extend(b.wa.items())
            deps.extend(b.r.items())
        for b in adds:
            deps.extend(b.w.items())
            deps.extend(b.r.items())
        return deps

    @staticmethod
    def _mark(tok, reads, writes, adds=()):
        for b in reads:
            if b.r.get(tok[0], 0) < tok[1]:
                b.r[tok[0]] = tok[1]
        for b in writes:
            b.w = {tok[0]: tok[1]}
            b.wa = {}
            b.r = {}
        for b in adds:
            if b.wa.get(tok[0], 0) < tok[1]:
                b.wa[tok[0]] = tok[1]

    def op(self, eng, fn, reads=(), writes=(), adds=()):
        if NOADDS:
            writes, adds = list(writes) + list(adds), ()
        ex = [b for b in reads if b.excl]
        if ex:
            reads = [b for b in reads if not b.excl]
            writes = list(writes) + ex
        self._waits(eng, self._deps(reads, writes, adds))
        self.cnt[eng] += 1
        tok = (eng, self.cnt[eng])
        self.prog[eng].append(('op', fn, eng))
        self._mark(tok, reads, writes, adds)
        return tok

    def dma(self, q, out_ap, in_ap, reads=(), writes=(), adds=()):
        if NOADDS:
            writes, adds = list(writes) + list(adds), ()
        half = NDMA // 2
        if q == 'sp':
            i = self.dma_rr % half
            self.dma_rr += 1
        else:
            i = half + self.dma_rr2 % half
            self.dma_rr2 += 1
        key = 'd%d' % i
        deps = self._deps(reads, writes, adds)
        if self.dma_val[i] > 0:
            deps.append((key, self.dma_val[i]))
        if len(self.outst[q]) >= (MAXOUT if q == 'sp' else 10):
            deps.append(self.outst[q].pop(0))
        self._waits(q, deps)
        self.dma_val[i] += 16
        tok = (key, self.dma_val[i])
        self.outst[q].append(tok)
        self.prog[q].append(('dma', out_ap, in_ap, key))
        self._mark(tok, reads, writes, adds)
        return tok

    def barrier(self):
        for e in ENGS:
            deps = [(e2, self.cnt[e2]) for e2 in ENGS if e2 != e and self.cnt[e2] > 0]
            deps += [('d%d' % i, v) for i, v in enumerate(self.dma_val) if v > 0]
            self._waits(e, deps)

    def emit(self, nc, sems):
        if os.environ.get("K_MULTIBLOCK", "0") == "1":
            return self.flush(nc, sems)
        self.barrier()

    def flush(self, nc, sems):
        self.barrier()
        prog = self.prog
        self.prog = {e: [] for e in ENGS}

        def mk(e):
            def f(engobj):
                for it in prog[e]:
                    if it[0] == 'wait':
                        engobj.wait_ge(sems[it[1]], it[2])
                    elif it[0] == 'op':
                        it[1](engobj).then_inc(sems[it[2]], 1)
                    else:
                        engobj.dma_start(out=it[1], in_=it[2]).then_inc(sems[it[3]], 16)
            return f

        with nc.Block() as block:
            block.tensor(mk('pe'))
            block.scalar(mk('act'))
            block.vector(mk('dve'))
            block.gpsimd(mk('pool'))
            block.sync(mk('sp'))


class Arena:
    _bases = {}

    def __init__(self, nc, base, limit):
        self.nc = nc
        self.base = base
        self.limit = limit
        self.off = base
        key = (id(nc), base, limit)
        if key not in Arena._bases:
            t = nc.alloc_sbuf_tensor_at("arena_%d" % base, [128, (limit - base) // 2], BF16, offset=base)
            Arena._bases[key] = t.ap()
        self.ap = Arena._bases[key]

    def reset(self):
        self.off = self.base

    def view(self, shape, dtype, off):
        esz = 4 if dtype == F32 else 2
        n = int(np.prod(shape[1:]))
        o2 = (off - self.base) // 2
        v = self.ap[0:shape[0], o2:o2 + n * esz // 2]
        if dtype != BF16:
            v = v.bitcast(dtype)
        if len(shape) > 2:
            names = " ".join("d%d" % i for i in range(len(shape) - 1))
            kw = {"d%d" % i: shape[i + 1] for i in range(len(shape) - 1)}
            v = v.rearrange("p (%s) -> p %s" % (names, names), **kw)
        return v

    def alloc(self, shape, dtype, name=None):
        esz = 4 if dtype == F32 else 2
        nbytes = int(np.prod(shape[1:])) * esz
        nbytes = (nbytes + 63) // 64 * 64
        assert self.off + nbytes <= self.limit, ("SBUF arena overflow", self.off, nbytes, self.limit)
        v = self.view(shape, dtype, self.off)
        self.off += nbytes
        return v


class Rot:
    def __init__(self, items):
        self.items = items
        self.i = 0

    def next(self):
        it = self.items[self.i % len(self.items)]
        self.i += 1
        return it


def pipeline2(gens):
    prev = None
    for g in gens:
        next(g)
        if prev is not None:
            for _ in prev:
                pass
        prev = g
    if prev is not None:
        for _ in prev:
            pass


def pipelineN(gens, nstage):
    n = len(gens)
    for t in range(n + nstage - 1):
        for k in range(nstage):
            i = t - k
            if 0 <= i < n:
                try:
                    next(gens[i])
                except StopIteration:
                    assert k == nstage - 1, (k, nstage)
                else:
                    assert k < nstage - 1, (k, nstage)


def store_item(fn):
    def g(nstage):
        for _ in range(nstage - 1):
            yield
        fn()
    return g


def bc(ap, shape):
    return ap.broadcast_to(list(shape))


class Builder:
    def __init__(self, debug=False, depth=DEPTH, stop_after=None):
        self.debug = debug
        self.depth = depth
        self.stop_after = stop_after
        nc = bass.Bass("TRN2", target_bir_lowering=False)
        self.nc = nc
        self.s = Sched()
        self.dbg_outs = []
        self.dram_bufs = {}

    def dma_tm(self, q, sb, dr, t0, t1, load, reads, writes, step=4):
        for a in range(t0, t1, step):
            b = min(a + step, t1)
            d = dr[a * 128:b * 128, :].rearrange("(i p) c -> p i c", p=128)
            sview = sb[:, a:b, :]
            if load:
                self.s.dma(q, sview, d, reads=reads, adds=writes)
            else:
                self.s.dma(q, d, sview, reads=reads, adds=writes)

    def din(self, name, shape, dtype=F32):
        return self.nc.dram_tensor(name, list(shape), dtype, kind="ExternalInput").ap()

    def dscr(self, name, shape, dtype):
        kind = "ExternalOutput" if self.debug else "Internal"
        t = self.nc.dram_tensor(name, list(shape), dtype, kind=kind).ap()
        if self.debug:
            self.dbg_outs.append(name)
        self.dram_bufs[name] = Buf()
        return t

    def build(self):
        nc = self.nc
        s = self.s
        self.xin = self.din("xin", [NT, D])
        self.cvec = self.din("cvec", [128, 8, 2])
        self.ident_f_d = self.din("ident_f", [128, 128])
        self.ident_b_d = self.din("ident_b", [128, 128], BF16)
        self.cs_d = self.din("cs_tab", [NL, 64])
        self.L = []
        for l in range(self.depth):
            Lw = dict(
                ada_w=self.din("ada_w%d" % l, [D, 3 * D]),
                ada_brow=self.din("ada_brow%d" % l, [2, 3 * D]),
                norm_wT=self.din("norm_wT%d" % l, [128, 8]),
                wp=self.din("wp%d" % l, [D, PW]),
                wc=self.din("wc%d" % l, [128, 384]),
                wd=self.din("wd%d" % l, [128, 512]),
            )
            self.L.append(Lw)
        self.out = self.nc.dram_tensor("out", [NL, D], F32, kind="ExternalOutput").ap()
        self.Gs = self.dscr("Gs", [NT, 1024], BF16)
        self.Aqkv = self.dscr("Aqkv", [NT, 512], BF16)
        self.LrT = self.dscr("LrT", [32, NT], F32)
        self.CT = self.dscr("CT", [4, 128, NT], BF16)
        self.CV1 = self.dscr("CV1", [NT, 132], BF16)
        self.DT = self.dscr("DT", [4, 128, NT], BF16)
        self.DV1 = self.dscr("DV1", [NT, 264], BF16)
        self.Bv = self.dscr("Bv", [NT, 256], BF16)
        self.NAG = na_geometry()
        if self.debug:
            self.modT_d = self.dscr("modT_dbg", [128, 48], F32)
        if self.stop_after == 'p12':
            return self._build_rest()
        self.Ymix = self.dscr("Ymix", [NT, 1024], BF16)
        self.X1 = self.dscr("X1", [NT, D], F32)
        self.cmask_d = self.din("cmask", [128, 2, 128], BF16)
        self.tri_d = self.din("tri", [128, 4, 128])
        self.fD1_d = self.din("fD1", [64, 128], BF16)
        self.fM3_d = self.din("fM3", [128, 64, 2, 128], BF16)
        self.fCH_d = self.din("fCH", [128, 8, 128], BF16)
        self.fDC_d = self.din("fDC", [128, 2, 512], BF16)
        self.fCHc_d = self.din("fCHc", [128, 2, 128], BF16)
        self.blk_d = self.din("blkmask", [128, 260])
        for l in range(self.depth):
            self.L[l]['sinkb'] = self.din("sinkb%d" % l, [128, 4])
            self.L[l]['wdec'] = self.din("wdec%d" % l, [33, 256])
            self.L[l]['onb'] = self.din("onb%d" % l, [128, 64])
            self.L[l]['fw'] = self.din("fw%d" % l, [256, 256])
            self.L[l]['wo'] = self.din("wo%d" % l, [D, D])
            self.L[l]['nab'] = self.din("nab%d" % l, [128, 4 * self.NAG['ntypes'], 128], BF16)
        return self._build_rest()

    def _build_rest(self):
        nc = self.nc
        s = self.s
        from contextlib import ExitStack
        with ExitStack() as st:
            sems = {}
            for e in ENGS:
                sems[e] = st.enter_context(nc.semaphore("sem_" + e))
            for i in range(NDMA):
                sems['d%d' % i] = st.enter_context(nc.semaphore("semd%d" % i))
            self.sems = sems
            self.ps = []
            for i in range(8):
                t = nc.alloc_psum_tensor("psb%d" % i, [128, 512], F32)
                self.ps.append((t.ap(), Buf(excl=True)))
            pa = Arena(nc, 16512, 16512 + 12 * 1024)
            self.ident_f = pa.alloc([128, 128], F32, "idf")
            self.ident_b = pa.alloc([128, 128], BF16, "idb")
            self.scT = pa.alloc([128, 8, 2], F32, "scT")
            self.modT = [pa.alloc([128, 24, 2], F32, "modT%d" % l) for l in range(self.depth)]
            self.Amod = [pa.alloc([128, 8, 2], F32, "Amod%d" % l) for l in range(self.depth)]
            self.cvs = pa.alloc([128, 8, 2], F32, "cvs")
            self.b_const = Buf()
            self.b_out = Buf()
            self.b_mod = [Buf() for _ in range(self.depth)]
            self.arena = Arena(nc, 16512 + 12 * 1024, 229312)

            s.dma('sp', self.ident_f, self.ident_f_d, adds=[self.b_const])
            s.dma('sp', self.ident_b, self.ident_b_d, adds=[self.b_const])
            s.dma('sp', self.cvs, self.cvec, adds=[self.b_const])
            s.op('act', lambda e: e.activation(out=self.scT, in_=self.cvs, func=AF.Silu),
                 reads=[self.b_const], writes=[self.b_const])
            for l in range(self.depth):
                self.phase_mod(l)
                s.emit(nc, sems)
            for l in range(self.depth):
                self.phase_weights(l)
                s.emit(nc, sems)
                self.x_src = self.xin if l == 0 else self.X1
                self.b_xsrc = Buf() if l == 0 else self.dram_bufs['X1']
                self.phase_p12(l)
                s.emit(nc, sems)
                if self.stop_after == 'p12':
                    break
                self.phase_c(l)
                s.emit(nc, sems)
                if self.stop_after == 'c':
                    break
                self.phase_d(l)
                s.emit(nc, sems)
                if self.stop_after == 'cd':
                    break
                self.phase_a(l)
                s.emit(nc, sems)
                if self.stop_after == 'a':
                    break
                self.phase_b(l)
                s.emit(nc, sems)
                if self.stop_after == 'b':
                    break
                self.phase_o(l)
                s.emit(nc, sems)
            s.flush(nc, sems)
        return nc

    def phase_mod(self, l):
        nc, s = self.nc, self.s
        ar = self.arena
        ar.reset()
        Lw = self.L[l]
        wb = Rot([(ar.alloc([128, 3 * D], F32, "adaw"), Buf()) for _ in range(4)])
        row = ar.alloc([2, 3 * D], F32, "modrow")
        brow = ar.alloc([2, 3 * D], F32, "brow")
        nwT = ar.alloc([128, 8], F32, "nwT")
        tmp = ar.alloc([128, 8, 2], F32, "tmpm")
        b_small, b_row = Buf(), Buf()
        s.dma('sp', brow, Lw['ada_brow'], adds=[b_small])
        s.dma('sp', nwT, Lw['norm_wT'], adds=[b_small])
        for k in range(8):
            w, bw = wb.next()
            s.dma('sp', w, Lw['ada_w'][k * 128:(k + 1) * 128, :], writes=[bw])
            for cc in range(6):
                pk, bpk = self.ps[cc]
                s.op('pe', (lambda w=w, k=k, cc=cc, pk=pk: lambda e: e.matmul(
                    pk[0:2, 0:512], lhsT=self.scT[:, k, :], rhs=w[:, cc * 512:(cc + 1) * 512],
                    start=(k == 0), stop=(k == 7)))(), reads=[bw, self.b_const], writes=[bpk])
        for cc in range(6):
            pk, bpk = self.ps[cc]
            s.op('dve', (lambda cc=cc, pk=pk: lambda e: e.tensor_tensor(
                out=row[:, cc * 512:(cc + 1) * 512], in0=pk[0:2, 0:512], in1=brow[:, cc * 512:(cc + 1) * 512],
                op=ALU.add))(), reads=[bpk, b_small], adds=[b_row])
        psT, bpT = self.ps[6]
        for j in range(24):
            s.op('pe', (lambda j=j: lambda e: e.matmul(
                psT[:, 2 * j:2 * j + 2], lhsT=row[0:2, j * 128:(j + 1) * 128], rhs=self.ident_f[0:2, 0:2],
                start=True, stop=True))(), reads=[b_row, self.b_const], writes=[bpT])
        modT = self.modT[l]
        bmod = self.b_mod[l]
        s.op('dve', lambda e: e.tensor_copy(out=modT, in_=psT[:, 0:48].rearrange("p (j i) -> p j i", i=2)),
             reads=[bpT], writes=[bmod])
        s.op('dve', lambda e: e.tensor_scalar(out=tmp, in0=modT[:, 8:16, :], scalar1=1.0, scalar2=None,
                                              op0=ALU.add), reads=[bmod], writes=[b_small])
        Am = self.Amod[l]
        s.op('dve', lambda e: e.tensor_tensor(out=Am, in0=tmp, in1=bc(nwT.unsqueeze(2), [128, 8, 2]),
                                              op=ALU.mult), reads=[b_small], writes=[bmod])
        if self.debug and l == 0:
            s.dma('pool', self.modT_d, modT.rearrange("p j i -> p (j i)"), reads=[bmod],
                  writes=[self.dram_bufs["modT_dbg"]])

    def phase_weights(self, l):
        nc, s = self.nc, self.s
        ar = self.arena
        ar.reset()
        self.Wb = ar.alloc([128, 8, PW], BF16, "Wb")
        self.b_W = Buf()
        self.p12_base = ar.off
        stg = Rot([(ar.alloc([128, PW], F32, "stgW"), Buf()) for _ in range(4)])
        wp = self.L[l]['wp']
        for k in range(8):
            w, bw = stg.next()
            s.dma('sp', w, wp[k * 128:(k + 1) * 128, :], writes=[bw])
            s.op('act', (lambda w=w, k=k: lambda e: e.activation(
                out=self.Wb[:, k, 0:1024], in_=w[:, 0:1024], func=AF.Copy))(), reads=[bw], adds=[self.b_W])
            s.op('dve', (lambda w=w, k=k: lambda e: e.tensor_copy(
                out=self.Wb[:, k, 1024:2048], in_=w[:, 1024:2048]))(), reads=[bw], adds=[self.b_W])
            s.op('pool', (lambda w=w, k=k: lambda e: e.tensor_copy(
                out=self.Wb[:, k, 2048:PW], in_=w[:, 2048:PW]))(), reads=[bw], adds=[self.b_W])

    def phase_p12(self, l):
        nc, s = self.nc, self.s
        ar = self.arena
        ar.off = self.p12_base
        Lw = self.L[l]
        Wb, bW = self.Wb, self.b_W
        Am, modT, bmod = self.Amod[l], self.modT[l], self.b_mod[l]
        wc = ar.alloc([128, 384], F32, "wc")
        wd = ar.alloc([128, 512], F32, "wd")
        b_nw = Buf()
        s.dma('sp', wc, Lw['wc'], adds=[b_nw])
        s.dma('sp', wd, Lw['wd'], adds=[b_nw])

        def rot(shape, dt, name, n):
            return Rot([(ar.alloc(shape, dt, name), Buf()) for _ in range(n)])
        xt = rot([128, D], F32, "xt", 3)
        junk = ar.alloc([128, D], BF16, "junk")
        b_junk = Buf()
        xn = rot([128, D], F32, "xn", 2)
        hT = rot([128, 8, 128], BF16, "hT", 3)
        st = rot([128, 8], F32, "stat", 4)
        raw = rot([128, 896], F32, "raw", 3)
        nt = rot([128, 896], F32, "nt", 3)
        nt2 = rot([128, 384], F32, "nt2", 3)
        st8 = rot([128, 64], F32, "st8", 4)
        rt = rot([128, 4, 192], F32, "rt", 2)
        qkr = rot([128, 384], BF16, "qkr", 3)
        kd = rot([128, 256], BF16, "kd", 3)
        dqk = rot([128, 512], BF16, "dqk", 3)
        cs = rot([128, 4, 64], F32, "cs", 3)

        def stage_set():
            d = dict(
                G=ar.alloc([128, 4, 1024], BF16, "sG"), A=ar.alloc([128, 4, 512], BF16, "sA"),
                CT=ar.alloc([128, 4, 512], BF16, "sCT"), CV=ar.alloc([128, 4, 2, 66], BF16, "sCV"),
                DT=ar.alloc([128, 4, 512], BF16, "sDT"), DV=ar.alloc([128, 4, 4, 66], BF16, "sDV"),
                BV=ar.alloc([128, 4, 256], BF16, "sBV"), LR=ar.alloc([32, 512], F32, "sLR"))
            d['buf'] = Buf()
            d['bufB'] = Buf()
            return d
        stg = Rot([stage_set(), stage_set()])
        for sset in stg.items:
            for nm in ('CV', 'DV'):
                s.op('pool', (lambda a=sset[nm]: lambda e: e.memset(a, 1.0))(), writes=[sset['buf']])

        TP = [self.ps[0], self.ps[1]]
        MM = Rot([self.ps[2], self.ps[3], self.ps[4], self.ps[5]])
        TQ, bTQ = self.ps[6]
        TL, bTL = self.ps[7]
        TQb = TQ.bitcast(BF16)
        db = self.dram_bufs

        def stores(sset, bS, t0, n4, batch):
            u0 = t0 * 128
            n = n4 * 128
            tm = lambda dr: dr[u0:u0 + n, :].rearrange("(i p) c -> p i c", p=128)
            if batch == 1:
                bSB = sset['bufB']
                s.dma('pool', self.CT[:, :, u0:u0 + n].rearrange("a p t -> p a t"), sset['CT'][:, :, 0:n],
                      reads=[bSB], adds=[db['CT']])
                s.dma('pool', self.DT[:, :, u0:u0 + n].rearrange("a p t -> p a t"), sset['DT'][:, :, 0:n],
                      reads=[bSB], adds=[db['DT']])
                return
            s.dma('pool', tm(self.Gs), sset['G'][:, 0:n4, :], reads=[bS], adds=[db['Gs']])
            s.dma('pool', tm(self.Aqkv), sset['A'][:, 0:n4, :], reads=[bS], adds=[db['Aqkv']])
            s.dma('pool', tm(self.CV1), sset['CV'][:, 0:n4].rearrange("p i g d -> p i (g d)"), reads=[bS],
                  adds=[db['CV1']])
            s.dma('pool', tm(self.DV1), sset['DV'][:, 0:n4].rearrange("p i g d -> p i (g d)"), reads=[bS],
                  adds=[db['DV1']])
            s.dma('pool', tm(self.Bv), sset['BV'][:, 0:n4, :], reads=[bS], adds=[db['Bv']])
            s.dma('pool', self.LrT[:, u0:u0 + n], sset['LR'][0:32, 0:n], reads=[bS], adds=[db['LrT']])

        def tile_body(t0, n4, i4, sset, cs_slot):
            bS = sset['buf']
            latent = t0 >= 2
            mi = 0 if latent else 1
            ti = t0 + i4
            u0 = ti * 128
            cst, bcs = cs_slot if latent else (None, None)
            if latent and i4 == 0:
                tl0 = (t0 - 2) * 128
                s.dma('sp', cst[:, 0:n4, :], self.cs_d[tl0:tl0 + n4 * 128, :].rearrange("(i p) c -> p i c", p=128),
                      writes=[bcs])
            x_t, bx = xt.next()
            s.dma('sp', x_t, self.x_src[u0:u0 + 128, :], reads=[self.b_xsrc], writes=[bx])
            stt, bst = st.next()
            s.op('act', lambda e: e.activation(out=junk, in_=x_t, func=AF.Square, accum_out=stt[:, 0:1]),
                 reads=[bx], writes=[b_junk, bst])
            yield
            s.op('dve', lambda e: e.tensor_scalar(out=stt[:, 1:2], in0=stt[:, 0:1], scalar1=1.0 / D, scalar2=EPS,
                                                  op0=ALU.mult, op1=ALU.add), reads=[bst], writes=[bst])
            self.rsqrt1(stt[:, 1:2], stt[:, 2:3], stt[:, 3:4], bst)
            yield
            x_n, bxn = xn.next()
            s.op('act', lambda e: e.activation(out=x_n, in_=x_t, func=AF.Identity, scale=stt[:, 2:3]),
                 reads=[bx, bst], writes=[bxn])
            yield
            h_t, bh = hT.next()
            for half in range(2):
                tp, btp = TP[half]
                for kk in range(4):
                    k = half * 4 + kk
                    s.op('pe', (lambda tp=tp, kk=kk, k=k: lambda e: e.transpose(
                        out=tp[:, kk * 128:(kk + 1) * 128], in_=x_n[:, k * 128:(k + 1) * 128],
                        identity=self.ident_f))(), reads=[bxn, self.b_const], writes=[btp])
                for kk in range(4):
                    k = half * 4 + kk
                    if False:
                        pass
                    else:
                        s.op('act', (lambda tp=tp, kk=kk, k=k: lambda e: e.activation(
                            out=h_t[:, k, :], in_=tp[:, kk * 128:(kk + 1) * 128], func=AF.Identity,
                            scale=Am[:, k, mi:mi + 1], bias=modT[:, k, mi:mi + 1]))(),
                            reads=[btp, bmod], adds=[bh])
            yield
            def mm_chunk(c):
                pm, bpm = MM.next()
                for k in range(8):
                    s.op('pe', (lambda pm=pm, k=k, c=c: lambda e: e.matmul(
                        pm[:, 0:512], lhsT=h_t[:, k, :], rhs=Wb[:, k, c * 512:(c + 1) * 512],
                        start=(k == 0), stop=(k == 7)))(), reads=[bh, bW], writes=[bpm])
                return pm, bpm
            r_w, brw = raw.next()
            pm, bpm = mm_chunk(0)
            s.op('act', (lambda pm=pm: lambda e: e.activation(out=sset['A'][:, i4, :], in_=pm, func=AF.Copy))(),
                 reads=[bpm], adds=[bS])
            for c in (1, 2):
                pm, bpm = mm_chunk(c)
                s.op('act', (lambda pm=pm, c=c: lambda e: e.activation(
                    out=sset['G'][:, i4, (c - 1) * 512:c * 512], in_=pm, func=AF.Silu))(), reads=[bpm], adds=[bS])
            pm, bpm = mm_chunk(3)
            s.op('dve', (lambda pm=pm: lambda e: e.tensor_copy(out=r_w[:, 0:384], in_=pm[:, 0:384]))(),
                 reads=[bpm], adds=[brw])
            s.op('dve', (lambda pm=pm: lambda e: e.tensor_copy(
                out=sset['CV'][:, i4, :, 0:64], in_=pm[:, 384:512].rearrange("p (g d) -> p g d", g=2)))(),
                reads=[bpm], adds=[bS])
            pm, bpm = mm_chunk(4)
            s.op('act', (lambda pm=pm: lambda e: e.activation(out=r_w[:, 384:896], in_=pm[:, 0:512], func=AF.Copy))(),
                 reads=[bpm], adds=[brw])
            pm, bpm = mm_chunk(5)
            s.op('dve', (lambda pm=pm: lambda e: e.tensor_copy(
                out=sset['DV'][:, i4, :, 0:64], in_=pm[:, 0:256].rearrange("p (g d) -> p g d", g=4)))(),
                reads=[bpm], adds=[bS])
            s.op('act', (lambda pm=pm: lambda e: e.activation(out=sset['BV'][:, i4, :], in_=pm[:, 256:512],
                                                              func=AF.Copy))(), reads=[bpm], adds=[bS])
            for k in range(8):
                s.op('pe', (lambda k=k: lambda e: e.matmul(
                    TL[0:32, 0:128], lhsT=Wb[:, k, 3072:3104], rhs=h_t[:, k, :],
                    start=(k == 0), stop=(k == 7)))(), reads=[bh, bW], writes=[bTL])
            s.op('act', lambda e: e.activation(out=sset['LR'][0:32, i4 * 128:(i4 + 1) * 128], in_=TL[0:32, 0:128],
                                               func=AF.Copy), reads=[bTL], adds=[bS])
            n_t, bnt = nt.next()
            s.op('pool', lambda e: e.tensor_tensor(out=n_t, in0=r_w, in1=r_w, op=ALU.mult), reads=[brw], writes=[bnt])
            if i4 == n4 - 1:
                stores(sset, bS, t0, n4, 0)
            yield
            s8, bs8 = st8.next()
            s.op('dve', lambda e: e.tensor_reduce(out=s8[:, 0:14], in_=n_t.rearrange("p (h d) -> p h d", d=64),
                                                  axis=AX.X, op=ALU.add), reads=[bnt], writes=[bs8])
            s.op('dve', lambda e: e.tensor_scalar(out=s8[:, 0:14], in0=s8[:, 0:14], scalar1=1.0 / 64, scalar2=EPS,
                                                  op0=ALU.mult, op1=ALU.add), reads=[bs8], writes=[bs8])
            self.rsqrt(s8[:, 0:14], s8[:, 16:30], s8[:, 32:46], s8[:, 48:62], bs8)
            s.op('dve', lambda e: e.tensor_scalar(out=s8[:, 16:20], in0=s8[:, 16:20], scalar1=0.125, scalar2=None,
                                                  op0=ALU.mult), reads=[bs8], writes=[bs8])
            s.op('dve', lambda e: e.tensor_scalar(out=s8[:, 22:26], in0=s8[:, 22:26], scalar1=0.125, scalar2=None,
                                                  op0=ALU.mult), reads=[bs8], writes=[bs8])
            yield
            s.op('dve', lambda e: e.tensor_tensor(
                out=n_t.rearrange("p (h d) -> p h d", d=64), in0=r_w.rearrange("p (h d) -> p h d", d=64),
                in1=bc(s8[:, 16:30].unsqueeze(2), [128, 14, 64]), op=ALU.mult), reads=[brw, bs8], writes=[bnt])
            q2, bq2 = nt2.next()
            s.op('pool', lambda e: e.tensor_tensor(out=q2, in0=n_t[:, 0:384], in1=wc, op=ALU.mult),
                 reads=[bnt, b_nw], writes=[bq2])
            d_t, bdt = dqk.next()
            s.op('pool', lambda e: e.tensor_tensor(out=d_t, in0=n_t[:, 384:896], in1=wd, op=ALU.mult),
                 reads=[bnt, b_nw], writes=[bdt])
            yield
            q_r, bqr = qkr.next()
            if latent:
                r_t, brt = rt.next()
                cst_i = cst[:, i4, :]
                xv = q2.rearrange("p (h a f) -> p h a f", a=2, f=16)
                ov = q_r.rearrange("p (h a f) -> p h a f", a=2, f=16)

                def csb(off):
                    v = cst_i[:, off:off + 32].rearrange("p (a f) -> p a f", a=2)
                    return bc(v.unsqueeze(1), [128, 6, 2, 16])
                x1h = xv[:, :, 0, :].rearrange("p (h a) f -> p h a f", a=2)
                x2h = xv[:, :, 1, :].rearrange("p (h a) f -> p h a f", a=2)
                o1h = ov[:, :, 0, :].rearrange("p (h a) f -> p h a f", a=2)
                o2h = ov[:, :, 1, :].rearrange("p (h a) f -> p h a f", a=2)
                tv = [r_t[:, j, :].rearrange("p (h a f) -> p h a f", a=2, f=16) for j in range(4)]
                cosb, sinb = csb(0), csb(32)
                s.op('pool', lambda e: e.tensor_tensor(out=tv[0], in0=x1h, in1=cosb, op=ALU.mult),
                     reads=[bq2, bcs], adds=[brt])
                s.op('pool', lambda e: e.tensor_tensor(out=tv[1], in0=x2h, in1=sinb, op=ALU.mult),
                     reads=[bq2, bcs], adds=[brt])
                s.op('dve', lambda e: e.tensor_tensor(out=tv[2], in0=x2h, in1=cosb, op=ALU.mult),
                     reads=[bq2, bcs], adds=[brt])
                s.op('dve', lambda e: e.tensor_tensor(out=tv[3], in0=x1h, in1=sinb, op=ALU.mult),
                     reads=[bq2, bcs], adds=[brt])
                s.op('pool', lambda e: e.tensor_tensor(out=o1h, in0=tv[0], in1=tv[1], op=ALU.subtract),
                     reads=[brt], writes=[bqr])
                s.op('dve', lambda e: e.tensor_tensor(out=o2h, in0=tv[2], in1=tv[3], op=ALU.add),
                     reads=[brt], writes=[bqr])
            else:
                s.op('pool', lambda e: e.tensor_copy(out=q_r, in_=q2), reads=[bq2], writes=[bqr])
            k_d, bkd = kd.next()
            s.op('pool', lambda e: e.tensor_copy(
                out=k_d.rearrange("p (g j d) -> p g j d", g=2, j=2),
                in_=bc(q_r[:, 256:384].rearrange("p (g d) -> p g d", g=2).unsqueeze(2), [128, 2, 2, 64])),
                reads=[bqr], writes=[bkd])
            yield
            for j in range(4):
                src = q_r[:, j * 128:(j + 1) * 128] if j < 2 else k_d[:, (j - 2) * 128:(j - 1) * 128]
                s.op('pe', (lambda src=src, j=j: lambda e: e.transpose(
                    out=TQb[:, j * 128:(j + 1) * 128], in_=src, identity=self.ident_b))(),
                    reads=[bqr, bkd, self.b_const], writes=[bTQ])
            for j in range(4):
                s.op('pe', (lambda j=j: lambda e: e.transpose(
                    out=TQb[:, 512 + j * 128:512 + (j + 1) * 128], in_=d_t[:, j * 128:(j + 1) * 128],
                    identity=self.ident_b))(), reads=[bdt, self.b_const], writes=[bTQ])
            s.op('act', lambda e: e.activation(
                out=sset['CT'][:, :, i4 * 128:(i4 + 1) * 128],
                in_=TQb[:, 0:512].rearrange("p (a t) -> p a t", a=4), func=AF.Copy), reads=[bTQ], adds=[sset['bufB']])
            s.op('act', lambda e: e.activation(
                out=sset['DT'][:, :, i4 * 128:(i4 + 1) * 128],
                in_=TQb[:, 512:1024].rearrange("p (a t) -> p a t", a=4), func=AF.Copy), reads=[bTQ], adds=[sset['bufB']])
            if i4 == n4 - 1:
                stores(sset, bS, t0, n4, 1)

        gens = []
        for (t0, n4) in GROUPS:
            sset = stg.next()
            cs_slot = cs.next() if t0 >= 2 else None
            for i4 in range(n4):
                gens.append(tile_body(t0, n4, i4, sset, cs_slot))
        pipelineN(gens, 9)

    def phase_c(self, l):
        return self.attn_phase(l, 'c')

    def phase_d(self, l):
        return self.attn_phase(l, 'd')

    def attn_phase(self, l, kind):
        nc, s = self.nc, self.s
        ar = self.arena
        ar.reset()
        need_ctx = (l < DEPTH - 1)
        db = self.dram_bufs
        isC = (kind == 'c')
        G = self.NAG
        ntp = G['ntypes']
        TT, V1d, vw, ycol = (self.CT, self.CV1, 132, 512) if isC else (self.DT, self.DV1, 264, 768)
        QT = [ar.alloc([128, NT], BF16, "QT%d" % g) for g in range(2)]
        KT = [ar.alloc([128, NT], BF16, "KT%d" % g) for g in range(2)]
        V1 = ar.alloc([128, NTILE, vw], BF16, "V1")
        Gg = ar.alloc([128, NTILE, 256], BF16, "Gg")
        Ys = ar.alloc([128, NTILE, 256], BF16, "Ys")
        bQK = [Buf(), Buf()]
        bV, bG, bM = Buf(), Buf(), Buf()
        bYg = [Buf() for _ in range(NTILE)]
        for g in range(2):
            s.dma('sp', QT[g], TT[g], reads=[db['CT' if isC else 'DT']], adds=[bQK[g]])
            s.dma('sp', KT[g], TT[2 + g], reads=[db['CT' if isC else 'DT']], adds=[bQK[g]])
        self.dma_tm('sp', V1, V1d, 0, NTILE, True, [db['CV1' if isC else 'DV1']], [bV])
        self.dma_tm('sp', Gg, self.Gs[:, ycol:ycol + 256], 0, NTILE, True, [db['Gs']], [bG])
        if isC:
            msk = ar.alloc([128, 2, 128], BF16, "cmsk")
            es = ar.alloc([128, 4], F32, "ces")
            s.dma('sp', msk, self.cmask_d, adds=[bM])
            s.dma('sp', es, self.L[l]['sinkb'], adds=[bM])
            s.op('act', lambda e: e.activation(out=es, in_=es, func=AF.Exp), reads=[bM], writes=[bM])
            mtile = lambda h, typ: msk[:, typ, :]
        else:
            NB = ar.alloc([128, 4 * ntp, 128], BF16, "dNB")
            s.dma('sp', NB, self.L[l]['nab'], writes=[bM])
            for h4 in range(4):
                s.op('act', (lambda h4=h4: lambda e: e.activation(
                    out=NB[:, h4 * ntp:(h4 + 1) * ntp, :], in_=NB[:, h4 * ntp:(h4 + 1) * ntp, :], func=AF.Exp))(),
                    reads=[bM], writes=[bM])
            mtile = lambda h, typ: NB[:, h * ntp + typ, :]
        PT = Rot([(ar.alloc([128, 512], BF16, "PT"), Buf()) for _ in range(8)])
        ST = Rot([self.ps[i] for i in range(6)])
        OP = Rot([self.ps[6], self.ps[7]])
        dn = Rot([(ar.alloc([128, 8], F32, "den"), Buf()) for _ in range(4)])

        def keys_for(qt):
            kts = [(0, None), (1, None)]
            if qt >= 2:
                if isC:
                    if qt - 1 >= 2:
                        kts.append((qt - 1, 0))
                    kts.append((qt, None))
                    if qt + 1 < NTILE:
                        kts.append((qt + 1, 1))
                else:
                    n = qt - 2
                    for m in G['nbrs'][n]:
                        kts.append((m + 2, G['table'][(n, m)]))
            return kts

        def body(h, qts):
            g, j = h // 2, h % 2
            jsl = slice(j * 64, (j + 1) * 64)
            hv = g if isC else h
            nq = len(qts)
            W = 128 * nq
            per_q = [keys_for(q) for q in qts]
            union = sorted({kt for kl in per_q for kt, _ in kl})
            upos = {kt: u for u, kt in enumerate(union)}
            per_bank = 512 // W
            nbank = (len(union) + per_bank - 1) // per_bank
            banks = [ST.next() for _ in range(nbank)]
            pts = [PT.next() for _ in range(nbank)]
            q0 = qts[0] * 128
            for u, kt in enumerate(union):
                bk, bbk = banks[u // per_bank]
                c0 = (u % per_bank) * W
                s.op('pe', (lambda bk=bk, c0=c0, kt=kt: lambda e: e.matmul(
                    bk[:, c0:c0 + W], lhsT=KT[g][jsl, kt * 128:(kt + 1) * 128], rhs=QT[g][jsl, q0:q0 + W],
                    start=True, stop=True))(), reads=[bQK[g]], writes=[bbk])
            for bi in range(nbank):
                ncol = min(per_bank, len(union) - bi * per_bank) * W
                bk, bbk = banks[bi]
                pt, bpt = pts[bi]
                s.op('act', (lambda bk=bk, pt=pt, ncol=ncol: lambda e: e.activation(
                    out=pt[:, 0:ncol], in_=bk[:, 0:ncol], func=AF.Exp))(), reads=[bbk], writes=[bpt])
            nm = 0
            for qi, kl in enumerate(per_q):
                for kt, typ in kl:
                    if typ is None:
                        continue
                    u = upos[kt]
                    pt, bpt = pts[u // per_bank]
                    c0 = (u % per_bank) * W + qi * 128
                    eng = 'dve' if (isC or nm % 5 != 4) else 'pool'
                    nm += 1
                    s.op(eng, (lambda pt=pt, c0=c0, typ=typ: lambda e: e.tensor_tensor(
                        out=pt[:, c0:c0 + 128], in0=pt[:, c0:c0 + 128], in1=mtile(h, typ), op=ALU.mult))(),
                        reads=[bM, bpt], writes=[bpt])
            yield
            o, bo = OP.next()
            for qi, kl in enumerate(per_q):
                for ki, (kt, typ) in enumerate(kl):
                    u = upos[kt]
                    pt, bpt = pts[u // per_bank]
                    c0 = (u % per_bank) * W + qi * 128
                    s.op('pe', (lambda pt=pt, c0=c0, kt=kt, ki=ki, qi=qi, nk=len(kl): lambda e: e.matmul(
                        o[:, qi * 66:qi * 66 + 65], lhsT=pt[:, c0:c0 + 128], rhs=V1[:, kt, hv * 66:hv * 66 + 65],
                        start=(ki == 0), stop=(ki == nk - 1)))(), reads=[bpt, bV], writes=[bo])
            d, bd = dn.next()
            ov = o[:, 0:66 * nq].rearrange("p (q c) -> p q c", c=66)
            if isC:
                s.op('dve', lambda e: e.tensor_tensor(out=d[:, 0:nq], in0=ov[:, :, 64],
                                                      in1=bc(es[:, h:h + 1], [128, nq]), op=ALU.add),
                     reads=[bo, bM], writes=[bd])
                s.op('dve', lambda e: e.reciprocal(out=d[:, 4:4 + nq], in_=d[:, 0:nq]), reads=[bd], writes=[bd])
            else:
                s.op('dve', lambda e: e.reciprocal(out=d[:, 4:4 + nq], in_=ov[:, :, 64]), reads=[bo], writes=[bd])
            for qi, qt in enumerate(qts):
                s.op('dve', (lambda qi=qi, qt=qt: lambda e: e.scalar_tensor_tensor(
                    out=Ys[:, qt, h * 64:(h + 1) * 64], in0=o[:, qi * 66:qi * 66 + 64], scalar=d[:, 4 + qi:5 + qi],
                    in1=Gg[:, qt, h * 64:(h + 1) * 64], op0=ALU.mult, op1=ALU.mult))(),
                    reads=[bo, bd, bG], adds=[bYg[qt // 4]])

        pairs = ([[0, 1]] if need_ctx else []) + [[q, q + 1] for q in range(2, NTILE, 2)]
        gens = []
        for pr in pairs:
            for h in range(4):
                gens.append(body(h, pr))
            qt = pr[-1]
            if qt % 4 == 3 or qt == NTILE - 1:
                g0 = max(pairs[0][0], (qt // 4) * 4)
                gens.append(store_item((lambda g0=g0, qt=qt: lambda: self.dma_tm(
                    'sp', Ys, self.Ymix[:, ycol:ycol + 256], g0, qt + 1, False, [bYg[qt // 4]], [db['Ymix']]))())(2))
        pipeline2(gens)

    def phase_a(self, l):
        nc, s = self.nc, self.s
        ar = self.arena
        ar.reset()
        need_ctx = (l < DEPTH - 1)
        db = self.dram_bufs
        A = ar.alloc([128, NTILE, 512], BF16, "aQKV")
        LRf = ar.alloc([96, NT], F32, "aLRf")
        LRx = ar.alloc([96, NT], BF16, "aLRx")
        Ga = ar.alloc([128, NTILE, 256], BF16, "aG")
        Ost = ar.alloc([128, NTILE, 256], F32, "aOst")
        Ys = ar.alloc([128, NTILE, 256], BF16, "aY")
        tri = ar.alloc([128, 4, 128], F32, "tri")
        blk = ar.alloc([128, 260], F32, "blk")
        Wf = ar.alloc([96, 256], F32, "wdecf")
        Wx = ar.alloc([96, 256], BF16, "wdecx")
        bdec = ar.alloc([1, 256], F32, "bdec")
        bhl = ar.alloc([1, 2, 256], BF16, "bhl")
        ones1 = ar.alloc([1, 128], BF16, "ones1")
        trib = ar.alloc([128, 4, 128], BF16, "trib")
        onb = ar.alloc([128, 64], F32, "onb")
        bA, bLR, bG, bK = Buf(), Buf(), Buf(), Buf()
        bYc = [Buf() for _ in range(NTILE)]
        bOst = [Buf() for _ in range(NTILE)]
        self.dma_tm('sp', A, self.Aqkv, 0, NTILE, True, [db['Aqkv']], [bA])
        for r3 in range(3):
            s.dma('sp', LRf[r3 * 32:(r3 + 1) * 32, :], self.LrT, reads=[db['LrT']], adds=[bLR])
            s.dma('sp', Wf[r3 * 32:(r3 + 1) * 32, :], self.L[l]['wdec'][0:32, :], adds=[bK])
        self.dma_tm('sp', Ga, self.Gs[:, 0:256], 0, NTILE, True, [db['Gs']], [bG])
        s.dma('sp', tri, self.tri_d, adds=[bK])
        s.dma('sp', blk, self.blk_d, adds=[bK])
        s.dma('sp', bdec, self.L[l]['wdec'][32:33, :], adds=[bK])
        s.dma('sp', onb, self.L[l]['onb'], adds=[bK])
        bLX, bWX = Buf(), Buf()
        s.op('pool', lambda e: e.memset(ones1, 1.0), adds=[bWX])
        s.op('act', lambda e: e.activation(out=LRx[0:32, :], in_=LRf[0:32, :], func=AF.Copy), reads=[bLR], adds=[bLX])
        s.op('act', lambda e: e.activation(out=LRx[64:96, :], in_=LRf[64:96, :], func=AF.Copy), reads=[bLR], adds=[bLX])
        s.op('dve', lambda e: e.tensor_copy(out=LRx[32:64, :], in_=LRf[32:64, :]), reads=[bLR], adds=[bLX])
        s.op('dve', lambda e: e.tensor_tensor(out=LRx[32:64, :], in0=LRf[32:64, :], in1=LRx[32:64, :], op=ALU.subtract),
             reads=[bLR, bLX], adds=[bLX])
        s.op('pool', lambda e: e.tensor_copy(out=Wx[0:64, :], in_=Wf[0:64, :]), reads=[bK], adds=[bWX])
        s.op('pool', lambda e: e.tensor_copy(out=Wx[64:96, :], in_=Wf[64:96, :]), reads=[bK], adds=[bWX])
        s.op('pool', lambda e: e.tensor_tensor(out=Wx[64:96, :], in0=Wf[64:96, :], in1=Wx[64:96, :], op=ALU.subtract),
             reads=[bK, bWX], adds=[bWX])
        s.op('pool', lambda e: e.tensor_copy(out=bhl[:, 0, :], in_=bdec), reads=[bK], adds=[bWX])
        s.op('pool', lambda e: e.tensor_tensor(out=bhl[:, 1, :], in0=bdec, in1=bhl[:, 0, :], op=ALU.subtract),
             reads=[bK, bWX], adds=[bWX])
        s.op('pool', lambda e: e.tensor_copy(out=trib, in_=tri), reads=[bK], adds=[bWX])
        blkm = blk[:, 0:256]
        hm = blk[:, 256:260]
        Sf = [ar.alloc([128, 256], F32, "Sf%d" % d) for d in range(2)]
        Sb = [ar.alloc([128, 256], BF16, "Sb%d" % d) for d in range(2)]
        bS = [Buf(), Buf()]
        for d in range(2):
            s.op('pool', (lambda d=d: lambda e: e.memset(Sf[d], 0.0))(), writes=[bS[d]])
            s.op('pool', (lambda d=d: lambda e: e.memset(Sb[d], 0.0))(), adds=[bS[d]])

        def rot(shape, dt, name, n=4):
            return [Rot([(ar.alloc(shape, dt, name), Buf()) for _ in range(n)]) for _ in range(2)]
        gS = rot([128, 128], F32, "gS")
        gH = rot([128, 2, 128], BF16, "gH")
        eT = rot([128, 128], F32, "eT")
        Eq = rot([128, 128], F32, "Eq")
        Ek = rot([128, 128], F32, "Ek")
        Eh = rot([128, 128], F32, "Eh")
        qt_ = rot([128, 128], BF16, "qt")
        kt_ = rot([128, 128], BF16, "kt")
        kh_ = rot([128, 128], BF16, "kh")
        Q4 = rot([128, 512], BF16, "Q4")
        Pm = rot([128, 512], BF16, "Pm")
        fin = Rot([(ar.alloc([128, 3, 256], F32, "fin"), Buf()) for _ in range(3)])
        fst = Rot([(ar.alloc([128, 64], F32, "fst"), Buf()) for _ in range(3)])
        ZG = [self.ps[0], self.ps[1]]
        TRs = Rot([self.ps[2], self.ps[7]])
        AT = [self.ps[3], self.ps[4]]
        OK = [self.ps[5], self.ps[6]]

        order = [list(range(NTILE)), [1, 0] + list(range(NTILE - 1, 1, -1))]
        pos = [{c: i for i, c in enumerate(order[d])} for d in range(2)]

        def step(c, d):
            zg, bzg = ZG[d]
            at, bat = AT[d]
            ok, bok = OK[d]
            first = pos[d][c] < pos[1 - d][c] or (pos[d][c] == pos[1 - d][c] and d == 0)
            cs_ = slice(c * 128, (c + 1) * 128)
            mi_incl, mi_tail = (0, 2) if d == 0 else (1, 3)
            s.op('pe', lambda e: e.matmul(zg[:, 0:128], lhsT=LRx[:, cs_], rhs=Wx[:, d * 128:(d + 1) * 128],
                                          start=True, stop=False), reads=[bLX, bWX], writes=[bzg])
            for hl in range(2):
                s.op('pe', (lambda hl=hl: lambda e: e.matmul(
                    zg[:, 0:128], lhsT=ones1[0:1, :], rhs=bhl[0:1, hl, d * 128:(d + 1) * 128],
                    start=False, stop=(hl == 1)))(), reads=[bWX], writes=[bzg])
            e_t, bet = eT[d].next()
            g_s, bgs = gS[d].next()
            s.op('act', lambda e: e.activation(out=e_t, in_=zg[:, 0:128], func=AF.Exp, scale=-1.0),
                 reads=[bzg], writes=[bet])
            s.op('act', lambda e: e.activation(out=g_s, in_=e_t, func=AF.Ln, bias=1.0), reads=[bet], writes=[bgs])
            g_h, bgh = gH[d].next()
            s.op('act', lambda e: e.activation(out=g_h[:, 0, :], in_=g_s, func=AF.Copy), reads=[bgs], writes=[bgh])
            s.op('pool', lambda e: e.tensor_tensor(out=g_h[:, 1, :], in0=g_s, in1=g_h[:, 0, :], op=ALU.subtract),
                 reads=[bgs, bgh], writes=[bgh])
            yield
            for hl in range(2):
                s.op('pe', (lambda hl=hl: lambda e: e.matmul(
                    zg[:, 128:256], lhsT=g_h[:, hl, :], rhs=trib[:, mi_incl, :], start=(hl == 0), stop=(hl == 1)))(),
                    reads=[bgh, bWX], writes=[bzg])
            for hl in range(2):
                s.op('pe', (lambda hl=hl: lambda e: e.matmul(
                    zg[:, 256:384], lhsT=trib[:, mi_tail, :], rhs=g_h[:, hl, :], start=(hl == 0), stop=(hl == 1)))(),
                    reads=[bgh, bWX], writes=[bzg])
            eq, beq = Eq[d].next()
            ek, bek = Ek[d].next()
            eh, beh = Eh[d].next()
            s.op('act', lambda e: e.activation(out=eq, in_=zg[:, 128:256], func=AF.Exp, scale=-1.0 / 16),
                 reads=[bzg], writes=[beq])
            s.op('act', lambda e: e.activation(out=ek, in_=zg[:, 128:256], func=AF.Exp, scale=1.0 / 16),
                 reads=[bzg], writes=[bek])
            s.op('act', lambda e: e.activation(out=eh, in_=zg[:, 256:384], func=AF.Exp, scale=-1.0 / 16),
                 reads=[bzg], writes=[beh])
            yield
            tr, btr = TRs.next()
            trb = tr.bitcast(BF16)
            s.op('pe', lambda e: e.transpose(out=trb[:, 0:128], in_=A[:, c, 0:128], identity=self.ident_b),
                 reads=[bA, self.b_const], writes=[btr])
            s.op('pe', lambda e: e.transpose(out=trb[:, 128:256], in_=A[:, c, 128:256], identity=self.ident_b),
                 reads=[bA, self.b_const], writes=[btr])
            q_t, bqt = qt_[d].next()
            k_t, bkt = kt_[d].next()
            k_h, bkh = kh_[d].next()
            s.op('pool', lambda e: e.tensor_tensor(out=k_h, in0=A[:, c, 128:256], in1=eh, op=ALU.mult),
                 reads=[bA, beh], writes=[bkh])
            s.op('dve', lambda e: e.scalar_tensor_tensor(out=q_t, in0=trb[:, 0:128], scalar=32.0 ** -0.5, in1=eq,
                                                         op0=ALU.mult, op1=ALU.mult), reads=[btr, beq], writes=[bqt])
            s.op('dve', lambda e: e.tensor_tensor(out=k_t, in0=trb[:, 128:256], in1=ek, op=ALU.mult),
                 reads=[btr, bek], writes=[bkt])
            yield
            q4, bq4 = Q4[d].next()
            s.op('pool', lambda e: e.tensor_tensor(
                out=q4.rearrange("p (h t) -> p h t", h=4), in0=bc(q_t.unsqueeze(1), [128, 4, 128]),
                in1=bc(hm.unsqueeze(2), [128, 4, 128]), op=ALU.mult), reads=[bqt, bK], writes=[bq4])
            yield
            s.op('pe', lambda e: e.matmul(at[:, 0:512], lhsT=k_t, rhs=q4, start=True, stop=True),
                 reads=[bkt, bq4], writes=[bat])
            p_m, bpm = Pm[d].next()
            s.op('dve', lambda e: e.tensor_tensor(
                out=p_m.rearrange("p (h t) -> p h t", h=4), in0=at[:, 0:512].rearrange("p (h t) -> p h t", h=4),
                in1=bc(tri[:, mi_incl, :].unsqueeze(1), [128, 4, 128]), op=ALU.mult), reads=[bat, bK], writes=[bpm])
            yield
            s.op('pe', lambda e: e.matmul(ok[:, 0:256], lhsT=q_t, rhs=Sb[d], start=True, stop=False),
                 reads=[bqt, bS[d]], writes=[bok])
            for h in range(4):
                s.op('pe', (lambda h=h: lambda e: e.matmul(
                    ok[:, h * 64:(h + 1) * 64], lhsT=p_m[:, h * 128:(h + 1) * 128],
                    rhs=A[:, c, 256 + h * 64:256 + (h + 1) * 64], start=False, stop=(h == 3)))(),
                    reads=[bpm, bA], writes=[bok])
            s.op('pe', lambda e: e.matmul(ok[:, 256:512], lhsT=k_h, rhs=A[:, c, 256:512], start=True, stop=True),
                 reads=[bkh, bA], writes=[bok])
            last = 127 if d == 0 else 0
            s.op('dve', lambda e: e.scalar_tensor_tensor(
                out=Sf[d], in0=Sf[d], scalar=eq[:, last:last + 1], in1=ok[:, 256:512], op0=ALU.mult, op1=ALU.add),
                reads=[bok, beq, bS[d]], writes=[bS[d]])
            s.op('dve', lambda e: e.tensor_tensor(out=Sb[d], in0=Sf[d], in1=blkm, op=ALU.mult),
                 reads=[bK, bS[d]], writes=[bS[d]])
            if first:
                s.op('act', lambda e: e.activation(out=Ost[:, c, :], in_=ok[:, 0:256], func=AF.Copy),
                     reads=[bok], writes=[bOst[c]])
                yield
                return
            f, bf = fin.next()
            st_, bst = fst.next()
            s.op('dve', lambda e: e.tensor_tensor(out=f[:, 0, :], in0=ok[:, 0:256], in1=Ost[:, c, :], op=ALU.add),
                 reads=[bok, bOst[c]], writes=[bf])
            yield
            if need_ctx or c >= 2:
                s.op('act', lambda e: e.activation(out=f[:, 1, :], in_=f[:, 0, :], func=AF.Square),
                     reads=[bf], writes=[bf])
                s.op('dve', lambda e: e.tensor_reduce(
                    out=st_[:, 0:4], in_=f[:, 1, :].rearrange("p (h d) -> p h d", d=64), axis=AX.X, op=ALU.add),
                    reads=[bf], writes=[bst])
                s.op('dve', lambda e: e.tensor_scalar(out=st_[:, 0:4], in0=st_[:, 0:4], scalar1=1.0 / 64, scalar2=EPS,
                                                      op0=ALU.mult, op1=ALU.add), reads=[bst], writes=[bst])
                s.op('act', lambda e: e.activation(out=st_[:, 32:36], in_=st_[:, 0:4], func=AF.Ln),
                     reads=[bst], writes=[bst])
                s.op('act', lambda e: e.activation(out=st_[:, 16:20], in_=st_[:, 32:36], func=AF.Exp, scale=-0.5),
                     reads=[bst], writes=[bst])
                s.op('pool', lambda e: e.tensor_tensor(
                    out=f[:, 2, :].rearrange("p (h d) -> p h d", d=64),
                    in0=Ga[:, c, :].rearrange("p (h d) -> p h d", d=64),
                    in1=bc(onb.unsqueeze(1), [128, 4, 64]), op=ALU.mult), reads=[bG, bK], writes=[bf])
                s.op('pool', lambda e: e.tensor_tensor(
                    out=f[:, 1, :].rearrange("p (h d) -> p h d", d=64),
                    in0=f[:, 0, :].rearrange("p (h d) -> p h d", d=64),
                    in1=bc(st_[:, 16:20].unsqueeze(2), [128, 4, 64]), op=ALU.mult), reads=[bf, bst], writes=[bf])
                s.op('pool', lambda e: e.tensor_tensor(out=Ys[:, c, :], in0=f[:, 1, :], in1=f[:, 2, :], op=ALU.mult),
                     reads=[bf], writes=[bYc[c]])

        def ystore(c0, c1):
            return store_item(lambda: self.dma_tm('sp', Ys, self.Ymix[:, 0:256], c0, c1, False,
                                                  [bYc[c] for c in range(c0, c1)], [db['Ymix']]))(7)
        gens = []
        for i in range(NTILE):
            gens.append(step(order[0][i], 0))
            gens.append(step(order[1][i], 1))
            if i == 1 and need_ctx:
                gens.append(ystore(0, 2))
            if i >= 19 and i % 2 == 1:
                gens.append(ystore(i - 1, i + 1))
                gens.append(ystore(35 - i, 37 - i))
        pipelineN(gens, 7)

    def phase_b(self, l):
        nc, s = self.nc, self.s
        ar = self.arena
        ar.reset()
        need_ctx = (l < DEPTH - 1)
        db = self.dram_bufs
        X0 = ar.alloc([64, 64, 256], BF16, "fX0")
        x3_off = ar.off
        X0p = ar.alloc([64, 128, 128], BF16, "fX0p")
        X1 = ar.alloc([128, 128, 128], BF16, "fX1")
        M3 = ar.alloc([128, 64, 2, 128], BF16, "fM3")
        D1 = ar.alloc([64, 128], BF16, "fD1")
        CH = ar.alloc([128, 8, 128], BF16, "fCH")
        RT = ar.alloc([128, 2, NT], BF16, "fRT")
        Gb = ar.alloc([128, NTILE, 256], BF16, "fG")
        Ys = ar.alloc([128, NTILE, 256], BF16, "fY")
        fwf = ar.alloc([128, 2, 256], F32, "fwf")
        fwb = ar.alloc([128, 2, 256], BF16, "fwb")
        bX0, bX1, bX3, bK, bRT, bG, bY, bFW, bX0p = Buf(), Buf(), Buf(), Buf(), Buf(), Buf(), Buf(), Buf(), Buf()
        bv_l = self.Bv[NC_:NT, :].rearrange("(r c) k -> r c k", c=64)
        for q in range(4):
            s.dma('sp', X0[:, q * 16:(q + 1) * 16, :], bv_l[:, q * 16:(q + 1) * 16, :], reads=[db['Bv']], adds=[bX0])
        s.dma('sp', D1, self.fD1_d, adds=[bK])
        for q in range(4):
            s.dma('sp', M3[:, q * 16:(q + 1) * 16], self.fM3_d[:, q * 16:(q + 1) * 16], adds=[bK])
        s.dma('sp', CH, self.fCH_d, adds=[bK])
        s.dma('sp', fwf, self.L[l]['fw'].rearrange("(a p) n -> p a n", p=128), adds=[bFW])
        s.op('pool', lambda e: e.tensor_copy(out=fwb, in_=fwf), reads=[bFW], writes=[bFW])
        self.dma_tm('sp', Gb, self.Gs[:, 256:512], 0, NTILE, True, [db['Gs']], [bG])
        PS = Rot([self.ps[i] for i in range(8)])
        bYg = [Buf() for _ in range(NTILE)]
        evr = [0]

        def evac(out_ap, in_ap, rb, wb, add=False, extra=()):
            eng = 'act' if evr[0] % 2 == 0 else 'dve'
            evr[0] += 1
            kw = dict(adds=[wb] + list(extra)) if add else dict(writes=[wb])
            if eng == 'act':
                s.op('act', lambda e: e.activation(out=out_ap, in_=in_ap, func=AF.Copy), reads=[rb], **kw)
            else:
                s.op('dve', lambda e: e.tensor_copy(out=out_ap, in_=in_ap), reads=[rb], **kw)

        X0v = X0.rearrange("r c (q t) -> r q t c", t=2)
        X0pv = X0p.rearrange("r q (t c) -> r q t c", t=2)
        for qi, eng in enumerate(('dve', 'act', 'dve', 'act')):
            sl = slice(qi * 32, (qi + 1) * 32)
            if eng == 'act':
                s.op('act', (lambda sl=sl: lambda e: e.activation(out=X0pv[:, sl], in_=X0v[:, sl], func=AF.Copy))(),
                     reads=[bX0], adds=[bX0p])
            else:
                s.op(eng, (lambda sl=sl: lambda e: e.tensor_copy(out=X0pv[:, sl], in_=X0v[:, sl]))(),
                     reads=[bX0], adds=[bX0p])
        for q4 in range(32):
            bk, bbk = PS.next()
            for i in range(4):
                chp = q4 * 4 + i
                s.op('pe', (lambda chp=chp, i=i, bk=bk: lambda e: e.matmul(
                    bk[:, i * 128:(i + 1) * 128], lhsT=X0p[:, chp, :], rhs=D1, start=True, stop=True))(),
                    reads=[bX0p, bK], writes=[bbk])
            evac(X1[:, q4 * 4:(q4 + 1) * 4, :], bk[:, 0:512].rearrange("p (a n) -> p a n", a=4), bbk, bX1, add=True)
        X3 = ar.view([128, 64, 2, 128], BF16, x3_off - 64 * 256 * 2)
        assert x3_off - 64 * 256 * 2 == self.arena.base
        X1v = X1.rearrange("p q (k z) -> p k z q", z=2)
        for k4 in range(16):
            bkA, bbA = PS.next()
            bkB, bbB = PS.next()
            for i in range(4):
                k1 = k4 * 4 + i
                for z in range(2):
                    for ch2, (bk, bbk) in enumerate(((bkA, bbA), (bkB, bbB))):
                        ps_ = slice(ch2 * 64, (ch2 + 1) * 64)
                        s.op('pe', (lambda k1=k1, z=z, ps_=ps_, bk=bk, i=i: lambda e: e.matmul(
                            bk[:, i * 128:(i + 1) * 128], lhsT=X1v[ps_, k1, z, :], rhs=M3[ps_, k1, z, :],
                            start=(z == 0), stop=(z == 1)))(), reads=[bX1, bK], writes=[bbk])
            for ch2, (bk, bbk) in enumerate(((bkA, bbA), (bkB, bbB))):
                evac(X3[:, k4 * 4:(k4 + 1) * 4, ch2, :], bk[:, 0:512].rearrange("p (a n) -> p a n", a=4),
                     bbk, bX3, add=True, extra=[bX0])
        X3v = X3.rearrange("p k t (j z) -> p t z k j", z=2)
        RTl = RT[:, :, NC_:NT].rearrange("p a (j k) -> p a k j", k=64)
        for mc in range(2):
            for kb in range(8):
                bk, bbk = PS.next()
                n = 0
                for ch2 in range(2):
                    for z in range(2):
                        s.op('pe', (lambda mc=mc, kb=kb, ch2=ch2, z=z, n=n, bk=bk: lambda e: e.matmul(
                            bk[:, 0:512], lhsT=CH[:, mc * 4 + ch2 * 2 + z, :],
                            rhs=X3v[:, ch2, z, kb * 8:(kb + 1) * 8, :], start=(n == 0), stop=(n == 3)))(),
                            reads=[bX3, bK], writes=[bbk])
                        n += 1
                evac(RTl[:, mc, kb * 8:(kb + 1) * 8, :], bk[:, 0:512].rearrange("p (k j) -> p k j", k=8),
                     bbk, bRT, add=True)
        if need_ctx:
            Vc = ar.alloc([128, 2, 256], BF16, "fVc")
            DC = ar.alloc([128, 2, 512], BF16, "fDC")
            CHc = ar.alloc([128, 2, 128], BF16, "fCHc")
            X3c = ar.alloc([128, 2, 512], BF16, "fX3c")
            bC, bX3c = Buf(), Buf()
            s.dma('sp', Vc, self.Bv[0:NC_, :].rearrange("(i p) k -> p i k", p=128), reads=[db['Bv']], adds=[bC])
            s.dma('sp', DC, self.fDC_d, adds=[bC])
            s.dma('sp', CHc, self.fCHc_d, adds=[bC])
            for cc in range(2):
                bk, bbk = PS.next()
                for i in range(2):
                    s.op('pe', (lambda cc=cc, i=i, bk=bk: lambda e: e.matmul(
                        bk[:, 0:512], lhsT=Vc[:, i, cc * 128:(cc + 1) * 128], rhs=DC[:, i, :],
                        start=(i == 0), stop=(i == 1)))(), reads=[bC], writes=[bbk])
                evac(X3c[:, cc, :], bk[:, 0:512], bbk, bX3c, add=True)
            X3cv = X3c.rearrange("p a (k z) -> p a z k", z=2)
            for cc in range(2):
                bk, bbk = PS.next()
                for z in range(2):
                    s.op('pe', (lambda cc=cc, z=z, bk=bk: lambda e: e.matmul(
                        bk[:, 0:256], lhsT=CHc[:, z, :], rhs=X3cv[:, cc, z, :], start=(z == 0), stop=(z == 1)))(),
                        reads=[bX3c, bC], writes=[bbk])
                evac(RT[:, cc, 0:NC_], bk[:, 0:256], bbk, bRT, add=True)
        t0 = 0 if need_ctx else 2
        for ti in range(t0, NTILE):
            bk, bbk = PS.next()
            for mc in range(2):
                s.op('pe', (lambda ti=ti, mc=mc, bk=bk: lambda e: e.matmul(
                    bk[:, 0:256], lhsT=RT[:, mc, ti * 128:(ti + 1) * 128], rhs=fwb[:, mc, :],
                    start=(mc == 0), stop=(mc == 1)))(), reads=[bRT, bFW], writes=[bbk])
            s.op('dve', (lambda ti=ti, bk=bk: lambda e: e.tensor_tensor(
                out=Ys[:, ti, :], in0=bk[:, 0:256], in1=Gb[:, ti, :], op=ALU.mult))(), reads=[bbk, bG],
                adds=[bYg[ti // 4]])
            if ti % 4 == 3 or ti == NTILE - 1:
                self.dma_tm('sp', Ys, self.Ymix[:, 256:512], max(t0, (ti // 4) * 4), ti + 1, False,
                            [bYg[ti // 4]], [db['Ymix']])

    def phase_o(self, l):
        nc, s = self.nc, self.s
        ar = self.arena
        ar.reset()
        last = (l == DEPTH - 1)
        db = self.dram_bufs
        modT, bmod = self.modT[l], self.b_mod[l]
        nmod = 1 if last else 2
        gcol = ar.alloc([128, 8, 2, 128], F32, "gcol")
        GB = ar.alloc([128, 2, D], F32, "GB")
        Wo = [ar.alloc([128, 8, D], BF16, "Wo%d" % i) for i in range(nmod)]
        bgc, bGB, bWo = Buf(), Buf(), Buf()
        for i in range(nmod):
            s.op('dve', (lambda i=i: lambda e: e.tensor_copy(
                out=gcol[:, :, i, :], in_=bc(modT[:, 16:24, i:i + 1], [128, 8, 128])))(), reads=[bmod], adds=[bgc])
        for i in range(nmod):
            for hf in range(2):
                bk, bbk = self.ps[i * 2 + hf]
                for kk in range(4):
                    k = hf * 4 + kk
                    s.op('pe', (lambda i=i, k=k, kk=kk, bk=bk: lambda e: e.matmul(
                        bk[:, kk * 128:(kk + 1) * 128], lhsT=gcol[:, k, i, :], rhs=self.ident_f,
                        start=True, stop=True))(), reads=[bgc, self.b_const], writes=[bbk])
                s.op('act', (lambda i=i, hf=hf, bk=bk: lambda e: e.activation(
                    out=GB[:, i, hf * 512:(hf + 1) * 512], in_=bk[:, 0:512], func=AF.Copy))(),
                    reads=[bbk], adds=[bGB])
        wst = Rot([(ar.alloc([128, D], F32, "wst"), Buf()) for _ in range(2)])
        for mk in range(8):
            w, bw = wst.next()
            s.dma('sp', w, self.L[l]['wo'][mk * 128:(mk + 1) * 128, :], writes=[bw])
            for i in range(nmod):
                eng = 'dve' if i == 0 else 'pool'
                s.op(eng, (lambda i=i, mk=mk, w=w: lambda e: e.tensor_tensor(
                    out=Wo[i][:, mk, :], in0=w, in1=GB[:, i, :], op=ALU.mult))(), reads=[bw, bGB], adds=[bWo])
        yt = Rot([(ar.alloc([128, 2, D], BF16, "yt"), Buf()) for _ in range(3)])
        xt = Rot([(ar.alloc([128, 2, D], F32, "xt"), Buf()) for _ in range(3)])
        yT = Rot([(ar.alloc([128, 8, 128], BF16, "yT"), Buf()) for _ in range(4)])
        xo = Rot([(ar.alloc([128, 2, D], F32, "xo"), Buf()) for _ in range(2)])
        TPs = Rot([self.ps[4], self.ps[5]])
        MMs = Rot([self.ps[0], self.ps[1], self.ps[2], self.ps[3], self.ps[6], self.ps[7]])
        src = self.xin if l == 0 else self.X1
        bsrc = Buf() if l == 0 else db['X1']
        t0 = 2 if last else 0

        def tile2(ti):
            u0 = ti * 128
            wi = 0 if ti >= 2 else 1
            y_t, byt = yt.next()
            x_t, bxt = xt.next()
            tm = lambda dr: dr[u0:u0 + 256, :].rearrange("(i p) c -> p i c", p=128)
            s.dma('sp', y_t, tm(self.Ymix), reads=[db['Ymix']], writes=[byt])
            s.dma('sp', x_t, tm(src), reads=[bsrc], writes=[bxt])
            yTs = []
            for i in range(2):
                tp, btp = TPs.next()
                tpb = tp.bitcast(BF16)
                for k in range(8):
                    s.op('pe', (lambda k=k, i=i, tpb=tpb: lambda e: e.transpose(
                        out=tpb[:, k * 128:(k + 1) * 128], in_=y_t[:, i, k * 128:(k + 1) * 128],
                        identity=self.ident_b))(), reads=[byt, self.b_const], writes=[btp])
                y_T, byT = yT.next()
                s.op('act', (lambda y_T=y_T, tpb=tpb: lambda e: e.activation(
                    out=y_T.rearrange("p k t -> p (k t)"), in_=tpb[:, 0:1024], func=AF.Copy))(),
                    reads=[btp], writes=[byT])
                yTs.append((y_T, byT))
            yield
            x_o, bxo = xo.next()
            for i in range(2):
                y_T, byT = yTs[i]
                for nc_ in range(2):
                    bk, bbk = MMs.next()
                    for k in range(8):
                        s.op('pe', (lambda k=k, nc_=nc_, bk=bk, y_T=y_T: lambda e: e.matmul(
                            bk[:, 0:512], lhsT=y_T[:, k, :], rhs=Wo[wi][:, k, nc_ * 512:(nc_ + 1) * 512],
                            start=(k == 0), stop=(k == 7)))(), reads=[byT, bWo], writes=[bbk])
                    s.op('dve', (lambda nc_=nc_, bk=bk, i=i: lambda e: e.tensor_tensor(
                        out=x_o[:, i, nc_ * 512:(nc_ + 1) * 512], in0=bk[:, 0:512],
                        in1=x_t[:, i, nc_ * 512:(nc_ + 1) * 512], op=ALU.add))(), reads=[bbk, bxt], adds=[bxo])
            if last:
                s.dma('pool', self.out[u0 - NC_:u0 - NC_ + 256, :].rearrange("(i p) c -> p i c", p=128), x_o,
                      reads=[bxo], adds=[self.b_out])
            else:
                s.dma('pool', self.X1[u0:u0 + 256, :].rearrange("(i p) c -> p i c", p=128), x_o,
                      reads=[bxo], adds=[db['X1']])

        pipeline2([tile2(ti) for ti in range(t0, NTILE, 2)])

    def rsqrt1(self, v, y, t1, buf):
        s = self.s
        I32 = mybir.dt.int32
        vi, yi, t1i = v.bitcast(I32), y.bitcast(I32), t1.bitcast(I32)
        s.op('dve', lambda e: e.tensor_single_scalar(out=t1i, in_=vi, scalar=1, op=ALU.arith_shift_right),
             reads=[buf], writes=[buf])
        s.op('dve', lambda e: e.tensor_scalar(out=yi, in0=t1i, scalar1=-1.0, scalar2=1597463007.0,
                                              op0=ALU.mult, op1=ALU.add), reads=[buf], writes=[buf])
        for _ in range(2):
            s.op('dve', lambda e: e.scalar_tensor_tensor(out=t1, in0=y, scalar=v, in1=y, op0=ALU.mult, op1=ALU.mult),
                 reads=[buf], writes=[buf])
            s.op('dve', lambda e: e.tensor_scalar(out=t1, in0=t1, scalar1=-0.5, scalar2=1.5,
                                                  op0=ALU.mult, op1=ALU.add), reads=[buf], writes=[buf])
            s.op('dve', lambda e: e.tensor_tensor(out=y, in0=y, in1=t1, op=ALU.mult), reads=[buf], writes=[buf])

    def rsqrt(self, v, y, t1, t2, buf):
        s = self.s
        I32 = mybir.dt.int32
        vi, yi, t1i = v.bitcast(I32), y.bitcast(I32), t1.bitcast(I32)
        s.op('dve', lambda e: e.tensor_single_scalar(out=t1i, in_=vi, scalar=1, op=ALU.arith_shift_right),
             reads=[buf], writes=[buf])
        s.op('dve', lambda e: e.tensor_scalar(out=yi, in0=t1i, scalar1=-1.0, scalar2=1597463007.0,
                                              op0=ALU.mult, op1=ALU.add), reads=[buf], writes=[buf])
        for _ in range(2):
            s.op('dve', lambda e: e.tensor_tensor(out=t1, in0=v, in1=y, op=ALU.mult), reads=[buf], writes=[buf])
            s.op('dve', lambda e: e.tensor_tensor(out=t2, in0=t1, in1=y, op=ALU.mult), reads=[buf], writes=[buf])
            s.op('dve', lambda e: e.tensor_scalar(out=t2, in0=t2, scalar1=-0.5, scalar2=1.5,
                                                  op0=ALU.mult, op1=ALU.add), reads=[buf], writes=[buf])
            s.op('dve', lambda e: e.tensor_tensor(out=y, in0=y, in1=t2, op=ALU.mult), reads=[buf], writes=[buf])

    def _norm_heads(self, pm, bpm, nh, wt, b_w, sq, nt, st8, out_f32, rot_out=None, out=None):
        s = self.s
        w = nh * 64
        sq_t, bsq = sq.next()
        s.op('act', lambda e: e.activation(out=sq_t[:, 0:w], in_=pm[:, 0:w], func=AF.Square),
             reads=[bpm], writes=[bsq])
        s8, bs8 = st8.next()
        s.op('dve', lambda e: e.tensor_reduce(out=s8[:, 0:nh], in_=sq_t[:, 0:w].rearrange("p (h d) -> p h d", d=64),
                                              axis=AX.X, op=ALU.add), reads=[bsq], writes=[bs8])
        s.op('dve', lambda e: e.tensor_scalar(out=s8[:, 0:nh], in0=s8[:, 0:nh], scalar1=1.0 / 64, scalar2=EPS,
                                              op0=ALU.mult, op1=ALU.add), reads=[bs8], writes=[bs8])
        s.op('dve', lambda e: e.tensor_scalar(out=s8[:, 8:8 + nh], in0=s8[:, 0:nh], scalar1=-0.5, scalar2=None,
                                              op0=ALU.pow), reads=[bs8], writes=[bs8])
        n_t, bnt = nt.next()
        s.op('dve', lambda e: e.tensor_tensor(
            out=n_t[:, 0:w].rearrange("p (h d) -> p h d", d=64), in0=pm[:, 0:w].rearrange("p (h d) -> p h d", d=64),
            in1=bc(s8[:, 8:8 + nh].unsqueeze(2), [128, nh, 64]), op=ALU.mult), reads=[bpm, bs8], writes=[bnt])
        if out_f32:
            o, bo = rot_out.next()
        else:
            o, bo = out
        s.op('pool', lambda e: e.tensor_tensor(out=o[:, 0:w], in0=n_t[:, 0:w], in1=wt[:, 0:w], op=ALU.mult),
             reads=[bnt, b_w], writes=[bo])
        self._last_norm = (o, bo)

    @property
    def xin_l(self):
        return self.xin


def _col_perm():
    off = {}
    o = 0
    for name, w in (("a_q", 128), ("a_k", 128), ("a_v", 256), ("a_g", 256), ("a_lr", 32), ("b_v", 256),
                    ("b_g", 256), ("c_q", 256), ("c_k", 128), ("c_v", 128), ("c_g", 256), ("d_q", 256),
                    ("d_k", 256), ("d_v", 256), ("d_g", 256)):
        off[name] = (o, w)
        o += w
    order = ["a_q", "a_k", "a_v", "a_g", "b_g", "c_g", "d_g", "c_q", "c_k", "c_v", "d_q", "d_k", "d_v", "b_v", "a_lr"]
    perm = []
    for nm in order:
        a, w = off[nm]
        perm.extend(range(a, a + w))
    return np.array(perm)


def _rope_table():
    t = np.arange(NL)
    row = (t // 64).astype(np.float32)
    col = (t % 64).astype(np.float32)
    inv = (10000.0 ** (-np.arange(0, 32, 2, dtype=np.float32) / 32)).astype(np.float32)
    ang = np.stack([row[:, None] * inv, col[:, None] * inv], axis=1)
    cs = np.concatenate([np.cos(ang).reshape(NL, 32), np.sin(ang).reshape(NL, 32)], axis=1)
    return cs.astype(np.float32)


def na_geometry():
    r = np.arange(64)
    row_start = np.clip(r - 4, 0, 56)
    col_start = np.clip(r - 8, 0, 48)
    kk = np.arange(128)
    ka, kc = (kk // 64)[:, None], (kk % 64)[:, None]
    qa, qc = (kk // 64)[None, :], (kk % 64)[None, :]
    vcol = (kc >= col_start[qc]) & (kc < col_start[qc] + 16)
    dx = np.clip(kc - qc, -15, 15) + 15
    sigs, table, nbrs, mats = {}, {}, [], []
    for n in range(32):
        nb = []
        rr = 2 * n + qa
        rs = row_start[rr]
        for m in range(32):
            a = 2 * m + ka
            valid = (a >= rs) & (a < rs + 8) & vcol
            if not valid.any():
                continue
            dy = np.where(valid, a - rr + 7, 0)
            sig = (valid.tobytes(), dy.tobytes())
            if sig not in sigs:
                sigs[sig] = len(mats)
                mats.append((valid, dy))
            table[(n, m)] = sigs[sig]
            nb.append(m)
        nbrs.append(nb)
    return dict(ntypes=len(mats), table=table, nbrs=nbrs, mats=mats, dx=dx)


def na_bias_mats(rel_bias, G):
    out = np.empty((128, 4 * G['ntypes'], 128), dtype=np.float32)
    for h in range(4):
        for t, (valid, dy) in enumerate(G['mats']):
            out[:, h * G['ntypes'] + t, :] = np.where(valid, rel_bias[h][dy, G['dx']], NEG)
    return out.astype(ml_dtypes.bfloat16)


def fnet_consts():
    bf = ml_dtypes.bfloat16
    r = np.arange(64, dtype=np.float64)
    k1 = np.arange(64, dtype=np.float64)
    a = 2 * np.pi * np.outer(r, k1) / 64.0
    D1 = np.stack([np.cos(a), -np.sin(a)], axis=2).reshape(64, 128)
    c = np.arange(64, dtype=np.float64)[:, None, None]
    kk1 = np.arange(64, dtype=np.float64)[None, :, None]
    k2 = np.arange(64, dtype=np.float64)[None, None, :]
    th = 2 * np.pi * c * (kk1 + 64 * k2) / 4096.0
    m3r, m3i = np.cos(th) / 512.0, -np.sin(th) / 512.0
    ra = np.stack([m3r, m3i], axis=3)
    rb = np.stack([-m3i, m3r], axis=3)
    M3 = np.stack([ra, rb], axis=2).reshape(64, 64, 2, 128)
    M3 = np.concatenate([M3, M3], axis=0)
    j = np.arange(64, dtype=np.float64)
    ph = 2 * np.pi * np.outer(j, j) / 64.0
    C, S = np.cos(ph), np.sin(ph)
    CH = np.zeros((128, 2, 2, 2, 2, 64))
    for chp in range(128):
        g, jj = chp // 32, chp % 32
        for ch2 in range(2):
            CH[chp, g // 2, ch2, 0, g % 2, :] = C[2 * jj + ch2]
            CH[chp, g // 2, ch2, 1, g % 2, :] = S[2 * jj + ch2]
    CH = CH.reshape(128, 8, 128)
    t = np.arange(256, dtype=np.float64)
    ac = 2 * np.pi * np.outer(t, t) / 256.0
    DC = np.stack([np.cos(ac), -np.sin(ac)], axis=2).reshape(2, 128, 512).transpose(1, 0, 2) / 128.0
    CHc = np.zeros((128, 2, 2, 64))
    for p_ in range(128):
        CHc[p_, 0, p_ // 64, :] = C[p_ % 64]
        CHc[p_, 1, p_ // 64, :] = S[p_ % 64]
    CHc = CHc.reshape(128, 2, 128)
    return dict(fD1=D1.astype(bf), fM3=M3.astype(bf), fCH=CH.astype(bf), fDC=np.ascontiguousarray(DC).astype(bf),
                fCHc=CHc.astype(bf))


_FC = {}


def make_core_inputs(b, inp, depth=DEPTH):
    f = lambda a: np.ascontiguousarray(np.asarray(a, dtype=np.float32))
    m = {}
    m["xin"] = f(np.concatenate([np.asarray(inp["ctx"][b]), np.asarray(inp["x"][b])], axis=0))
    cv = np.stack([np.asarray(inp["c"][b]).reshape(8, 128).T, np.asarray(inp["c_ctx"]).reshape(8, 128).T], axis=2)
    m["cvec"] = f(cv)
    m["ident_f"] = np.eye(128, dtype=np.float32)
    m["ident_b"] = np.eye(128).astype(ml_dtypes.bfloat16)
    m["cs_tab"] = _rope_table()
    jj, ii = np.arange(128)[:, None], np.arange(128)[None, :]
    m["cmask"] = np.stack([(ii <= jj), (jj <= ii)], axis=1).astype(ml_dtypes.bfloat16)
    G = na_geometry()
    if not _FC:
        _FC.update(fnet_consts())
    m.update(_FC)
    ss, tt = np.arange(128)[:, None], np.arange(128)[None, :]
    m["tri"] = np.stack([ss <= tt, ss >= tt, ss > tt, ss < tt], axis=1).astype(np.float32)
    hd = np.arange(128)[:, None] // 32
    m["blkmask"] = np.concatenate([(hd == (np.arange(256)[None, :] // 64)), (hd == np.arange(4)[None, :])],
                                  axis=1).astype(np.float32)
    perm = _col_perm()
    for l in range(depth):
        m["ada_w%d" % l] = f(inp["ada_w"][l])
        m["ada_brow%d" % l] = f(np.broadcast_to(np.asarray(inp["ada_b"][l])[None, :], (2, 3 * D)))
        m["norm_wT%d" % l] = f(np.asarray(inp["norm_w"][l]).reshape(8, 128).T)
        m["wp%d" % l] = f(np.asarray(inp["w_in"][l])[:, perm])
        qn, kn = np.asarray(inp["swa_q_norm"][l]), np.asarray(inp["swa_k_norm"][l])
        m["wc%d" % l] = f(np.broadcast_to(np.concatenate([np.tile(qn, 4), np.tile(kn, 2)])[None, :], (128, 384)))
        qn, kn = np.asarray(inp["na_q_norm"][l]), np.asarray(inp["na_k_norm"][l])
        m["wd%d" % l] = f(np.broadcast_to(np.concatenate([np.tile(qn, 4), np.tile(kn, 4)])[None, :], (128, 512)))
        wdec = np.zeros((33, 256), np.float32)
        wdec[0:16, 0:128] = np.asarray(inp["gla_dec_w"][l][0])
        wdec[16:32, 128:256] = np.asarray(inp["gla_dec_w"][l][1])
        wdec[32, :] = np.asarray(inp["gla_dec_b"][l]).reshape(256)
        m["wdec%d" % l] = wdec
        m["fw%d" % l] = f(inp["fnet_w"][l])
        m["wo%d" % l] = f(inp["w_out"][l])
        m["onb%d" % l] = f(np.broadcast_to(np.asarray(inp["gla_out_norm"][l])[None, :], (128, 64)))
        m["sinkb%d" % l] = f(np.broadcast_to(np.asarray(inp["swa_sink"][l])[None, :], (128, 4)))
        m["nab%d" % l] = na_bias_mats(np.asarray(inp["na_rel_bias"][l], dtype=np.float32), G)
    return m


_CACHE = {}


def kernel(**inputs):
    if "nc" not in _CACHE:
        _CACHE["nc"] = Builder().build()
    nc = _CACHE["nc"]
    ncores = int(os.environ.get("K_NCORES", "8"))
    in_maps = [make_core_inputs(i % 4, inputs) for i in range(ncores)]
    res = run_bass_kernel_spmd(nc, in_maps, core_ids=list(range(ncores)))
    out = np.stack([np.asarray(res.results[b]["out"], dtype=np.float32) for b in range(4)], axis=0)
    return out
```

```python
import os
import numpy as np
import ml_dtypes
import concourse.bass as bass
import concourse.mybir as mybir
from concourse.bass_utils import run_bass_kernel_spmd

F32 = mybir.dt.float32
BF16 = mybir.dt.bfloat16
AF = mybir.ActivationFunctionType
ALU = mybir.AluOpType
AX = mybir.AxisListType

D = 1024
NL = 4096
NC_ = 256
NT = NL + NC_
NTILE = NT // 128
DEPTH = 2
EPS = 1e-6
PW = 3104
GROUPS = [(0, 2)] + [(2 + 4 * i, 4) for i in range(8)]
NEG = -30000.0

ENGS = ['pe', 'act', 'dve', 'pool', 'sp']
NDMA = 40


class Buf:
    __slots__ = ('w', 'wa', 'r', 'excl')

    def __init__(self, excl=False):
        self.w = {}
        self.wa = {}
        self.r = {}
        self.excl = excl


MAXOUT = int(os.environ.get('KS_MAXOUT', '6'))
NOADDS = os.environ.get('KS_NOADDS', '0') == '1'


class Sched:
    def __init__(self):
        self.prog = {e: [] for e in ENGS}
        self.cnt = {e: 0 for e in ENGS}
        self.waited = {e: {} for e in ENGS}
        self.dma_val = [0] * NDMA
        self.dma_rr = 0
        self.dma_rr2 = 0
        self.outst = {e: [] for e in ENGS}

    def _waits(self, eng, deps):
        for key, val in deps:
            if key == 'pe' and eng == 'pe':
                continue
            if self.waited[eng].get(key, 0) >= val:
                continue
            self.waited[eng][key] = val
            self.prog[eng].append(('wait', key, val))

    @staticmethod
    def _deps(reads, writes, adds=()):
        deps = []
        for b in reads:
            deps.extend(b.w.items())
            deps.extend(b.wa.items())
        for b in writes:
            deps.extend(b.w.items())
            deps.extend(b.wa.items())
            deps.extend(b.r.items())
        for b in adds:
            deps.extend(b.w.items())
            deps.extend(b.r.items())
        return deps

    @staticmethod
    def _mark(tok, reads, writes, adds=()):
        for b in reads:
            if b.r.get(tok[0], 0) < tok[1]:
                b.r[tok[0]] = tok[1]
        for b in writes:
            b.w = {tok[0]: tok[1]}
            b.wa = {}
            b.r = {}
        for b in adds:
            if b.wa.get(tok[0], 0) < tok[1]:
                b.wa[tok[0]] = tok[1]

    def op(self, eng, fn, reads=(), writes=(), adds=()):
        if NOADDS:
            writes, adds = list(writes) + list(adds), ()
        ex = [b for b in reads if b.excl]
        if ex:
            reads = [b for b in reads if not b.excl]
            writes = list(writes) + ex
        self._waits(eng, self._deps(reads, writes, adds))
        self.cnt[eng] += 1
        tok = (eng, self.cnt[eng])
        self.prog[eng].append(('op', fn, eng))
        self._mark(tok, reads, writes, adds)
        return tok

    def dma(self, q, out_ap, in_ap, reads=(), writes=(), adds=()):
        if NOADDS:
            writes, adds = list(writes) + list(adds), ()
        half = NDMA // 2
        if q == 'sp':
            i = self.dma_rr % half
            self.dma_rr += 1
        else:
            i = half + self.dma_rr2 % half
            self.dma_rr2 += 1
        key = 'd%d' % i
        deps = self._deps(reads, writes, adds)
        if self.dma_val[i] > 0:
            deps.append((key, self.dma_val[i]))
        if len(self.outst[q]) >= (MAXOUT if q == 'sp' else 10):
            deps.append(self.outst[q].pop(0))
        self._waits(q, deps)
        self.dma_val[i] += 16
        tok = (key, self.dma_val[i])
        self.outst[q].append(tok)
        self.prog[q].append(('dma', out_ap, in_ap, key))
        self._mark(tok, reads, writes, adds)
        return tok

    def barrier(self):
        for e in ENGS:
            deps = [(e2, self.cnt[e2]) for e2 in ENGS if e2 != e and self.cnt[e2] > 0]
            deps += [('d%d' % i, v) for i, v in enumerate(self.dma_val) if v > 0]
            self._waits(e, deps)

    def emit(self, nc, sems):
        if os.environ.get("K_MULTIBLOCK", "0") == "1":
            return self.flush(nc, sems)
        self.barrier()

    def flush(self, nc, sems):
        self.barrier()
        prog = self.prog
        self.prog = {e: [] for e in ENGS}

        def mk(e):
            def f(engobj):
                for it in prog[e]:
                    if it[0] == 'wait':
                        engobj.wait_ge(sems[it[1]], it[2])
                    elif it[0] == 'op':
                        it[1](engobj).then_inc(sems[it[2]], 1)
                    else:
                        engobj.dma_start(out=it[1], in_=it[2]).then_inc(sems[it[3]], 16)
            return f

        with nc.Block() as block:
            block.tensor(mk('pe'))
            block.scalar(mk('act'))
            block.vector(mk('dve'))
            block.gpsimd(mk('pool'))
            block.sync(mk('sp'))


class Arena:
    _bases = {}

    def __init__(self, nc, base, limit):
        self.nc = nc
        self.base = base
        self.limit = limit
        self.off = base
        key = (id(nc), base, limit)
        if key not in Arena._bases:
            t = nc.alloc_sbuf_tensor_at("arena_%d" % base, [128, (limit - base) // 2], BF16, offset=base)
            Arena._bases[key] = t.ap()
        self.ap = Arena._bases[key]

    def reset(self):
        self.off = self.base

    def view(self, shape, dtype, off):
        esz = 4 if dtype == F32 else 2
        n = int(np.prod(shape[1:]))
        o2 = (off - self.base) // 2
        v = self.ap[0:shape[0], o2:o2 + n * esz // 2]
        if dtype != BF16:
            v = v.bitcast(dtype)
        if len(shape) > 2:
            names = " ".join("d%d" % i for i in range(len(shape) - 1))
            kw = {"d%d" % i: shape[i + 1] for i in range(len(shape) - 1)}
            v = v.rearrange("p (%s) -> p %s" % (names, names), **kw)
        return v

    def alloc(self, shape, dtype, name=None):
        esz = 4 if dtype == F32 else 2
        nbytes = int(np.prod(shape[1:])) * esz
        nbytes = (nbytes + 63) // 64 * 64
        assert self.off + nbytes <= self.limit, ("SBUF arena overflow", self.off, nbytes, self.limit)
        v = self.view(shape, dtype, self.off)
        self.off += nbytes
        return v


class Rot:
    def __init__(self, items):
        self.items = items
        self.i = 0

    def next(self):
        it = self.items[self.i % len(self.items)]
        self.i += 1
        return it


def pipeline2(gens):
    prev = None
    for g in gens:
        next(g)
        if prev is not None:
            for _ in prev:
                pass
        prev = g
    if prev is not None:
        for _ in prev:
            pass


def pipelineN(gens, nstage):
    n = len(gens)
    for t in range(n + nstage - 1):
        for k in range(nstage):
            i = t - k
            if 0 <= i < n:
                try:
                    next(gens[i])
                except StopIteration:
                    assert k == nstage - 1, (k, nstage)
                else:
                    assert k < nstage - 1, (k, nstage)


def store_item(fn):
    def g(nstage):
        for _ in range(nstage - 1):
            yield
        fn()
    return g


def bc(ap, shape):
    return ap.broadcast_to(list(shape))


class Builder:
    def __init__(self, debug=False, depth=DEPTH, stop_after=None):
        self.debug = debug
        self.depth = depth
        self.stop_after = stop_after
        nc = bass.Bass("TRN2", target_bir_lowering=False)
        self.nc = nc
        self.s = Sched()
        self.dbg_outs = []
        self.dram_bufs = {}

    def dma_tm(self, q, sb, dr, t0, t1, load, reads, writes, step=4):
        for a in range(t0, t1, step):
            b = min(a + step, t1)
            d = dr[a * 128:b * 128, :].rearrange("(i p) c -> p i c", p=128)
            sview = sb[:, a:b, :]
            if load:
                self.s.dma(q, sview, d, reads=reads, adds=writes)
            else:
                self.s.dma(q, d, sview, reads=reads, adds=writes)

    def din(self, name, shape, dtype=F32):
        return self.nc.dram_tensor(name, list(shape), dtype, kind="ExternalInput").ap()

    def dscr(self, name, shape, dtype):
        kind = "ExternalOutput" if self.debug else "Internal"
        t = self.nc.dram_tensor(name, list(shape), dtype, kind=kind).ap()
        if self.debug:
            self.dbg_outs.append(name)
        self.dram_bufs[name] = Buf()
        return t

    def build(self):
        nc = self.nc
        s = self.s
        self.xin = self.din("xin", [NT, D])
        self.cvec = self.din("cvec", [128, 8, 2])
        self.ident_f_d = self.din("ident_f", [128, 128])
        self.ident_b_d = self.din("ident_b", [128, 128], BF16)
        self.cs_d = self.din("cs_tab", [NL, 64])
        self.L = []
        for l in range(self.depth):
            Lw = dict(
                ada_w=self.din("ada_w%d" % l, [D, 3 * D]),
                ada_brow=self.din("ada_brow%d" % l, [2, 3 * D]),
                norm_wT=self.din("norm_wT%d" % l, [128, 8]),
                wp=self.din("wp%d" % l, [D, PW]),
                wc=self.din("wc%d" % l, [128, 384]),
                wd=self.din("wd%d" % l, [128, 512]),
            )
            self.L.append(Lw)
        self.out = self.nc.dram_tensor("out", [NL, D], F32, kind="ExternalOutput").ap()
        self.Gs = self.dscr("Gs", [NT, 1024], BF16)
        self.Aqkv = self.dscr("Aqkv", [NT, 512], BF16)
        self.LrT = self.dscr("LrT", [32, NT], F32)
        self.CT = self.dscr("CT", [4, 128, NT], BF16)
        self.CV1 = self.dscr("CV1", [NT, 132], BF16)
        self.DT = self.dscr("DT", [4, 128, NT], BF16)
        self.DV1 = self.dscr("DV1", [NT, 264], BF16)
        self.Bv = self.dscr("Bv", [NT, 256], BF16)
        self.NAG = na_geometry()
        if self.debug:
            self.modT_d = self.dscr("modT_dbg", [128, 48], F32)
        if self.stop_after == 'p12':
            return self._build_rest()
        self.Ymix = self.dscr("Ymix", [NT, 1024], BF16)
        self.X1 = self.dscr("X1", [NT, D], F32)
        self.cmask_d = self.din("cmask", [128, 2, 128], BF16)
        self.tri_d = self.din("tri", [128, 4, 128])
        self.fD1_d = self.din("fD1", [64, 128], BF16)
        self.fM3_d = self.din("fM3", [128, 64, 2, 128], BF16)
        self.fCH_d = self.din("fCH", [128, 8, 128], BF16)
        self.fDC_d = self.din("fDC", [128, 2, 512], BF16)
        self.fCHc_d = self.din("fCHc", [128, 2, 128], BF16)
        self.blk_d = self.din("blkmask", [128, 260])
        for l in range(self.depth):
            self.L[l]['sinkb'] = self.din("sinkb%d" % l, [128, 4])
            self.L[l]['wdec'] = self.din("wdec%d" % l, [33, 256])
            self.L[l]['onb'] = self.din("onb%d" % l, [128, 64])
            self.L[l]['fw'] = self.din("fw%d" % l, [256, 256])
            self.L[l]['wo'] = self.din("wo%d" % l, [D, D])
            self.L[l]['nab'] = self.din("nab%d" % l, [128, 4 * self.NAG['ntypes'], 128], BF16)
        return self._build_rest()

    def _build_rest(self):
        nc = self.nc
        s = self.s
        from contextlib import ExitStack
        with ExitStack() as st:
            sems = {}
            for e in ENGS:
                sems[e] = st.enter_context(nc.semaphore("sem_" + e))
            for i in range(NDMA):
                sems['d%d' % i] = st.enter_context(nc.semaphore("semd%d" % i))
            self.sems = sems
            self.ps = []
            for i in range(8):
                t = nc.alloc_psum_tensor("psb%d" % i, [128, 512], F32)
                self.ps.append((t.ap(), Buf(excl=True)))
            pa = Arena(nc, 16512, 16512 + 12 * 1024)
            self.ident_f = pa.alloc([128, 128], F32, "idf")
            self.ident_b = pa.alloc([128, 128], BF16, "idb")
            self.scT = pa.alloc([128, 8, 2], F32, "scT")
            self.modT = [pa.alloc([128, 24, 2], F32, "modT%d" % l) for l in range(self.depth)]
            self.Amod = [pa.alloc([128, 8, 2], F32, "Amod%d" % l) for l in range(self.depth)]
            self.cvs = pa.alloc([128, 8, 2], F32, "cvs")
            self.b_const = Buf()
            self.b_out = Buf()
            self.b_mod = [Buf() for _ in range(self.depth)]
            self.arena = Arena(nc, 16512 + 12 * 1024, 229312)

            s.dma('sp', self.ident_f, self.ident_f_d, adds=[self.b_const])
            s.dma('sp', self.ident_b, self.ident_b_d, adds=[self.b_const])
            s.dma('sp', self.cvs, self.cvec, adds=[self.b_const])
            s.op('act', lambda e: e.activation(out=self.scT, in_=self.cvs, func=AF.Silu),
                 reads=[self.b_const], writes=[self.b_const])
            for l in range(self.depth):
                self.phase_mod(l)
                s.emit(nc, sems)
            for l in range(self.depth):
                self.phase_weights(l)
                s.emit(nc, sems)
                self.x_src = self.xin if l == 0 else self.X1
                self.b_xsrc = Buf() if l == 0 else self.dram_bufs['X1']
                self.phase_p12(l)
                s.emit(nc, sems)
                if self.stop_after == 'p12':
                    break
                self.phase_c(l)
                s.emit(nc, sems)
                if self.stop_after == 'c':
                    break
                self.phase_d(l)
                s.emit(nc, sems)
                if self.stop_after == 'cd':
                    break
                self.phase_a(l)
                s.emit(nc, sems)
                if self.stop_after == 'a':
                    break
                self.phase_b(l)
                s.emit(nc, sems)
                if self.stop_after == 'b':
                    break
                self.phase_o(l)
                s.emit(nc, sems)
            s.flush(nc, sems)
        return nc

    def phase_mod(self, l):
        nc, s = self.nc, self.s
        ar = self.arena
        ar.reset()
        Lw = self.L[l]
        wb = Rot([(ar.alloc([128, 3 * D], F32, "adaw"), Buf()) for _ in range(4)])
        row = ar.alloc([2, 3 * D], F32, "modrow")
        brow = ar.alloc([2, 3 * D], F32, "brow")
        nwT = ar.alloc([128, 8], F32, "nwT")
        tmp = ar.alloc([128, 8, 2], F32, "tmpm")
        b_small, b_row = Buf(), Buf()
        s.dma('sp', brow, Lw['ada_brow'], adds=[b_small])
        s.dma('sp', nwT, Lw['norm_wT'], adds=[b_small])
        for k in range(8):
            w, bw = wb.next()
            s.dma('sp', w, Lw['ada_w'][k * 128:(k + 1) * 128, :], writes=[bw])
            for cc in range(6):
                pk, bpk = self.ps[cc]
                s.op('pe', (lambda w=w, k=k, cc=cc, pk=pk: lambda e: e.matmul(
                    pk[0:2, 0:512], lhsT=self.scT[:, k, :], rhs=w[:, cc * 512:(cc + 1) * 512],
                    start=(k == 0), stop=(k == 7)))(), reads=[bw, self.b_const], writes=[bpk])
        for cc in range(6):
            pk, bpk = self.ps[cc]
            s.op('dve', (lambda cc=cc, pk=pk: lambda e: e.tensor_tensor(
                out=row[:, cc * 512:(cc + 1) * 512], in0=pk[0:2, 0:512], in1=brow[:, cc * 512:(cc + 1) * 512],
                op=ALU.add))(), reads=[bpk, b_small], adds=[b_row])
        psT, bpT = self.ps[6]
        for j in range(24):
            s.op('pe', (lambda j=j: lambda e: e.matmul(
                psT[:, 2 * j:2 * j + 2], lhsT=row[0:2, j * 128:(j + 1) * 128], rhs=self.ident_f[0:2, 0:2],
                start=True, stop=True))(), reads=[b_row, self.b_const], writes=[bpT])
        modT = self.modT[l]
        bmod = self.b_mod[l]
        s.op('dve', lambda e: e.tensor_copy(out=modT, in_=psT[:, 0:48].rearrange("p (j i) -> p j i", i=2)),
             reads=[bpT], writes=[bmod])
        s.op('dve', lambda e: e.tensor_scalar(out=tmp, in0=modT[:, 8:16, :], scalar1=1.0, scalar2=None,
                                              op0=ALU.add), reads=[bmod], writes=[b_small])
        Am = self.Amod[l]
        s.op('dve', lambda e: e.tensor_tensor(out=Am, in0=tmp, in1=bc(nwT.unsqueeze(2), [128, 8, 2]),
                                              op=ALU.mult), reads=[b_small], writes=[bmod])
        if self.debug and l == 0:
            s.dma('pool', self.modT_d, modT.rearrange("p j i -> p (j i)"), reads=[bmod],
                  writes=[self.dram_bufs["modT_dbg"]])

    def phase_weights(self, l):
        nc, s = self.nc, self.s
        ar = self.arena
        ar.reset()
        self.Wb = ar.alloc([128, 8, PW], BF16, "Wb")
        self.b_W = Buf()
        self.p12_base = ar.off
        stg = Rot([(ar.alloc([128, PW], F32, "stgW"), Buf()) for _ in range(4)])
        wp = self.L[l]['wp']
        for k in range(8):
            w, bw = stg.next()
            s.dma('sp', w, wp[k * 128:(k + 1) * 128, :], writes=[bw])
            s.op('act', (lambda w=w, k=k: lambda e: e.activation(
                out=self.Wb[:, k, 0:1024], in_=w[:, 0:1024], func=AF.Copy))(), reads=[bw], adds=[self.b_W])
            s.op('dve', (lambda w=w, k=k: lambda e: e.tensor_copy(
                out=self.Wb[:, k, 1024:2048], in_=w[:, 1024:2048]))(), reads=[bw], adds=[self.b_W])
            s.op('pool', (lambda w=w, k=k: lambda e: e.tensor_copy(
                out=self.Wb[:, k, 2048:PW], in_=w[:, 2048:PW]))(), reads=[bw], adds=[self.b_W])

    def phase_p12(self, l):
        nc, s = self.nc, self.s
        ar = self.arena
        ar.off = self.p12_base
        Lw = self.L[l]
        Wb, bW = self.Wb, self.b_W
        Am, modT, bmod = self.Amod[l], self.modT[l], self.b_mod[l]
        wc = ar.alloc([128, 384], F32, "wc")
        wd = ar.alloc([128, 512], F32, "wd")
        b_nw = Buf()
        s.dma('sp', wc, Lw['wc'], adds=[b_nw])
        s.dma('sp', wd, Lw['wd'], adds=[b_nw])

        def rot(shape, dt, name, n):
            return Rot([(ar.alloc(shape, dt, name), Buf()) for _ in range(n)])
        xt = rot([128, D], F32, "xt", 3)
        junk = ar.alloc([128, D], BF16, "junk")
        b_junk = Buf()
        xn = rot([128, D], F32, "xn", 2)
        hT = rot([128, 8, 128], BF16, "hT", 3)
        st = rot([128, 8], F32, "stat", 4)
        raw = rot([128, 896], F32, "raw", 3)
        nt = rot([128, 896], F32, "nt", 3)
        nt2 = rot([128, 384], F32, "nt2", 3)
        st8 = rot([128, 64], F32, "st8", 4)
        rt = rot([128, 4, 192], F32, "rt", 2)
        qkr = rot([128, 384], BF16, "qkr", 3)
        kd = rot([128, 256], BF16, "kd", 3)
        dqk = rot([128, 512], BF16, "dqk", 3)
        cs = rot([128, 4, 64], F32, "cs", 3)

        def stage_set():
            d = dict(
                G=ar.alloc([128, 4, 1024], BF16, "sG"), A=ar.alloc([128, 4, 512], BF16, "sA"),
                CT=ar.alloc([128, 4, 512], BF16, "sCT"), CV=ar.alloc([128, 4, 2, 66], BF16, "sCV"),
                DT=ar.alloc([128, 4, 512], BF16, "sDT"), DV=ar.alloc([128, 4, 4, 66], BF16, "sDV"),
                BV=ar.alloc([128, 4, 256], BF16, "sBV"), LR=ar.alloc([32, 512], F32, "sLR"))
            d['buf'] = Buf()
            d['bufB'] = Buf()
            return d
        stg = Rot([stage_set(), stage_set()])
        for sset in stg.items:
            for nm in ('CV', 'DV'):
                s.op('pool', (lambda a=sset[nm]: lambda e: e.memset(a, 1.0))(), writes=[sset['buf']])

        TP = [self.ps[0], self.ps[1]]
        MM = Rot([self.ps[2], self.ps[3], self.ps[4], self.ps[5]])
        TQ, bTQ = self.ps[6]
        TL, bTL = self.ps[7]
        TQb = TQ.bitcast(BF16)
        db = self.dram_bufs

        def stores(sset, bS, t0, n4, batch):
            u0 = t0 * 128
            n = n4 * 128
            tm = lambda dr: dr[u0:u0 + n, :].rearrange("(i p) c -> p i c", p=128)
            if batch == 1:
                bSB = sset['bufB']
                s.dma('pool', self.CT[:, :, u0:u0 + n].rearrange("a p t -> p a t"), sset['CT'][:, :, 0:n],
                      reads=[bSB], adds=[db['CT']])
                s.dma('pool', self.DT[:, :, u0:u0 + n].rearrange("a p t -> p a t"), sset['DT'][:, :, 0:n],
                      reads=[bSB], adds=[db['DT']])
                return
            s.dma('pool', tm(self.Gs), sset['G'][:, 0:n4, :], reads=[bS], adds=[db['Gs']])
            s.dma('pool', tm(self.Aqkv), sset['A'][:, 0:n4, :], reads=[bS], adds=[db['Aqkv']])
            s.dma('pool', tm(self.CV1), sset['CV'][:, 0:n4].rearrange("p i g d -> p i (g d)"), reads=[bS],
                  adds=[db['CV1']])
            s.dma('pool', tm(self.DV1), sset['DV'][:, 0:n4].rearrange("p i g d -> p i (g d)"), reads=[bS],
                  adds=[db['DV1']])
            s.dma('pool', tm(self.Bv), sset['BV'][:, 0:n4, :], reads=[bS], adds=[db['Bv']])
            s.dma('pool', self.LrT[:, u0:u0 + n], sset['LR'][0:32, 0:n], reads=[bS], adds=[db['LrT']])

        def tile_body(t0, n4, i4, sset, cs_slot):
            bS = sset['buf']
            latent = t0 >= 2
            mi = 0 if latent else 1
            ti = t0 + i4
            u0 = ti * 128
            cst, bcs = cs_slot if latent else (None, None)
            if latent and i4 == 0:
                tl0 = (t0 - 2) * 128
                s.dma('sp', cst[:, 0:n4, :], self.cs_d[tl0:tl0 + n4 * 128, :].rearrange("(i p) c -> p i c", p=128),
                      writes=[bcs])
            x_t, bx = xt.next()
            s.dma('sp', x_t, self.x_src[u0:u0 + 128, :], reads=[self.b_xsrc], writes=[bx])
            stt, bst = st.next()
            s.op('act', lambda e: e.activation(out=junk, in_=x_t, func=AF.Square, accum_out=stt[:, 0:1]),
                 reads=[bx], writes=[b_junk, bst])
            yield
            s.op('dve', lambda e: e.tensor_scalar(out=stt[:, 1:2], in0=stt[:, 0:1], scalar1=1.0 / D, scalar2=EPS,
                                                  op0=ALU.mult, op1=ALU.add), reads=[bst], writes=[bst])
            self.rsqrt1(stt[:, 1:2], stt[:, 2:3], stt[:, 3:4], bst)
            yield
            x_n, bxn = xn.next()
            s.op('act', lambda e: e.activation(out=x_n, in_=x_t, func=AF.Identity, scale=stt[:, 2:3]),
                 reads=[bx, bst], writes=[bxn])
            yield
            h_t, bh = hT.next()
            for half in range(2):
                tp, btp = TP[half]
                for kk in range(4):
                    k = half * 4 + kk
                    s.op('pe', (lambda tp=tp, kk=kk, k=k: lambda e: e.transpose(
                        out=tp[:, kk * 128:(kk + 1) * 128], in_=x_n[:, k * 128:(k + 1) * 128],
                        identity=self.ident_f))(), reads=[bxn, self.b_const], writes=[btp])
                for kk in range(4):
                    k = half * 4 + kk
                    if False:
                        pass
                    else:
                        s.op('act', (lambda tp=tp, kk=kk, k=k: lambda e: e.activation(
                            out=h_t[:, k, :], in_=tp[:, kk * 128:(kk + 1) * 128], func=AF.Identity,
                            scale=Am[:, k, mi:mi + 1], bias=modT[:, k, mi:mi + 1]))(),
                            reads=[btp, bmod], adds=[bh])
            yield
            def mm_chunk(c):
                pm, bpm = MM.next()
                for k in range(8):
                    s.op('pe', (lambda pm=pm, k=k, c=c: lambda e: e.matmul(
                        pm[:, 0:512], lhsT=h_t[:, k, :], rhs=Wb[:, k, c * 512:(c + 1) * 512],
                        start=(k == 0), stop=(k == 7)))(), reads=[bh, bW], writes=[bpm])
                return pm, bpm
            r_w, brw = raw.next()
            pm, bpm = mm_chunk(0)
            s.op('act', (lambda pm=pm: lambda e: e.activation(out=sset['A'][:, i4, :], in_=pm, func=AF.Copy))(),
                 reads=[bpm], adds=[bS])
            for c in (1, 2):
                pm, bpm = mm_chunk(c)
                s.op('act', (lambda pm=pm, c=c: lambda e: e.activation(
                    out=sset['G'][:, i4, (c - 1) * 512:c * 512], in_=pm, func=AF.Silu))(), reads=[bpm], adds=[bS])
            pm, bpm = mm_chunk(3)
            s.op('dve', (lambda pm=pm: lambda e: e.tensor_copy(out=r_w[:, 0:384], in_=pm[:, 0:384]))(),
                 reads=[bpm], adds=[brw])
            s.op('dve', (lambda pm=pm: lambda e: e.tensor_copy(
                out=sset['CV'][:, i4, :, 0:64], in_=pm[:, 384:512].rearrange("p (g d) -> p g d", g=2)))(),
                reads=[bpm], adds=[bS])
            pm, bpm = mm_chunk(4)
            s.op('act', (lambda pm=pm: lambda e: e.activation(out=r_w[:, 384:896], in_=pm[:, 0:512], func=AF.Copy))(),
                 reads=[bpm], adds=[brw])
            pm, bpm = mm_chunk(5)
            s.op('dve', (lambda pm=pm: lambda e: e.tensor_copy(
                out=sset['DV'][:, i4, :, 0:64], in_=pm[:, 0:256].rearrange("p (g d) -> p g d", g=4)))(),
                reads=[bpm], adds=[bS])
            s.op('act', (lambda pm=pm: lambda e: e.activation(out=sset['BV'][:, i4, :], in_=pm[:, 256:512],
                                                              func=AF.Copy))(), reads=[bpm], adds=[bS])
            for k in range(8):
                s.op('pe', (lambda k=k: lambda e: e.matmul(
                    TL[0:32, 0:128], lhsT=Wb[:, k, 3072:3104], rhs=h_t[:, k, :],
                    start=(k == 0), stop=(k == 7)))(), reads=[bh, bW], writes=[bTL])
            s.op('act', lambda e: e.activation(out=sset['LR'][0:32, i4 * 128:(i4 + 1) * 128], in_=TL[0:32, 0:128],
                                               func=AF.Copy), reads=[bTL], adds=[bS])
            n_t, bnt = nt.next()
            s.op('pool', lambda e: e.tensor_tensor(out=n_t, in0=r_w, in1=r_w, op=ALU.mult), reads=[brw], writes=[bnt])
            if i4 == n4 - 1:
                stores(sset, bS, t0, n4, 0)
            yield
            s8, bs8 = st8.next()
            s.op('dve', lambda e: e.tensor_reduce(out=s8[:, 0:14], in_=n_t.rearrange("p (h d) -> p h d", d=64),
                                                  axis=AX.X, op=ALU.add), reads=[bnt], writes=[bs8])
            s.op('dve', lambda e: e.tensor_scalar(out=s8[:, 0:14], in0=s8[:, 0:14], scalar1=1.0 / 64, scalar2=EPS,
                                                  op0=ALU.mult, op1=ALU.add), reads=[bs8], writes=[bs8])
            self.rsqrt(s8[:, 0:14], s8[:, 16:30], s8[:, 32:46], s8[:, 48:62], bs8)
            s.op('dve', lambda e: e.tensor_scalar(out=s8[:, 16:20], in0=s8[:, 16:20], scalar1=0.125, scalar2=None,
                                                  op0=ALU.mult), reads=[bs8], writes=[bs8])
            s.op('dve', lambda e: e.tensor_scalar(out=s8[:, 22:26], in0=s8[:, 22:26], scalar1=0.125, scalar2=None,
                                                  op0=ALU.mult), reads=[bs8], writes=[bs8])
            yield
            s.op('dve', lambda e: e.tensor_tensor(
                out=n_t.rearrange("p (h d) -> p h d", d=64), in0=r_w.rearrange("p (h d) -> p h d", d=64),
                in1=bc(s8[:, 16:30].unsqueeze(2), [128, 14, 64]), op=ALU.mult), reads=[brw, bs8], writes=[bnt])
            q2, bq2 = nt2.next()
            s.op('pool', lambda e: e.tensor_tensor(out=q2, in0=n_t[:, 0:384], in1=wc, op=ALU.mult),
                 reads=[bnt, b_nw], writes=[bq2])
            d_t, bdt = dqk.next()
            s.op('pool', lambda e: e.tensor_tensor(out=d_t, in0=n_t[:, 384:896], in1=wd, op=ALU.mult),
                 reads=[bnt, b_nw], writes=[bdt])
            yield
            q_r, bqr = qkr.next()
            if latent:
                r_t, brt = rt.next()
                cst_i = cst[:, i4, :]
                xv = q2.rearrange("p (h a f) -> p h a f", a=2, f=16)
                ov = q_r.rearrange("p (h a f) -> p h a f", a=2, f=16)

                def csb(off):
                    v = cst_i[:, off:off + 32].rearrange("p (a f) -> p a f", a=2)
                    return bc(v.unsqueeze(1), [128, 6, 2, 16])
                x1h = xv[:, :, 0, :].rearrange("p (h a) f -> p h a f", a=2)
                x2h = xv[:, :, 1, :].rearrange("p (h a) f -> p h a f", a=2)
                o1h = ov[:, :, 0, :].rearrange("p (h a) f -> p h a f", a=2)
                o2h = ov[:, :, 1, :].rearrange("p (h a) f -> p h a f", a=2)
                tv = [r_t[:, j, :].rearrange("p (h a f) -> p h a f", a=2, f=16) for j in range(4)]
                cosb, sinb = csb(0), csb(32)
                s.op('pool', lambda e: e.tensor_tensor(out=tv[0], in0=x1h, in1=cosb, op=ALU.mult),
                     reads=[bq2, bcs], adds=[brt])
                s.op('pool', lambda e: e.tensor_tensor(out=tv[1], in0=x2h, in1=sinb, op=ALU.mult),
                     reads=[bq2, bcs], adds=[brt])
                s.op('dve', lambda e: e.tensor_tensor(out=tv[2], in0=x2h, in1=cosb, op=ALU.mult),
                     reads=[bq2, bcs], adds=[brt])
                s.op('dve', lambda e: e.tensor_tensor(out=tv[3], in0=x1h, in1=sinb, op=ALU.mult),
                     reads=[bq2, bcs], adds=[brt])
                s.op('pool', lambda e: e.tensor_tensor(out=o1h, in0=tv[0], in1=tv[1], op=ALU.subtract),
                     reads=[brt], writes=[bqr])
                s.op('dve', lambda e: e.tensor_tensor(out=o2h, in0=tv[2], in1=tv[3], op=ALU.add),
                     reads=[brt], writes=[bqr])
            else:
                s.op('pool', lambda e: e.tensor_copy(out=q_r, in_=q2), reads=[bq2], writes=[bqr])
            k_d, bkd = kd.next()
            s.op('pool', lambda e: e.tensor_copy(
                out=k_d.rearrange("p (g j d) -> p g j d", g=2, j=2),
                in_=bc(q_r[:, 256:384].rearrange("p (g d) -> p g d", g=2).unsqueeze(2), [128, 2, 2, 64])),
                reads=[bqr], writes=[bkd])
            yield
            for j in range(4):
                src = q_r[:, j * 128:(j + 1) * 128] if j < 2 else k_d[:, (j - 2) * 128:(j - 1) * 128]
                s.op('pe', (lambda src=src, j=j: lambda e: e.transpose(
                    out=TQb[:, j * 128:(j + 1) * 128], in_=src, identity=self.ident_b))(),
                    reads=[bqr, bkd, self.b_const], writes=[bTQ])
            for j in range(4):
                s.op('pe', (lambda j=j: lambda e: e.transpose(
                    out=TQb[:, 512 + j * 128:512 + (j + 1) * 128], in_=d_t[:, j * 128:(j + 1) * 128],
                    identity=self.ident_b))(), reads=[bdt, self.b_const], writes=[bTQ])
            s.op('act', lambda e: e.activation(
                out=sset['CT'][:, :, i4 * 128:(i4 + 1) * 128],
                in_=TQb[:, 0:512].rearrange("p (a t) -> p a t", a=4), func=AF.Copy), reads=[bTQ], adds=[sset['bufB']])
            s.op('act', lambda e: e.activation(
                out=sset['DT'][:, :, i4 * 128:(i4 + 1) * 128],
                in_=TQb[:, 512:1024].rearrange("p (a t) -> p a t", a=4), func=AF.Copy), reads=[bTQ], adds=[sset['bufB']])
            if i4 == n4 - 1:
                stores(sset, bS, t0, n4, 1)

        gens = []
        for (t0, n4) in GROUPS:
            sset = stg.next()
            cs_slot = cs.next() if t0 >= 2 else None
            for i4 in range(n4):
                gens.append(tile_body(t0, n4, i4, sset, cs_slot))
        pipelineN(gens, 9)

    def phase_c(self, l):
        return self.attn_phase(l, 'c')

    def phase_d(self, l):
        return self.attn_phase(l, 'd')

    def attn_phase(self, l, kind):
        nc, s = self.nc, self.s
        ar = self.arena
        ar.reset()
        need_ctx = (l < DEPTH - 1)
        db = self.dram_bufs
        isC = (kind == 'c')
        G = self.NAG
        ntp = G['ntypes']
        TT, V1d, vw, ycol = (self.CT, self.CV1, 132, 512) if isC else (self.DT, self.DV1, 264, 768)
        QT = [ar.alloc([128, NT], BF16, "QT%d" % g) for g in range(2)]
        KT = [ar.alloc([128, NT], BF16, "KT%d" % g) for g in range(2)]
        V1 = ar.alloc([128, NTILE, vw], BF16, "V1")
        Gg = ar.alloc([128, NTILE, 256], BF16, "Gg")
        Ys = ar.alloc([128, NTILE, 256], BF16, "Ys")
        bQK = [Buf(), Buf()]
        bV, bG, bM = Buf(), Buf(), Buf()
        bYg = [Buf() for _ in range(NTILE)]
        for g in range(2):
            s.dma('sp', QT[g], TT[g], reads=[db['CT' if isC else 'DT']], adds=[bQK[g]])
            s.dma('sp', KT[g], TT[2 + g], reads=[db['CT' if isC else 'DT']], adds=[bQK[g]])
        self.dma_tm('sp', V1, V1d, 0, NTILE, True, [db['CV1' if isC else 'DV1']], [bV])
        self.dma_tm('sp', Gg, self.Gs[:, ycol:ycol + 256], 0, NTILE, True, [db['Gs']], [bG])
        if isC:
            msk = ar.alloc([128, 2, 128], BF16, "cmsk")
            es = ar.alloc([128, 4], F32, "ces")
            s.dma('sp', msk, self.cmask_d, adds=[bM])
            s.dma('sp', es, self.L[l]['sinkb'], adds=[bM])
            s.op('act', lambda e: e.activation(out=es, in_=es, func=AF.Exp), reads=[bM], writes=[bM])
            mtile = lambda h, typ: msk[:, typ, :]
        else:
            NB = ar.alloc([128, 4 * ntp, 128], BF16, "dNB")
            s.dma('sp', NB, self.L[l]['nab'], writes=[bM])
            for h4 in range(4):
                s.op('act', (lambda h4=h4: lambda e: e.activation(
                    out=NB[:, h4 * ntp:(h4 + 1) * ntp, :], in_=NB[:, h4 * ntp:(h4 + 1) * ntp, :], func=AF.Exp))(),
                    reads=[bM], writes=[bM])
            mtile = lambda h, typ: NB[:, h * ntp + typ, :]
        PT = Rot([(ar.alloc([128, 512], BF16, "PT"), Buf()) for _ in range(8)])
        ST = Rot([self.ps[i] for i in range(6)])
        OP = Rot([self.ps[6], self.ps[7]])
        dn = Rot([(ar.alloc([128, 8], F32, "den"), Buf()) for _ in range(4)])

        def keys_for(qt):
            kts = [(0, None), (1, None)]
            if qt >= 2:
                if isC:
                    if qt - 1 >= 2:
                        kts.append((qt - 1, 0))
                    kts.append((qt, None))
                    if qt + 1 < NTILE:
                        kts.append((qt + 1, 1))
                else:
                    n = qt - 2
                    for m in G['nbrs'][n]:
                        kts.append((m + 2, G['table'][(n, m)]))
            return kts

        def body(h, qts):
            g, j = h // 2, h % 2
            jsl = slice(j * 64, (j + 1) * 64)
            hv = g if isC else h
            nq = len(qts)
            W = 128 * nq
            per_q = [keys_for(q) for q in qts]
            union = sorted({kt for kl in per_q for kt, _ in kl})
            upos = {kt: u for u, kt in enumerate(union)}
            per_bank = 512 // W
            nbank = (len(union) + per_bank - 1) // per_bank
            banks = [ST.next() for _ in range(nbank)]
            pts = [PT.next() for _ in range(nbank)]
            q0 = qts[0] * 128
            for u, kt in enumerate(union):
                bk, bbk = banks[u // per_bank]
                c0 = (u % per_bank) * W
                s.op('pe', (lambda bk=bk, c0=c0, kt=kt: lambda e: e.matmul(
                    bk[:, c0:c0 + W], lhsT=KT[g][jsl, kt * 128:(kt + 1) * 128], rhs=QT[g][jsl, q0:q0 + W],
                    start=True, stop=True))(), reads=[bQK[g]], writes=[bbk])
            for bi in range(nbank):
                ncol = min(per_bank, len(union) - bi * per_bank) * W
                bk, bbk = banks[bi]
                pt, bpt = pts[bi]
                s.op('act', (lambda bk=bk, pt=pt, ncol=ncol: lambda e: e.activation(
                    out=pt[:, 0:ncol], in_=bk[:, 0:ncol], func=AF.Exp))(), reads=[bbk], writes=[bpt])
            nm = 0
            for qi, kl in enumerate(per_q):
                for kt, typ in kl:
                    if typ is None:
                        continue
                    u = upos[kt]
                    pt, bpt = pts[u // per_bank]
                    c0 = (u % per_bank) * W + qi * 128
                    eng = 'dve' if (isC or nm % 5 != 4) else 'pool'
                    nm += 1
                    s.op(eng, (lambda pt=pt, c0=c0, typ=typ: lambda e: e.tensor_tensor(
                        out=pt[:, c0:c0 + 128], in0=pt[:, c0:c0 + 128], in1=mtile(h, typ), op=ALU.mult))(),
                        reads=[bM, bpt], writes=[bpt])
            yield
            o, bo = OP.next()
            for qi, kl in enumerate(per_q):
                for ki, (kt, typ) in enumerate(kl):
                    u = upos[kt]
                    pt, bpt = pts[u // per_bank]
                    c0 = (u % per_bank) * W + qi * 128
                    s.op('pe', (lambda pt=pt, c0=c0, kt=kt, ki=ki, qi=qi, nk=len(kl): lambda e: e.matmul(
                        o[:, qi * 66:qi * 66 + 65], lhsT=pt[:, c0:c0 + 128], rhs=V1[:, kt, hv * 66:hv * 66 + 65],
                        start=(ki == 0), stop=(ki == nk - 1)))(), reads=[bpt, bV], writes=[bo])
            d, bd = dn.next()
            ov = o[:, 0:66 * nq].rearrange("p (q c) -> p q c", c=66)
            if isC:
                s.op('dve', lambda e: e.tensor_tensor(out=d[:, 0:nq], in0=ov[:, :, 64],
                                                      in1=bc(es[:, h:h + 1], [128, nq]), op=ALU.add),
                     reads=[bo, bM], writes=[bd])
                s.op('dve', lambda e: e.reciprocal(out=d[:, 4:4 + nq], in_=d[:, 0:nq]), reads=[bd], writes=[bd])
            else:
                s.op('dve', lambda e: e.reciprocal(out=d[:, 4:4 + nq], in_=ov[:, :, 64]), reads=[bo], writes=[bd])
            for qi, qt in enumerate(qts):
                s.op('dve', (lambda qi=qi, qt=qt: lambda e: e.scalar_tensor_tensor(
                    out=Ys[:, qt, h * 64:(h + 1) * 64], in0=o[:, qi * 66:qi * 66 + 64], scalar=d[:, 4 + qi:5 + qi],
                    in1=Gg[:, qt, h * 64:(h + 1) * 64], op0=ALU.mult, op1=ALU.mult))(),
                    reads=[bo, bd, bG], adds=[bYg[qt // 4]])

        pairs = ([[0, 1]] if need_ctx else []) + [[q, q + 1] for q in range(2, NTILE, 2)]
        gens = []
        for pr in pairs:
            for h in range(4):
                gens.append(body(h, pr))
            qt = pr[-1]
            if qt % 4 == 3 or qt == NTILE - 1:
                g0 = max(pairs[0][0], (qt // 4) * 4)
                gens.append(store_item((lambda g0=g0, qt=qt: lambda: self.dma_tm(
                    'sp', Ys, self.Ymix[:, ycol:ycol + 256], g0, qt + 1, False, [bYg[qt // 4]], [db['Ymix']]))())(2))
        pipeline2(gens)

    def phase_a(self, l):
        nc, s = self.nc, self.s
        ar = self.arena
        ar.reset()
        need_ctx = (l < DEPTH - 1)
        db = self.dram_bufs
        A = ar.alloc([128, NTILE, 512], BF16, "aQKV")
        LRf = ar.alloc([96, NT], F32, "aLRf")
        LRx = ar.alloc([96, NT], BF16, "aLRx")
        Ga = ar.alloc([128, NTILE, 256], BF16, "aG")
        Ost = ar.alloc([128, NTILE, 256], F32, "aOst")
        Ys = ar.alloc([128, NTILE, 256], BF16, "aY")
        tri = ar.alloc([128, 4, 128], F32, "tri")
        blk = ar.alloc([128, 260], F32, "blk")
        Wf = ar.alloc([96, 256], F32, "wdecf")
        Wx = ar.alloc([96, 256], BF16, "wdecx")
        bdec = ar.alloc([1, 256], F32, "bdec")
        bhl = ar.alloc([1, 2, 256], BF16, "bhl")
        ones1 = ar.alloc([1, 128], BF16, "ones1")
        trib = ar.alloc([128, 4, 128], BF16, "trib")
        onb = ar.alloc([128, 64], F32, "onb")
        bA, bLR, bG, bK = Buf(), Buf(), Buf(), Buf()
        bYc = [Buf() for _ in range(NTILE)]
        bOst = [Buf() for _ in range(NTILE)]
        self.dma_tm('sp', A, self.Aqkv, 0, NTILE, True, [db['Aqkv']], [bA])
        for r3 in range(3):
            s.dma('sp', LRf[r3 * 32:(r3 + 1) * 32, :], self.LrT, reads=[db['LrT']], adds=[bLR])
            s.dma('sp', Wf[r3 * 32:(r3 + 1) * 32, :], self.L[l]['wdec'][0:32, :], adds=[bK])
        self.dma_tm('sp', Ga, self.Gs[:, 0:256], 0, NTILE, True, [db['Gs']], [bG])
        s.dma('sp', tri, self.tri_d, adds=[bK])
        s.dma('sp', blk, self.blk_d, adds=[bK])
        s.dma('sp', bdec, self.L[l]['wdec'][32:33, :], adds=[bK])
        s.dma('sp', onb, self.L[l]['onb'], adds=[bK])
        bLX, bWX = Buf(), Buf()
        s.op('pool', lambda e: e.memset(ones1, 1.0), adds=[bWX])
        s.op('act', lambda e: e.activation(out=LRx[0:32, :], in_=LRf[0:32, :], func=AF.Copy), reads=[bLR], adds=[bLX])
        s.op('act', lambda e: e.activation(out=LRx[64:96, :], in_=LRf[64:96, :], func=AF.Copy), reads=[bLR], adds=[bLX])
        s.op('dve', lambda e: e.tensor_copy(out=LRx[32:64, :], in_=LRf[32:64, :]), reads=[bLR], adds=[bLX])
        s.op('dve', lambda e: e.tensor_tensor(out=LRx[32:64, :], in0=LRf[32:64, :], in1=LRx[32:64, :], op=ALU.subtract),
             reads=[bLR, bLX], adds=[bLX])
        s.op('pool', lambda e: e.tensor_copy(out=Wx[0:64, :], in_=Wf[0:64, :]), reads=[bK], adds=[bWX])
        s.op('pool', lambda e: e.tensor_copy(out=Wx[64:96, :], in_=Wf[64:96, :]), reads=[bK], adds=[bWX])
        s.op('pool', lambda e: e.tensor_tensor(out=Wx[64:96, :], in0=Wf[64:96, :], in1=Wx[64:96, :], op=ALU.subtract),
             reads=[bK, bWX], adds=[bWX])
        s.op('pool', lambda e: e.tensor_copy(out=bhl[:, 0, :], in_=bdec), reads=[bK], adds=[bWX])
        s.op('pool', lambda e: e.tensor_tensor(out=bhl[:, 1, :], in0=bdec, in1=bhl[:, 0, :], op=ALU.subtract),
             reads=[bK, bWX], adds=[bWX])
        s.op('pool', lambda e: e.tensor_copy(out=trib, in_=tri), reads=[bK], adds=[bWX])
        blkm = blk[:, 0:256]
        hm = blk[:, 256:260]
        Sf = [ar.alloc([128, 256], F32, "Sf%d" % d) for d in range(2)]
        Sb = [ar.alloc([128, 256], BF16, "Sb%d" % d) for d in range(2)]
        bS = [Buf(), Buf()]
        for d in range(2):
            s.op('pool', (lambda d=d: lambda e: e.memset(Sf[d], 0.0))(), writes=[bS[d]])
            s.op('pool', (lambda d=d: lambda e: e.memset(Sb[d], 0.0))(), adds=[bS[d]])

        def rot(shape, dt, name, n=4):
            return [Rot([(ar.alloc(shape, dt, name), Buf()) for _ in range(n)]) for _ in range(2)]
        gS = rot([128, 128], F32, "gS")
        gH = rot([128, 2, 128], BF16, "gH")
        eT = rot([128, 128], F32, "eT")
        Eq = rot([128, 128], F32, "Eq")
        Ek = rot([128, 128], F32, "Ek")
        Eh = rot([128, 128], F32, "Eh")
        qt_ = rot([128, 128], BF16, "qt")
        kt_ = rot([128, 128], BF16, "kt")
        kh_ = rot([128, 128], BF16, "kh")
        Q4 = rot([128, 512], BF16, "Q4")
        Pm = rot([128, 512], BF16, "Pm")
        fin = Rot([(ar.alloc([128, 3, 256], F32, "fin"), Buf()) for _ in range(3)])
        fst = Rot([(ar.alloc([128, 64], F32, "fst"), Buf()) for _ in range(3)])
        ZG = [self.ps[0], self.ps[1]]
        TRs = Rot([self.ps[2], self.ps[7]])
        AT = [self.ps[3], self.ps[4]]
        OK = [self.ps[5], self.ps[6]]

        order = [list(range(NTILE)), [1, 0] + list(range(NTILE - 1, 1, -1))]
        pos = [{c: i for i, c in enumerate(order[d])} for d in range(2)]

        def step(c, d):
            zg, bzg = ZG[d]
            at, bat = AT[d]
            ok, bok = OK[d]
            first = pos[d][c] < pos[1 - d][c] or (pos[d][c] == pos[1 - d][c] and d == 0)
            cs_ = slice(c * 128, (c + 1) * 128)
            mi_incl, mi_tail = (0, 2) if d == 0 else (1, 3)
            s.op('pe', lambda e: e.matmul(zg[:, 0:128], lhsT=LRx[:, cs_], rhs=Wx[:, d * 128:(d + 1) * 128],
                                          start=True, stop=False), reads=[bLX, bWX], writes=[bzg])
            for hl in range(2):
                s.op('pe', (lambda hl=hl: lambda e: e.matmul(
                    zg[:, 0:128], lhsT=ones1[0:1, :], rhs=bhl[0:1, hl, d * 128:(d + 1) * 128],
                    start=False, stop=(hl == 1)))(), reads=[bWX], writes=[bzg])
            e_t, bet = eT[d].next()
            g_s, bgs = gS[d].next()
            s.op('act', lambda e: e.activation(out=e_t, in_=zg[:, 0:128], func=AF.Exp, scale=-1.0),
                 reads=[bzg], writes=[bet])
            s.op('act', lambda e: e.activation(out=g_s, in_=e_t, func=AF.Ln, bias=1.0), reads=[bet], writes=[bgs])
            g_h, bgh = gH[d].next()
            s.op('act', lambda e: e.activation(out=g_h[:, 0, :], in_=g_s, func=AF.Copy), reads=[bgs], writes=[bgh])
            s.op('pool', lambda e: e.tensor_tensor(out=g_h[:, 1, :], in0=g_s, in1=g_h[:, 0, :], op=ALU.subtract),
                 reads=[bgs, bgh], writes=[bgh])
            yield
            for hl in range(2):
                s.op('pe', (lambda hl=hl: lambda e: e.matmul(
                    zg[:, 128:256], lhsT=g_h[:, hl, :], rhs=trib[:, mi_incl, :], start=(hl == 0), stop=(hl == 1)))(),
                    reads=[bgh, bWX], writes=[bzg])
            for hl in range(2):
                s.op('pe', (lambda hl=hl: lambda e: e.matmul(
                    zg[:, 256:384], lhsT=trib[:, mi_tail, :], rhs=g_h[:, hl, :], start=(hl == 0), stop=(hl == 1)))(),
                    reads=[bgh, bWX], writes=[bzg])
            eq, beq = Eq[d].next()
            ek, bek = Ek[d].next()
            eh, beh = Eh[d].next()
            s.op('act', lambda e: e.activation(out=eq, in_=zg[:, 128:256], func=AF.Exp, scale=-1.0 / 16),
                 reads=[bzg], writes=[beq])
            s.op('act', lambda e: e.activation(out=ek, in_=zg[:, 128:256], func=AF.Exp, scale=1.0 / 16),
                 reads=[bzg], writes=[bek])
            s.op('act', lambda e: e.activation(out=eh, in_=zg[:, 256:384], func=AF.Exp, scale=-1.0 / 16),
                 reads=[bzg], writes=[beh])
            yield
            tr, btr = TRs.next()
            trb = tr.bitcast(BF16)
            s.op('pe', lambda e: e.transpose(out=trb[:, 0:128], in_=A[:, c, 0:128], identity=self.ident_b),
                 reads=[bA, self.b_const], writes=[btr])
            s.op('pe', lambda e: e.transpose(out=trb[:, 128:256], in_=A[:, c, 128:256], identity=self.ident_b),
                 reads=[bA, self.b_const], writes=[btr])
            q_t, bqt = qt_[d].next()
            k_t, bkt = kt_[d].next()
            k_h, bkh = kh_[d].next()
            s.op('pool', lambda e: e.tensor_tensor(out=k_h, in0=A[:, c, 128:256], in1=eh, op=ALU.mult),
                 reads=[bA, beh], writes=[bkh])
            s.op('dve', lambda e: e.scalar_tensor_tensor(out=q_t, in0=trb[:, 0:128], scalar=32.0 ** -0.5, in1=eq,
                                                         op0=ALU.mult, op1=ALU.mult), reads=[btr, beq], writes=[bqt])
            s.op('dve', lambda e: e.tensor_tensor(out=k_t, in0=trb[:, 128:256], in1=ek, op=ALU.mult),
                 reads=[btr, bek], writes=[bkt])
            yield
            q4, bq4 = Q4[d].next()
            s.op('pool', lambda e: e.tensor_tensor(
                out=q4.rearrange("p (h t) -> p h t", h=4), in0=bc(q_t.unsqueeze(1), [128, 4, 128]),
                in1=bc(hm.unsqueeze(2), [128, 4, 128]), op=ALU.mult), reads=[bqt, bK], writes=[bq4])
            yield
            s.op('pe', lambda e: e.matmul(at[:, 0:512], lhsT=k_t, rhs=q4, start=True, stop=True),
                 reads=[bkt, bq4], writes=[bat])
            p_m, bpm = Pm[d].next()
            s.op('dve', lambda e: e.tensor_tensor(
                out=p_m.rearrange("p (h t) -> p h t", h=4), in0=at[:, 0:512].rearrange("p (h t) -> p h t", h=4),
                in1=bc(tri[:, mi_incl, :].unsqueeze(1), [128, 4, 128]), op=ALU.mult), reads=[bat, bK], writes=[bpm])
            yield
            s.op('pe', lambda e: e.matmul(ok[:, 0:256], lhsT=q_t, rhs=Sb[d], start=True, stop=False),
                 reads=[bqt, bS[d]], writes=[bok])
            for h in range(4):
                s.op('pe', (lambda h=h: lambda e: e.matmul(
                    ok[:, h * 64:(h + 1) * 64], lhsT=p_m[:, h * 128:(h + 1) * 128],
                    rhs=A[:, c, 256 + h * 64:256 + (h + 1) * 64], start=False, stop=(h == 3)))(),
                    reads=[bpm, bA], writes=[bok])
            s.op('pe', lambda e: e.matmul(ok[:, 256:512], lhsT=k_h, rhs=A[:, c, 256:512], start=True, stop=True),
                 reads=[bkh, bA], writes=[bok])
            last = 127 if d == 0 else 0
            s.op('dve', lambda e: e.scalar_tensor_tensor(
                out=Sf[d], in0=Sf[d], scalar=eq[:, last:last + 1], in1=ok[:, 256:512], op0=ALU.mult, op1=ALU.add),
                reads=[bok, beq, bS[d]], writes=[bS[d]])
            s.op('dve', lambda e: e.tensor_tensor(out=Sb[d], in0=Sf[d], in1=blkm, op=ALU.mult),
                 reads=[bK, bS[d]], writes=[bS[d]])
            if first:
                s.op('act', lambda e: e.activation(out=Ost[:, c, :], in_=ok[:, 0:256], func=AF.Copy),
                     reads=[bok], writes=[bOst[c]])
                yield
                return
            f, bf = fin.next()
            st_, bst = fst.next()
            s.op('dve', lambda e: e.tensor_tensor(out=f[:, 0, :], in0=ok[:, 0:256], in1=Ost[:, c, :], op=ALU.add),
                 reads=[bok, bOst[c]], writes=[bf])
            yield
            if need_ctx or c >= 2:
                s.op('act', lambda e: e.activation(out=f[:, 1, :], in_=f[:, 0, :], func=AF.Square),
                     reads=[bf], writes=[bf])
                s.op('dve', lambda e: e.tensor_reduce(
                    out=st_[:, 0:4], in_=f[:, 1, :].rearrange("p (h d) -> p h d", d=64), axis=AX.X, op=ALU.add),
                    reads=[bf], writes=[bst])
                s.op('dve', lambda e: e.tensor_scalar(out=st_[:, 0:4], in0=st_[:, 0:4], scalar1=1.0 / 64, scalar2=EPS,
                                                      op0=ALU.mult, op1=ALU.add), reads=[bst], writes=[bst])
                s.op('act', lambda e: e.activation(out=st_[:, 32:36], in_=st_[:, 0:4], func=AF.Ln),
                     reads=[bst], writes=[bst])
                s.op('act', lambda e: e.activation(out=st_[:, 16:20], in_=st_[:, 32:36], func=AF.Exp, scale=-0.5),
                     reads=[bst], writes=[bst])
                s.op('pool', lambda e: e.tensor_tensor(
                    out=f[:, 2, :].rearrange("p (h d) -> p h d", d=64),
                    in0=Ga[:, c, :].rearrange("p (h d) -> p h d", d=64),
                    in1=bc(onb.unsqueeze(1), [128, 4, 64]), op=ALU.mult), reads=[bG, bK], writes=[bf])
                s.op('pool', lambda e: e.tensor_tensor(
                    out=f[:, 1, :].rearrange("p (h d) -> p h d", d=64),
                    in0=f[:, 0, :].rearrange("p (h d) -> p h d", d=64),
                    in1=bc(st_[:, 16:20].unsqueeze(2), [128, 4, 64]), op=ALU.mult), reads=[bf, bst], writes=[bf])
                s.op('pool', lambda e: e.tensor_tensor(out=Ys[:, c, :], in0=f[:, 1, :], in1=f[:, 2, :], op=ALU.mult),
                     reads=[bf], writes=[bYc[c]])

        def ystore(c0, c1):
            return store_item(lambda: self.dma_tm('sp', Ys, self.Ymix[:, 0:256], c0, c1, False,
                                                  [bYc[c] for c in range(c0, c1)], [db['Ymix']]))(7)
        gens = []
        for i in range(NTILE):
            gens.append(step(order[0][i], 0))
            gens.append(step(order[1][i], 1))
            if i == 1 and need_ctx:
                gens.append(ystore(0, 2))
            if i >= 19 and i % 2 == 1:
                gens.append(ystore(i - 1, i + 1))
                gens.append(ystore(35 - i, 37 - i))
        pipelineN(gens, 7)

    def phase_b(self, l):
        nc, s = self.nc, self.s
        ar = self.arena
        ar.reset()
        need_ctx = (l < DEPTH - 1)
        db = self.dram_bufs
        X0 = ar.alloc([64, 64, 256], BF16, "fX0")
        x3_off = ar.off
        X0p = ar.alloc([64, 128, 128], BF16, "fX0p")
        X1 = ar.alloc([128, 128, 128], BF16, "fX1")
        M3 = ar.alloc([128, 64, 2, 128], BF16, "fM3")
        D1 = ar.alloc([64, 128], BF16, "fD1")
        CH = ar.alloc([128, 8, 128], BF16, "fCH")
        RT = ar.alloc([128, 2, NT], BF16, "fRT")
        Gb = ar.alloc([128, NTILE, 256], BF16, "fG")
        Ys = ar.alloc([128, NTILE, 256], BF16, "fY")
        fwf = ar.alloc([128, 2, 256], F32, "fwf")
        fwb = ar.alloc([128, 2, 256], BF16, "fwb")
        bX0, bX1, bX3, bK, bRT, bG, bY, bFW, bX0p = Buf(), Buf(), Buf(), Buf(), Buf(), Buf(), Buf(), Buf(), Buf()
        bv_l = self.Bv[NC_:NT, :].rearrange("(r c) k -> r c k", c=64)
        for q in range(4):
            s.dma('sp', X0[:, q * 16:(q + 1) * 16, :], bv_l[:, q * 16:(q + 1) * 16, :], reads=[db['Bv']], adds=[bX0])
        s.dma('sp', D1, self.fD1_d, adds=[bK])
        for q in range(4):
            s.dma('sp', M3[:, q * 16:(q + 1) * 16], self.fM3_d[:, q * 16:(q + 1) * 16], adds=[bK])
        s.dma('sp', CH, self.fCH_d, adds=[bK])
        s.dma('sp', fwf, self.L[l]['fw'].rearrange("(a p) n -> p a n", p=128), adds=[bFW])
        s.op('pool', lambda e: e.tensor_copy(out=fwb, in_=fwf), reads=[bFW], writes=[bFW])
        self.dma_tm('sp', Gb, self.Gs[:, 256:512], 0, NTILE, True, [db['Gs']], [bG])
        PS = Rot([self.ps[i] for i in range(8)])
        bYg = [Buf() for _ in range(NTILE)]
        evr = [0]

        def evac(out_ap, in_ap, rb, wb, add=False, extra=()):
            eng = 'act' if evr[0] % 2 == 0 else 'dve'
            evr[0] += 1
            kw = dict(adds=[wb] + list(extra)) if add else dict(writes=[wb])
            if eng == 'act':
                s.op('act', lambda e: e.activation(out=out_ap, in_=in_ap, func=AF.Copy), reads=[rb], **kw)
            else:
                s.op('dve', lambda e: e.tensor_copy(out=out_ap, in_=in_ap), reads=[rb], **kw)

        X0v = X0.rearrange("r c (q t) -> r q t c", t=2)
        X0pv = X0p.rearrange("r q (t c) -> r q t c", t=2)
        for qi, eng in enumerate(('dve', 'act', 'dve', 'act')):
            sl = slice(qi * 32, (qi + 1) * 32)
            if eng == 'act':
                s.op('act', (lambda sl=sl: lambda e: e.activation(out=X0pv[:, sl], in_=X0v[:, sl], func=AF.Copy))(),
                     reads=[bX0], adds=[bX0p])
            else:
                s.op(eng, (lambda sl=sl: lambda e: e.tensor_copy(out=X0pv[:, sl], in_=X0v[:, sl]))(),
                     reads=[bX0], adds=[bX0p])
        for q4 in range(32):
            bk, bbk = PS.next()
            for i in range(4):
                chp = q4 * 4 + i
                s.op('pe', (lambda chp=chp, i=i, bk=bk: lambda e: e.matmul(
                    bk[:, i * 128:(i + 1) * 128], lhsT=X0p[:, chp, :], rhs=D1, start=True, stop=True))(),
                    reads=[bX0p, bK], writes=[bbk])
            evac(X1[:, q4 * 4:(q4 + 1) * 4, :], bk[:, 0:512].rearrange("p (a n) -> p a n", a=4), bbk, bX1, add=True)
        X3 = ar.view([128, 64, 2, 128], BF16, x3_off - 64 * 256 * 2)
        assert x3_off - 64 * 256 * 2 == self.arena.base
        X1v = X1.rearrange("p q (k z) -> p k z q", z=2)
        for k4 in range(16):
            bkA, bbA = PS.next()
            bkB, bbB = PS.next()
            for i in range(4):
                k1 = k4 * 4 + i
                for z in range(2):
                    for ch2, (bk, bbk) in enumerate(((bkA, bbA), (bkB, bbB))):
                        ps_ = slice(ch2 * 64, (ch2 + 1) * 64)
                        s.op('pe', (lambda k1=k1, z=z, ps_=ps_, bk=bk, i=i: lambda e: e.matmul(
                            bk[:, i * 128:(i + 1) * 128], lhsT=X1v[ps_, k1, z, :], rhs=M3[ps_, k1, z, :],
                            start=(z == 0), stop=(z == 1)))(), reads=[bX1, bK], writes=[bbk])
            for ch2, (bk, bbk) in enumerate(((bkA, bbA), (bkB, bbB))):
                evac(X3[:, k4 * 4:(k4 + 1) * 4, ch2, :], bk[:, 0:512].rearrange("p (a n) -> p a n", a=4),
                     bbk, bX3, add=True, extra=[bX0])
        X3v = X3.rearrange("p k t (j z) -> p t z k j", z=2)
        RTl = RT[:, :, NC_:NT].rearrange("p a (j k) -> p a k j", k=64)
        for mc in range(2):
            for kb in range(8):
                bk, bbk = PS.next()
                n = 0
                for ch2 in range(2):
                    for z in range(2):
                        s.op('pe', (lambda mc=mc, kb=kb, ch2=ch2, z=z, n=n, bk=bk: lambda e: e.matmul(
                            bk[:, 0:512], lhsT=CH[:, mc * 4 + ch2 * 2 + z, :],
                            rhs=X3v[:, ch2, z, kb * 8:(kb + 1) * 8, :], start=(n == 0), stop=(n == 3)))(),
                            reads=[bX3, bK], writes=[bbk])
                        n += 1
                evac(RTl[:, mc, kb * 8:(kb + 1) * 8, :], bk[:, 0:512].rearrange("p (k j) -> p k j", k=8),
                     bbk, bRT, add=True)
        if need_ctx:
            Vc = ar.alloc([128, 2, 256], BF16, "fVc")
            DC = ar.alloc([128, 2, 512], BF16, "fDC")
            CHc = ar.alloc([128, 2, 128], BF16, "fCHc")
            X3c = ar.alloc([128, 2, 512], BF16, "fX3c")
            bC, bX3c = Buf(), Buf()
            s.dma('sp', Vc, self.Bv[0:NC_, :].rearrange("(i p) k -> p i k", p=128), reads=[db['Bv']], adds=[bC])
            s.dma('sp', DC, self.fDC_d, adds=[bC])
            s.dma('sp', CHc, self.fCHc_d, adds=[bC])
            for cc in range(2):
                bk, bbk = PS.next()
                for i in range(2):
                    s.op('pe', (lambda cc=cc, i=i, bk=bk: lambda e: e.matmul(
                        bk[:, 0:512], lhsT=Vc[:, i, cc * 128:(cc + 1) * 128], rhs=DC[:, i, :],
                        start=(i == 0), stop=(i == 1)))(), reads=[bC], writes=[bbk])
                evac(X3c[:, cc, :], bk[:, 0:512], bbk, bX3c, add=True)
            X3cv = X3c.rearrange("p a (k z) -> p a z k", z=2)
            for cc in range(2):
                bk, bbk = PS.next()
                for z in range(2):
                    s.op('pe', (lambda cc=cc, z=z, bk=bk: lambda e: e.matmul(
                        bk[:, 0:256], lhsT=CHc[:, z, :], rhs=X3cv[:, cc, z, :], start=(z == 0), stop=(z == 1)))(),
                        reads=[bX3c, bC], writes=[bbk])
                evac(RT[:, cc, 0:NC_], bk[:, 0:256], bbk, bRT, add=True)
        t0 = 0 if need_ctx else 2
        for ti in range(t0, NTILE):
            bk, bbk = PS.next()
            for mc in range(2):
                s.op('pe', (lambda ti=ti, mc=mc, bk=bk: lambda e: e.matmul(
                    bk[:, 0:256], lhsT=RT[:, mc, ti * 128:(ti + 1) * 128], rhs=fwb[:, mc, :],
                    start=(mc == 0), stop=(mc == 1)))(), reads=[bRT, bFW], writes=[bbk])
            s.op('dve', (lambda ti=ti, bk=bk: lambda e: e.tensor_tensor(
                out=Ys[:, ti, :], in0=bk[:, 0:256], in1=Gb[:, ti, :], op=ALU.mult))(), reads=[bbk, bG],
                adds=[bYg[ti // 4]])
            if ti % 4 == 3 or ti == NTILE - 1:
                self.dma_tm('sp', Ys, self.Ymix[:, 256:512], max(t0, (ti // 4) * 4), ti + 1, False,
                            [bYg[ti // 4]], [db['Ymix']])

    def phase_o(self, l):
        nc, s = self.nc, self.s
        ar = self.arena
        ar.reset()
        last = (l == DEPTH - 1)
        db = self.dram_bufs
        modT, bmod = self.modT[l], self.b_mod[l]
        nmod = 1 if last else 2
        gcol = ar.alloc([128, 8, 2, 128], F32, "gcol")
        GB = ar.alloc([128, 2, D], F32, "GB")
        Wo = [ar.alloc([128, 8, D], BF16, "Wo%d" % i) for i in range(nmod)]
        bgc, bGB, bWo = Buf(), Buf(), Buf()
        for i in range(nmod):
            s.op('dve', (lambda i=i: lambda e: e.tensor_copy(
                out=gcol[:, :, i, :], in_=bc(modT[:, 16:24, i:i + 1], [128, 8, 128])))(), reads=[bmod], adds=[bgc])
        for i in range(nmod):
            for hf in range(2):
                bk, bbk = self.ps[i * 2 + hf]
                for kk in range(4):
                    k = hf * 4 + kk
                    s.op('pe', (lambda i=i, k=k, kk=kk, bk=bk: lambda e: e.matmul(
                        bk[:, kk * 128:(kk + 1) * 128], lhsT=gcol[:, k, i, :], rhs=self.ident_f,
                        start=True, stop=True))(), reads=[bgc, self.b_const], writes=[bbk])
                s.op('act', (lambda i=i, hf=hf, bk=bk: lambda e: e.activation(
                    out=GB[:, i, hf * 512:(hf + 1) * 512], in_=bk[:, 0:512], func=AF.Copy))(),
                    reads=[bbk], adds=[bGB])
        wst = Rot([(ar.alloc([128, D], F32, "wst"), Buf()) for _ in range(4)])
        for mk in range(8):
            w, bw = wst.next()
            s.dma('sp', w, self.L[l]['wo'][mk * 128:(mk + 1) * 128, :], writes=[bw])
            for i in range(nmod):
                eng = 'dve' if i == 0 else 'pool'
                s.op(eng, (lambda i=i, mk=mk, w=w: lambda e: e.tensor_tensor(
                    out=Wo[i][:, mk, :], in0=w, in1=GB[:, i, :], op=ALU.mult))(), reads=[bw, bGB], adds=[bWo])
        yt = Rot([(ar.alloc([128, 2, D], BF16, "yt"), Buf()) for _ in range(4)])
        xt = Rot([(ar.alloc([128, 2, D], F32, "xt"), Buf()) for _ in range(4)])
        yT = Rot([(ar.alloc([128, 8, 128], BF16, "yT"), Buf()) for _ in range(4)])
        xo = Rot([(ar.alloc([128, 2, D], F32, "xo"), Buf()) for _ in range(4)])
        TPs = Rot([self.ps[4], self.ps[5]])
        MMs = Rot([self.ps[0], self.ps[1], self.ps[2], self.ps[3], self.ps[6], self.ps[7]])
        src = self.xin if l == 0 else self.X1
        bsrc = Buf() if l == 0 else db['X1']
        t0 = 2 if last else 0

        def tile2(ti):
            u0 = ti * 128
            wi = 0 if ti >= 2 else 1
            y_t, byt = yt.next()
            x_t, bxt = xt.next()
            tm = lambda dr: dr[u0:u0 + 256, :].rearrange("(i p) c -> p i c", p=128)
            s.dma('sp', y_t, tm(self.Ymix), reads=[db['Ymix']], writes=[byt])
            s.dma('sp', x_t, tm(src), reads=[bsrc], writes=[bxt])
            yTs = []
            for i in range(2):
                tp, btp = TPs.next()
                tpb = tp.bitcast(BF16)
                for k in range(8):
                    s.op('pe', (lambda k=k, i=i, tpb=tpb: lambda e: e.transpose(
                        out=tpb[:, k * 128:(k + 1) * 128], in_=y_t[:, i, k * 128:(k + 1) * 128],
                        identity=self.ident_b))(), reads=[byt, self.b_const], writes=[btp])
                y_T, byT = yT.next()
                s.op('act', (lambda y_T=y_T, tpb=tpb: lambda e: e.activation(
                    out=y_T.rearrange("p k t -> p (k t)"), in_=tpb[:, 0:1024], func=AF.Copy))(),
                    reads=[btp], writes=[byT])
                yTs.append((y_T, byT))
            yield
            x_o, bxo = xo.next()
            for i in range(2):
                y_T, byT = yTs[i]
                for nc_ in range(2):
                    bk, bbk = MMs.next()
                    for k in range(8):
                        s.op('pe', (lambda k=k, nc_=nc_, bk=bk, y_T=y_T: lambda e: e.matmul(
                            bk[:, 0:512], lhsT=y_T[:, k, :], rhs=Wo[wi][:, k, nc_ * 512:(nc_ + 1) * 512],
                            start=(k == 0), stop=(k == 7)))(), reads=[byT, bWo], writes=[bbk])
                    s.op('dve', (lambda nc_=nc_, bk=bk, i=i: lambda e: e.tensor_tensor(
                        out=x_o[:, i, nc_ * 512:(nc_ + 1) * 512], in0=bk[:, 0:512],
                        in1=x_t[:, i, nc_ * 512:(nc_ + 1) * 512], op=ALU.add))(), reads=[bbk, bxt], adds=[bxo])
            if last:
                s.dma('pool', self.out[u0 - NC_:u0 - NC_ + 256, :].rearrange("(i p) c -> p i c", p=128), x_o,
                      reads=[bxo], adds=[self.b_out])
            else:
                s.dma('pool', self.X1[u0:u0 + 256, :].rearrange("(i p) c -> p i c", p=128), x_o,
                      reads=[bxo], adds=[db['X1']])

        pipeline2([tile2(ti) for ti in range(t0, NTILE, 2)])

    def rsqrt1(self, v, y, t1, buf):
        s = self.s
        I32 = mybir.dt.int32
        vi, yi, t1i = v.bitcast(I32), y.bitcast(I32), t1.bitcast(I32)
        s.op('dve', lambda e: e.tensor_single_scalar(out=t1i, in_=vi, scalar=1, op=ALU.arith_shift_right),
             reads=[buf], writes=[buf])
        s.op('dve', lambda e: e.tensor_scalar(out=yi, in0=t1i, scalar1=-1.0, scalar2=1597463007.0,
                                              op0=ALU.mult, op1=ALU.add), reads=[buf], writes=[buf])
        for _ in range(2):
            s.op('dve', lambda e: e.scalar_tensor_tensor(out=t1, in0=y, scalar=v, in1=y, op0=ALU.mult, op1=ALU.mult),
                 reads=[buf], writes=[buf])
            s.op('dve', lambda e: e.tensor_scalar(out=t1, in0=t1, scalar1=-0.5, scalar2=1.5,
                                                  op0=ALU.mult, op1=ALU.add), reads=[buf], writes=[buf])
            s.op('dve', lambda e: e.tensor_tensor(out=y, in0=y, in1=t1, op=ALU.mult), reads=[buf], writes=[buf])

    def rsqrt(self, v, y, t1, t2, buf):
        s = self.s
        I32 = mybir.dt.int32
        vi, yi, t1i = v.bitcast(I32), y.bitcast(I32), t1.bitcast(I32)
        s.op('dve', lambda e: e.tensor_single_scalar(out=t1i, in_=vi, scalar=1, op=ALU.arith_shift_right),
             reads=[buf], writes=[buf])
        s.op('dve', lambda e: e.tensor_scalar(out=yi, in0=t1i, scalar1=-1.0, scalar2=1597463007.0,
                                              op0=ALU.mult, op1=ALU.add), reads=[buf], writes=[buf])
        for _ in range(2):
            s.op('dve', lambda e: e.tensor_tensor(out=t1, in0=v, in1=y, op=ALU.mult), reads=[buf], writes=[buf])
            s.op('dve', lambda e: e.tensor_tensor(out=t2, in0=t1, in1=y, op=ALU.mult), reads=[buf], writes=[buf])
            s.op('dve', lambda e: e.tensor_scalar(out=t2, in0=t2, scalar1=-0.5, scalar2=1.5,
                                                  op0=ALU.mult, op1=ALU.add), reads=[buf], writes=[buf])
            s.op('dve', lambda e: e.tensor_tensor(out=y, in0=y, in1=t2, op=ALU.mult), reads=[buf], writes=[buf])

    def _norm_heads(self, pm, bpm, nh, wt, b_w, sq, nt, st8, out_f32, rot_out=None, out=None):
        s = self.s
        w = nh * 64
        sq_t, bsq = sq.next()
        s.op('act', lambda e: e.activation(out=sq_t[:, 0:w], in_=pm[:, 0:w], func=AF.Square),
             reads=[bpm], writes=[bsq])
        s8, bs8 = st8.next()
        s.op('dve', lambda e: e.tensor_reduce(out=s8[:, 0:nh], in_=sq_t[:, 0:w].rearrange("p (h d) -> p h d", d=64),
                                              axis=AX.X, op=ALU.add), reads=[bsq], writes=[bs8])
        s.op('dve', lambda e: e.tensor_scalar(out=s8[:, 0:nh], in0=s8[:, 0:nh], scalar1=1.0 / 64, scalar2=EPS,
                                              op0=ALU.mult, op1=ALU.add), reads=[bs8], writes=[bs8])
        s.op('dve', lambda e: e.tensor_scalar(out=s8[:, 8:8 + nh], in0=s8[:, 0:nh], scalar1=-0.5, scalar2=None,
                                              op0=ALU.pow), reads=[bs8], writes=[bs8])
        n_t, bnt = nt.next()
        s.op('dve', lambda e: e.tensor_tensor(
            out=n_t[:, 0:w].rearrange("p (h d) -> p h d", d=64), in0=pm[:, 0:w].rearrange("p (h d) -> p h d", d=64),
            in1=bc(s8[:, 8:8 + nh].unsqueeze(2), [128, nh, 64]), op=ALU.mult), reads=[bpm, bs8], writes=[bnt])
        if out_f32:
            o, bo = rot_out.next()
        else:
            o, bo = out
        s.op('pool', lambda e: e.tensor_tensor(out=o[:, 0:w], in0=n_t[:, 0:w], in1=wt[:, 0:w], op=ALU.mult),
             reads=[bnt, b_w], writes=[bo])
        self._last_norm = (o, bo)

    @property
    def xin_l(self):
        return self.xin


def _col_perm():
    off = {}
    o = 0
    for name, w in (("a_q", 128), ("a_k", 128), ("a_v", 256), ("a_g", 256), ("a_lr", 32), ("b_v", 256),
                    ("b_g", 256), ("c_q", 256), ("c_k", 128), ("c_v", 128), ("c_g", 256), ("d_q", 256),
                    ("d_k", 256), ("d_v", 256), ("d_g", 256)):
        off[name] = (o, w)
        o += w
    order = ["a_q", "a_k", "a_v", "a_g", "b_g", "c_g", "d_g", "c_q", "c_k", "c_v", "d_q", "d_k", "d_v", "b_v", "a_lr"]
    perm = []
    for nm in order:
        a, w = off[nm]
        perm.extend(range(a, a + w))
    return np.array(perm)


def _rope_table():
    t = np.arange(NL)
    row = (t // 64).astype(np.float32)
    col = (t % 64).astype(np.float32)
    inv = (10000.0 ** (-np.arange(0, 32, 2, dtype=np.float32) / 32)).astype(np.float32)
    ang = np.stack([row[:, None] * inv, col[:, None] * inv], axis=1)
    cs = np.concatenate([np.cos(ang).reshape(NL, 32), np.sin(ang).reshape(NL, 32)], axis=1)
    return cs.astype(np.float32)


def na_geometry():
    r = np.arange(64)
    row_start = np.clip(r - 4, 0, 56)
    col_start = np.clip(r - 8, 0, 48)
    kk = np.arange(128)
    ka, kc = (kk // 64)[:, None], (kk % 64)[:, None]
    qa, qc = (kk // 64)[None, :], (kk % 64)[None, :]
    vcol = (kc >= col_start[qc]) & (kc < col_start[qc] + 16)
    dx = np.clip(kc - qc, -15, 15) + 15
    sigs, table, nbrs, mats = {}, {}, [], []
    for n in range(32):
        nb = []
        rr = 2 * n + qa
        rs = row_start[rr]
        for m in range(32):
            a = 2 * m + ka
            valid = (a >= rs) & (a < rs + 8) & vcol
            if not valid.any():
                continue
            dy = np.where(valid, a - rr + 7, 0)
            sig = (valid.tobytes(), dy.tobytes())
            if sig not in sigs:
                sigs[sig] = len(mats)
                mats.append((valid, dy))
            table[(n, m)] = sigs[sig]
            nb.append(m)
        nbrs.append(nb)
    return dict(ntypes=len(mats), table=table, nbrs=nbrs, mats=mats, dx=dx)


def na_bias_mats(rel_bias, G):
    out = np.empty((128, 4 * G['ntypes'], 128), dtype=np.float32)
    for h in range(4):
        for t, (valid, dy) in enumerate(G['mats']):
            out[:, h * G['ntypes'] + t, :] = np.where(valid, rel_bias[h][dy, G['dx']], NEG)
    return out.astype(ml_dtypes.bfloat16)


def fnet_consts():
    bf = ml_dtypes.bfloat16
    r = np.arange(64, dtype=np.float64)
    k1 = np.arange(64, dtype=np.float64)
    a = 2 * np.pi * np.outer(r, k1) / 64.0
    D1 = np.stack([np.cos(a), -np.sin(a)], axis=2).reshape(64, 128)
    c = np.arange(64, dtype=np.float64)[:, None, None]
    kk1 = np.arange(64, dtype=np.float64)[None, :, None]
    k2 = np.arange(64, dtype=np.float64)[None, None, :]
    th = 2 * np.pi * c * (kk1 + 64 * k2) / 4096.0
    m3r, m3i = np.cos(th) / 512.0, -np.sin(th) / 512.0
    ra = np.stack([m3r, m3i], axis=3)
    rb = np.stack([-m3i, m3r], axis=3)
    M3 = np.stack([ra, rb], axis=2).reshape(64, 64, 2, 128)
    M3 = np.concatenate([M3, M3], axis=0)
    j = np.arange(64, dtype=np.float64)
    ph = 2 * np.pi * np.outer(j, j) / 64.0
    C, S = np.cos(ph), np.sin(ph)
    CH = np.zeros((128, 2, 2, 2, 2, 64))
    for chp in range(128):
        g, jj = chp // 32, chp % 32
        for ch2 in range(2):
            CH[chp, g // 2, ch2, 0, g % 2, :] = C[2 * jj + ch2]
            CH[chp, g // 2, ch2, 1, g % 2, :] = S[2 * jj + ch2]
    CH = CH.reshape(128, 8, 128)
    t = np.arange(256, dtype=np.float64)
    ac = 2 * np.pi * np.outer(t, t) / 256.0
    DC = np.stack([np.cos(ac), -np.sin(ac)], axis=2).reshape(2, 128, 512).transpose(1, 0, 2) / 128.0
    CHc = np.zeros((128, 2, 2, 64))
    for p_ in range(128):
        CHc[p_, 0, p_ // 64, :] = C[p_ % 64]
        CHc[p_, 1, p_ // 64, :] = S[p_ % 64]
    CHc = CHc.reshape(128, 2, 128)
    return dict(fD1=D1.astype(bf), fM3=M3.astype(bf), fCH=CH.astype(bf), fDC=np.ascontiguousarray(DC).astype(bf),
                fCHc=CHc.astype(bf))


_FC = {}


def make_core_inputs(b, inp, depth=DEPTH):
    f = lambda a: np.ascontiguousarray(np.asarray(a, dtype=np.float32))
    m = {}
    m["xin"] = f(np.concatenate([np.asarray(inp["ctx"][b]), np.asarray(inp["x"][b])], axis=0))
    cv = np.stack([np.asarray(inp["c"][b]).reshape(8, 128).T, np.asarray(inp["c_ctx"]).reshape(8, 128).T], axis=2)
    m["cvec"] = f(cv)
    m["ident_f"] = np.eye(128, dtype=np.float32)
    m["ident_b"] = np.eye(128).astype(ml_dtypes.bfloat16)
    m["cs_tab"] = _rope_table()
    jj, ii = np.arange(128)[:, None], np.arange(128)[None, :]
    m["cmask"] = np.stack([(ii <= jj), (jj <= ii)], axis=1).astype(ml_dtypes.bfloat16)
    G = na_geometry()
    if not _FC:
        _FC.update(fnet_consts())
    m.update(_FC)
    ss, tt = np.arange(128)[:, None], np.arange(128)[None, :]
    m["tri"] = np.stack([ss <= tt, ss >= tt, ss > tt, ss < tt], axis=1).astype(np.float32)
    hd = np.arange(128)[:, None] // 32
    m["blkmask"] = np.concatenate([(hd == (np.arange(256)[None, :] // 64)), (hd == np.arange(4)[None, :])],
                                  axis=1).astype(np.float32)
    perm = _col_perm()
    for l in range(depth):
        m["ada_w%d" % l] = f(inp["ada_w"][l])
        m["ada_brow%d" % l] = f(np.broadcast_to(np.asarray(inp["ada_b"][l])[None, :], (2, 3 * D)))
        m["norm_wT%d" % l] = f(np.asarray(inp["norm_w"][l]).reshape(8, 128).T)
        m["wp%d" % l] = f(np.asarray(inp["w_in"][l])[:, perm])
        qn, kn = np.asarray(inp["swa_q_norm"][l]), np.asarray(inp["swa_k_norm"][l])
        m["wc%d" % l] = f(np.broadcast_to(np.concatenate([np.tile(qn, 4), np.tile(kn, 2)])[None, :], (128, 384)))
        qn, kn = np.asarray(inp["na_q_norm"][l]), np.asarray(inp["na_k_norm"][l])
        m["wd%d" % l] = f(np.broadcast_to(np.concatenate([np.tile(qn, 4), np.tile(kn, 4)])[None, :], (128, 512)))
        wdec = np.zeros((33, 256), np.float32)
        wdec[0:16, 0:128] = np.asarray(inp["gla_dec_w"][l][0])
        wdec[16:32, 128:256] = np.asarray(inp["gla_dec_w"][l][1])
        wdec[32, :] = np.asarray(inp["gla_dec_b"][l]).reshape(256)
        m["wdec%d" % l] = wdec
        m["fw%d" % l] = f(inp["fnet_w"][l])
        m["wo%d" % l] = f(inp["w_out"][l])
        m["onb%d" % l] = f(np.broadcast_to(np.asarray(inp["gla_out_norm"][l])[None, :], (128, 64)))
        m["sinkb%d" % l] = f(np.broadcast_to(np.asarray(inp["swa_sink"][l])[None, :], (128, 4)))
        m["nab%d" % l] = na_bias_mats(np.asarray(inp["na_rel_bias"][l], dtype=np.float32), G)
    return m


_CACHE = {}


def kernel(**inputs):
    if "nc" not in _CACHE:
        _CACHE["nc"] = Builder().build()
    nc = _CACHE["nc"]
    ncores = int(os.environ.get("K_NCORES", "8"))
    in_maps = [make_core_inputs(i % 4, inputs) for i in range(ncores)]
    res = run_bass_kernel_spmd(nc, in_maps, core_ids=list(range(ncores)))
    out = np.stack([np.asarray(res.results[b]["out"], dtype=np.float32) for b in range(4)], axis=0)
    return out
```

```python
import os
import numpy as np
import ml_dtypes
import concourse.bass as bass
import concourse.mybir as mybir
from concourse.bass_utils import run_bass_kernel_spmd

F32 = mybir.dt.float32
BF16 = mybir.dt.bfloat16
AF = mybir.ActivationFunctionType
ALU = mybir.AluOpType
AX = mybir.AxisListType

D = 1024
NL = 4096
NC_ = 256
NT = NL + NC_
NTILE = NT // 128
DEPTH = 2
EPS = 1e-6
PW = 3104
GROUPS = [(0, 2)] + [(2 + 4 * i, 4) for i in range(8)]
NEG = -30000.0

ENGS = ['pe', 'act', 'dve', 'pool', 'sp']
NDMA = 40


class Buf:
    __slots__ = ('w', 'wa', 'r', 'excl')

    def __init__(self, excl=False):
        self.w = {}
        self.wa = {}
        self.r = {}
        self.excl = excl


MAXOUT = int(os.environ.get('KS_MAXOUT', '6'))
NOADDS = os.environ.get('KS_NOADDS', '0') == '1'


class Sched:
    def __init__(self):
        self.prog = {e: [] for e in ENGS}
        self.cnt = {e: 0 for e in ENGS}
        self.waited = {e: {} for e in ENGS}
        self.dma_val = [0] * NDMA
        self.dma_rr = 0
        self.dma_rr2 = 0
        self.outst = {e: [] for e in ENGS}

    def _waits(self, eng, deps):
        for key, val in deps:
            if key == 'pe' and eng == 'pe':
                continue
            if self.waited[eng].get(key, 0) >= val:
                continue
            self.waited[eng][key] = val
            self.prog[eng].append(('wait', key, val))

    @staticmethod
    def _deps(reads, writes, adds=()):
        deps = []
        for b in reads:
            deps.extend(b.w.items())
            deps.extend(b.wa.items())
        for b in writes:
            deps.extend(b.w.items())
            deps.extend(b.wa.items())
            deps.extend(b.r.items())
        for b in adds:
            deps.extend(b.w.items())
            deps.extend(b.r.items())
        return deps

    @staticmethod
    def _mark(tok, reads, writes, adds=()):
        for b in reads:
            if b.r.get(tok[0], 0) < tok[1]:
                b.r[tok[0]] = tok[1]
        for b in writes:
            b.w = {tok[0]: tok[1]}
            b.wa = {}
            b.r = {}
        for b in adds:
            if b.wa.get(tok[0], 0) < tok[1]:
                b.wa[tok[0]] = tok[1]

    def op(self, eng, fn, reads=(), writes=(), adds=()):
        if NOADDS:
            writes, adds = list(writes) + list(adds), ()
        ex = [b for b in reads if b.excl]
        if ex:
            reads = [b for b in reads if not b.excl]
            writes = list(writes) + ex
        self._waits(eng, self._deps(reads, writes, adds))
        self.cnt[eng] += 1
        tok = (eng, self.cnt[eng])
        self.prog[eng].append(('op', fn, eng))
        self._mark(tok, reads, writes, adds)
        return tok

    def dma(self, q, out_ap, in_ap, reads=(), writes=(), adds=()):
        if NOADDS:
            writes, adds = list(writes) + list(adds), ()
        half = NDMA // 2
        if q == 'sp':
            i = self.dma_rr % half
            self.dma_rr += 1
        else:
            i = half + self.dma_rr2 % half
            self.dma_rr2 += 1
        key = 'd%d' % i
        deps = self._deps(reads, writes, adds)
        if self.dma_val[i] > 0:
            deps.append((key, self.dma_val[i]))
        if len(self.outst[q]) >= (MAXOUT if q == 'sp' else 10):
            deps.append(self.outst[q].pop(0))
        self._waits(q, deps)
        self.dma_val[i] += 16
        tok = (key, self.dma_val[i])
        self.outst[q].append(tok)
        self.prog[q].append(('dma', out_ap, in_ap, key))
        self._mark(tok, reads, writes, adds)
        return tok

    def barrier(self):
        for e in ENGS:
            deps = [(e2, self.cnt[e2]) for e2 in ENGS if e2 != e and self.cnt[e2] > 0]
            deps += [('d%d' % i, v) for i, v in enumerate(self.dma_val) if v > 0]
            self._waits(e, deps)

    def emit(self, nc, sems):
        if os.environ.get("K_MULTIBLOCK", "0") == "1":
            return self.flush(nc, sems)
        self.barrier()

    def flush(self, nc, sems):
        self.barrier()
        prog = self.prog
        self.prog = {e: [] for e in ENGS}

        def mk(e):
            def f(engobj):
                for it in prog[e]:
                    if it[0] == 'wait':
                        engobj.wait_ge(sems[it[1]], it[2])
                    elif it[0] == 'op':
                        it[1](engobj).then_inc(sems[it[2]], 1)
                    else:
                        engobj.dma_start(out=it[1], in_=it[2]).then_inc(sems[it[3]], 16)
            return f

        with nc.Block() as block:
            block.tensor(mk('pe'))
            block.scalar(mk('act'))
            block.vector(mk('dve'))
            block.gpsimd(mk('pool'))
            block.sync(mk('sp'))


class Arena:
    _bases = {}

    def __init__(self, nc, base, limit):
        self.nc = nc
        self.base = base
        self.limit = limit
        self.off = base
        key = (id(nc), base, limit)
        if key not in Arena._bases:
            t = nc.alloc_sbuf_tensor_at("arena_%d" % base, [128, (limit - base) // 2], BF16, offset=base)
            Arena._bases[key] = t.ap()
        self.ap = Arena._bases[key]

    def reset(self):
        self.off = self.base

    def view(self, shape, dtype, off):
        esz = 4 if dtype == F32 else 2
        n = int(np.prod(shape[1:]))
        o2 = (off - self.base) // 2
        v = self.ap[0:shape[0], o2:o2 + n * esz // 2]
        if dtype != BF16:
            v = v.bitcast(dtype)
        if len(shape) > 2:
            names = " ".join("d%d" % i for i in range(len(shape) - 1))
            kw = {"d%d" % i: shape[i + 1] for i in range(len(shape) - 1)}
            v = v.rearrange("p (%s) -> p %s" % (names, names), **kw)
        return v

    def alloc(self, shape, dtype, name=None):
        esz = 4 if dtype == F32 else 2
        nbytes = int(np.prod(shape[1:])) * esz
        nbytes = (nbytes + 63) // 64 * 64
        assert self.off + nbytes <= self.limit, ("SBUF arena overflow", self.off, nbytes, self.limit)
        v = self.view(shape, dtype, self.off)
        self.off += nbytes
        return v


class Rot:
    def __init__(self, items):
        self.items = items
        self.i = 0

    def next(self):
        it = self.items[self.i % len(self.items)]
        self.i += 1
        return it


def pipeline2(gens):
    prev = None
    for g in gens:
        next(g)
        if prev is not None:
            for _ in prev:
                pass
        prev = g
    if prev is not None:
        for _ in prev:
            pass


def pipelineN(gens, nstage):
    n = len(gens)
    for t in range(n + nstage - 1):
        for k in range(nstage):
            i = t - k
            if 0 <= i < n:
                try:
                    next(gens[i])
                except StopIteration:
                    assert k == nstage - 1, (k, nstage)
                else:
                    assert k < nstage - 1, (k, nstage)


def store_item(fn):
    def g(nstage):
        for _ in range(nstage - 1):
            yield
        fn()
    return g


def bc(ap, shape):
    return ap.broadcast_to(list(shape))


class Builder:
    def __init__(self, debug=False, depth=DEPTH, stop_after=None):
        self.debug = debug
        self.depth = depth
        self.stop_after = stop_after
        nc = bass.Bass("TRN2", target_bir_lowering=False)
        self.nc = nc
        self.s = Sched()
        self.dbg_outs = []
        self.dram_bufs = {}

    def dma_tm(self, q, sb, dr, t0, t1, load, reads, writes, step=4):
        for a in range(t0, t1, step):
            b = min(a + step, t1)
            d = dr[a * 128:b * 128, :].rearrange("(i p) c -> p i c", p=128)
            sview = sb[:, a:b, :]
            if load:
                self.s.dma(q, sview, d, reads=reads, adds=writes)
            else:
                self.s.dma(q, d, sview, reads=reads, adds=writes)

    def din(self, name, shape, dtype=F32):
        return self.nc.dram_tensor(name, list(shape), dtype, kind="ExternalInput").ap()

    def dscr(self, name, shape, dtype):
        kind = "ExternalOutput" if self.debug else "Internal"
        t = self.nc.dram_tensor(name, list(shape), dtype, kind=kind).ap()
        if self.debug:
            self.dbg_outs.append(name)
        self.dram_bufs[name] = Buf()
        return t

    def build(self):
        nc = self.nc
        s = self.s
        self.xin = self.din("xin", [NT, D])
        self.cvec = self.din("cvec", [128, 8, 2])
        self.ident_f_d = self.din("ident_f", [128, 128])
        self.ident_b_d = self.din("ident_b", [128, 128], BF16)
        self.cs_d = self.din("cs_tab", [NL, 64])
        self.L = []
        for l in range(self.depth):
            Lw = dict(
                ada_w=self.din("ada_w%d" % l, [D, 3 * D]),
                ada_brow=self.din("ada_brow%d" % l, [2, 3 * D]),
                norm_wT=self.din("norm_wT%d" % l, [128, 8]),
                wp=self.din("wp%d" % l, [D, PW]),
                wc=self.din("wc%d" % l, [128, 384]),
                wd=self.din("wd%d" % l, [128, 512]),
            )
            self.L.append(Lw)
        self.out = self.nc.dram_tensor("out", [NL, D], F32, kind="ExternalOutput").ap()
        self.Gs = self.dscr("Gs", [NT, 1024], BF16)
        self.Aqkv = self.dscr("Aqkv", [NT, 512], BF16)
        self.LrT = self.dscr("LrT", [32, NT], F32)
        self.CT = self.dscr("CT", [4, 128, NT], BF16)
        self.CV1 = self.dscr("CV1", [NT, 132], BF16)
        self.DT = self.dscr("DT", [4, 128, NT], BF16)
        self.DV1 = self.dscr("DV1", [NT, 264], BF16)
        self.Bv = self.dscr("Bv", [NT, 256], BF16)
        self.NAG = na_geometry()
        if self.debug:
            self.modT_d = self.dscr("modT_dbg", [128, 48], F32)
        if self.stop_after == 'p12':
            return self._build_rest()
        self.Ymix = self.dscr("Ymix", [NT, 1024], BF16)
        self.X1 = self.dscr("X1", [NT, D], F32)
        self.cmask_d = self.din("cmask", [128, 2, 128], BF16)
        self.tri_d = self.din("tri", [128, 4, 128])
        self.fD1_d = self.din("fD1", [64, 128], BF16)
        self.fM3_d = self.din("fM3", [128, 64, 2, 128], BF16)
        self.fCH_d = self.din("fCH", [128, 8, 128], BF16)
        self.fDC_d = self.din("fDC", [128, 2, 512], BF16)
        self.fCHc_d = self.din("fCHc", [128, 2, 128], BF16)
        self.blk_d = self.din("blkmask", [128, 260])
        for l in range(self.depth):
            self.L[l]['sinkb'] = self.din("sinkb%d" % l, [128, 4])
            self.L[l]['wdec'] = self.din("wdec%d" % l, [33, 256])
            self.L[l]['onb'] = self.din("onb%d" % l, [128, 64])
            self.L[l]['fw'] = self.din("fw%d" % l, [256, 256])
            self.L[l]['wo'] = self.din("wo%d" % l, [D, D])
            self.L[l]['nab'] = self.din("nab%d" % l, [128, 4 * self.NAG['ntypes'], 128], BF16)
        return self._build_rest()

    def _build_rest(self):
        nc = self.nc
        s = self.s
        from contextlib import ExitStack
        with ExitStack() as st:
            sems = {}
            for e in ENGS:
                sems[e] = st.enter_context(nc.semaphore("sem_" + e))
            for i in range(NDMA):
                sems['d%d' % i] = st.enter_context(nc.semaphore("semd%d" % i))
            self.sems = sems
            self.ps = []
            for i in range(8):
                t = nc.alloc_psum_tensor("psb%d" % i, [128, 512], F32)
                self.ps.append((t.ap(), Buf(excl=True)))
            pa = Arena(nc, 16512, 16512 + 12 * 1024)
            self.ident_f = pa.alloc([128, 128], F32, "idf")
            self.ident_b = pa.alloc([128, 128], BF16, "idb")
            self.scT = pa.alloc([128, 8, 2], F32, "scT")
            self.modT = [pa.alloc([128, 24, 2], F32, "modT%d" % l) for l in range(self.depth)]
            self.Amod = [pa.alloc([128, 8, 2], F32, "Amod%d" % l) for l in range(self.depth)]
            self.cvs = pa.alloc([128, 8, 2], F32, "cvs")
            self.b_const = Buf()
            self.b_out = Buf()
            self.b_mod = [Buf() for _ in range(self.depth)]
            self.arena = Arena(nc, 16512 + 12 * 1024, 229312)

            s.dma('sp', self.ident_f, self.ident_f_d, adds=[self.b_const])
            s.dma('sp', self.ident_b, self.ident_b_d, adds=[self.b_const])
            s.dma('sp', self.cvs, self.cvec, adds=[self.b_const])
            s.op('act', lambda e: e.activation(out=self.scT, in_=self.cvs, func=AF.Silu),
                 reads=[self.b_const], writes=[self.b_const])
            for l in range(self.depth):
                self.phase_mod(l)
                s.emit(nc, sems)
            for l in range(self.depth):
                self.phase_weights(l)
                s.emit(nc, sems)
                self.x_src = self.xin if l == 0 else self.X1
                self.b_xsrc = Buf() if l == 0 else self.dram_bufs['X1']
                self.phase_p12(l)
                s.emit(nc, sems)
                if self.stop_after == 'p12':
                    break
                self.phase_c(l)
                s.emit(nc, sems)
                if self.stop_after == 'c':
                    break
                self.phase_d(l)
                s.emit(nc, sems)
                if self.stop_after == 'cd':
                    break
                self.phase_a(l)
                s.emit(nc, sems)
                if self.stop_after == 'a':
                    break
                self.phase_b(l)
                s.emit(nc, sems)
                if self.stop_after == 'b':
                    break
                self.phase_o(l)
                s.emit(nc, sems)
            s.flush(nc, sems)
        return nc

    def phase_mod(self, l):
        nc, s = self.nc, self.s
        ar = self.arena
        ar.reset()
        Lw = self.L[l]
        wb = Rot([(ar.alloc([128, 3 * D], F32, "adaw"), Buf()) for _ in range(4)])
        row = ar.alloc([2, 3 * D], F32, "modrow")
        brow = ar.alloc([2, 3 * D], F32, "brow")
        nwT = ar.alloc([128, 8], F32, "nwT")
        tmp = ar.alloc([128, 8, 2], F32, "tmpm")
        b_small, b_row = Buf(), Buf()
        s.dma('sp', brow, Lw['ada_brow'], adds=[b_small])
        s.dma('sp', nwT, Lw['norm_wT'], adds=[b_small])
        for k in range(8):
            w, bw = wb.next()
            s.dma('sp', w, Lw['ada_w'][k * 128:(k + 1) * 128, :], writes=[bw])
            for cc in range(6):
                pk, bpk = self.ps[cc]
                s.op('pe', (lambda w=w, k=k, cc=cc, pk=pk: lambda e: e.matmul(
                    pk[0:2, 0:512], lhsT=self.scT[:, k, :], rhs=w[:, cc * 512:(cc + 1) * 512],
                    start=(k == 0), stop=(k == 7)))(), reads=[bw, self.b_const], writes=[bpk])
        for cc in range(6):
            pk, bpk = self.ps[cc]
            s.op('dve', (lambda cc=cc, pk=pk: lambda e: e.tensor_tensor(
                out=row[:, cc * 512:(cc + 1) * 512], in0=pk[0:2, 0:512], in1=brow[:, cc * 512:(cc + 1) * 512],
                op=ALU.add))(), reads=[bpk, b_small], adds=[b_row])
        psT, bpT = self.ps[6]
        for j in range(24):
            s.op('pe', (lambda j=j: lambda e: e.matmul(
                psT[:, 2 * j:2 * j + 2], lhsT=row[0:2, j * 128:(j + 1) * 128], rhs=self.ident_f[0:2, 0:2],
                start=True, stop=True))(), reads=[b_row, self.b_const], writes=[bpT])
        modT = self.modT[l]
        bmod = self.b_mod[l]
        s.op('dve', lambda e: e.tensor_copy(out=modT, in_=psT[:, 0:48].rearrange("p (j i) -> p j i", i=2)),
             reads=[bpT], writes=[bmod])
        s.op('dve', lambda e: e.tensor_scalar(out=tmp, in0=modT[:, 8:16, :], scalar1=1.0, scalar2=None,
                                              op0=ALU.add), reads=[bmod], writes=[b_small])
        Am = self.Amod[l]
        s.op('dve', lambda e: e.tensor_tensor(out=Am, in0=tmp, in1=bc(nwT.unsqueeze(2), [128, 8, 2]),
                                              op=ALU.mult), reads=[b_small], writes=[bmod])
        if self.debug and l == 0:
            s.dma('pool', self.modT_d, modT.rearrange("p j i -> p (j i)"), reads=[bmod],
                  writes=[self.dram_bufs["modT_dbg"]])

    def phase_weights(self, l):
        nc, s = self.nc, self.s
        ar = self.arena
        ar.reset()
        self.Wb = ar.alloc([128, 8, PW], BF16, "Wb")
        self.b_W = Buf()
        self.p12_base = ar.off
        stg = Rot([(ar.alloc([128, PW], F32, "stgW"), Buf()) for _ in range(4)])
        wp = self.L[l]['wp']
        for k in range(8):
            w, bw = stg.next()
            s.dma('sp', w, wp[k * 128:(k + 1) * 128, :], writes=[bw])
            s.op('act', (lambda w=w, k=k: lambda e: e.activation(
                out=self.Wb[:, k, 0:1024], in_=w[:, 0:1024], func=AF.Copy))(), reads=[bw], adds=[self.b_W])
            s.op('dve', (lambda w=w, k=k: lambda e: e.tensor_copy(
                out=self.Wb[:, k, 1024:2048], in_=w[:, 1024:2048]))(), reads=[bw], adds=[self.b_W])
            s.op('pool', (lambda w=w, k=k: lambda e: e.tensor_copy(
                out=self.Wb[:, k, 2048:PW], in_=w[:, 2048:PW]))(), reads=[bw], adds=[self.b_W])

    def phase_p12(self, l):
        nc, s = self.nc, self.s
        ar = self.arena
        ar.off = self.p12_base
        Lw = self.L[l]
        Wb, bW = self.Wb, self.b_W
        Am, modT, bmod = self.Amod[l], self.modT[l], self.b_mod[l]
        wc = ar.alloc([128, 384], F32, "wc")
        wd = ar.alloc([128, 512], F32, "wd")
        b_nw = Buf()
        s.dma('sp', wc, Lw['wc'], adds=[b_nw])
        s.dma('sp', wd, Lw['wd'], adds=[b_nw])

        def rot(shape, dt, name, n):
            return Rot([(ar.alloc(shape, dt, name), Buf()) for _ in range(n)])
        xt = rot([128, D], F32, "xt", 4)
        junk = ar.alloc([128, D], BF16, "junk")
        b_junk = Buf()
        xn = rot([128, D], F32, "xn", 3)
        hT = rot([128, 8, 128], BF16, "hT", 4)
        st = rot([128, 8], F32, "stat", 4)
        raw = rot([128, 896], F32, "raw", 3)
        nt = rot([128, 896], F32, "nt", 3)
        nt2 = rot([128, 384], F32, "nt2", 3)
        st8 = rot([128, 64], F32, "st8", 4)
        rt = rot([128, 4, 192], F32, "rt", 2)
        qkr = rot([128, 384], BF16, "qkr", 3)
        kd = rot([128, 256], BF16, "kd", 3)
        dqk = rot([128, 512], BF16, "dqk", 3)
        cs = rot([128, 4, 64], F32, "cs", 3)

        def stage_set():
            d = dict(
                G=ar.alloc([128, 4, 1024], BF16, "sG"), A=ar.alloc([128, 4, 512], BF16, "sA"),
                CT=ar.alloc([128, 4, 512], BF16, "sCT"), CV=ar.alloc([128, 4, 2, 66], BF16, "sCV"),
                DT=ar.alloc([128, 4, 512], BF16, "sDT"), DV=ar.alloc([128, 4, 4, 66], BF16, "sDV"),
                BV=ar.alloc([128, 4, 256], BF16, "sBV"), LR=ar.alloc([32, 512], F32, "sLR"))
            d['buf'] = Buf()
            d['bufB'] = Buf()
            return d
        stg = Rot([stage_set(), stage_set()])
        for sset in stg.items:
            for nm in ('CV', 'DV'):
                s.op('pool', (lambda a=sset[nm]: lambda e: e.memset(a, 1.0))(), writes=[sset['buf']])

        TP = [self.ps[0], self.ps[1]]
        MM = Rot([self.ps[2], self.ps[3], self.ps[4], self.ps[5]])
        TQ, bTQ = self.ps[6]
        TL, bTL = self.ps[7]
        TQb = TQ.bitcast(BF16)
        db = self.dram_bufs

        def stores(sset, bS, t0, n4, batch):
            u0 = t0 * 128
            n = n4 * 128
            tm = lambda dr: dr[u0:u0 + n, :].rearrange("(i p) c -> p i c", p=128)
            if batch == 1:
                bSB = sset['bufB']
                s.dma('pool', self.CT[:, :, u0:u0 + n].rearrange("a p t -> p a t"), sset['CT'][:, :, 0:n],
                      reads=[bSB], adds=[db['CT']])
                s.dma('pool', self.DT[:, :, u0:u0 + n].rearrange("a p t -> p a t"), sset['DT'][:, :, 0:n],
                      reads=[bSB], adds=[db['DT']])
                return
            s.dma('pool', tm(self.Gs), sset['G'][:, 0:n4, :], reads=[bS], adds=[db['Gs']])
            s.dma('pool', tm(self.Aqkv), sset['A'][:, 0:n4, :], reads=[bS], adds=[db['Aqkv']])
            s.dma('pool', tm(self.CV1), sset['CV'][:, 0:n4].rearrange("p i g d -> p i (g d)"), reads=[bS],
                  adds=[db['CV1']])
            s.dma('pool', tm(self.DV1), sset['DV'][:, 0:n4].rearrange("p i g d -> p i (g d)"), reads=[bS],
                  adds=[db['DV1']])
            s.dma('pool', tm(self.Bv), sset['BV'][:, 0:n4, :], reads=[bS], adds=[db['Bv']])
            s.dma('pool', self.LrT[:, u0:u0 + n], sset['LR'][0:32, 0:n], reads=[bS], adds=[db['LrT']])

        def tile_body(t0, n4, i4, sset, cs_slot):
            bS = sset['buf']
            latent = t0 >= 2
            mi = 0 if latent else 1
            ti = t0 + i4
            u0 = ti * 128
            cst, bcs = cs_slot if latent else (None, None)
            if latent and i4 == 0:
                tl0 = (t0 - 2) * 128
                s.dma('sp', cst[:, 0:n4, :], self.cs_d[tl0:tl0 + n4 * 128, :].rearrange("(i p) c -> p i c", p=128),
                      writes=[bcs])
            x_t, bx = xt.next()
            s.dma('sp', x_t, self.x_src[u0:u0 + 128, :], reads=[self.b_xsrc], writes=[bx])
            stt, bst = st.next()
            s.op('act', lambda e: e.activation(out=junk, in_=x_t, func=AF.Square, accum_out=stt[:, 0:1]),
                 reads=[bx], writes=[b_junk, bst])
            yield
            s.op('dve', lambda e: e.tensor_scalar(out=stt[:, 1:2], in0=stt[:, 0:1], scalar1=1.0 / D, scalar2=EPS,
                                                  op0=ALU.mult, op1=ALU.add), reads=[bst], writes=[bst])
            self.rsqrt1(stt[:, 1:2], stt[:, 2:3], stt[:, 3:4], bst)
            yield
            x_n, bxn = xn.next()
            s.op('act', lambda e: e.activation(out=x_n, in_=x_t, func=AF.Identity, scale=stt[:, 2:3]),
                 reads=[bx, bst], writes=[bxn])
            yield
            h_t, bh = hT.next()
            for half in range(2):
                tp, btp = TP[half]
                for kk in range(4):
                    k = half * 4 + kk
                    s.op('pe', (lambda tp=tp, kk=kk, k=k: lambda e: e.transpose(
                        out=tp[:, kk * 128:(kk + 1) * 128], in_=x_n[:, k * 128:(k + 1) * 128],
                        identity=self.ident_f))(), reads=[bxn, self.b_const], writes=[btp])
                for kk in range(4):
                    k = half * 4 + kk
                    if False:
                        pass
                    else:
                        s.op('act', (lambda tp=tp, kk=kk, k=k: lambda e: e.activation(
                            out=h_t[:, k, :], in_=tp[:, kk * 128:(kk + 1) * 128], func=AF.Identity,
                            scale=Am[:, k, mi:mi + 1], bias=modT[:, k, mi:mi + 1]))(),
                            reads=[btp, bmod], adds=[bh])
            yield
            def mm_chunk(c):
                pm, bpm = MM.next()
                for k in range(8):
                    s.op('pe', (lambda pm=pm, k=k, c=c: lambda e: e.matmul(
                        pm[:, 0:512], lhsT=h_t[:, k, :], rhs=Wb[:, k, c * 512:(c + 1) * 512],
                        start=(k == 0), stop=(k == 7)))(), reads=[bh, bW], writes=[bpm])
                return pm, bpm
            r_w, brw = raw.next()
            pm, bpm = mm_chunk(0)
            s.op('act', (lambda pm=pm: lambda e: e.activation(out=sset['A'][:, i4, :], in_=pm, func=AF.Copy))(),
                 reads=[bpm], adds=[bS])
            for c in (1, 2):
                pm, bpm = mm_chunk(c)
                s.op('act', (lambda pm=pm, c=c: lambda e: e.activation(
                    out=sset['G'][:, i4, (c - 1) * 512:c * 512], in_=pm, func=AF.Silu))(), reads=[bpm], adds=[bS])
            pm, bpm = mm_chunk(3)
            s.op('dve', (lambda pm=pm: lambda e: e.tensor_copy(out=r_w[:, 0:384], in_=pm[:, 0:384]))(),
                 reads=[bpm], adds=[brw])
            s.op('dve', (lambda pm=pm: lambda e: e.tensor_copy(
                out=sset['CV'][:, i4, :, 0:64], in_=pm[:, 384:512].rearrange("p (g d) -> p g d", g=2)))(),
                reads=[bpm], adds=[bS])
            pm, bpm = mm_chunk(4)
            s.op('act', (lambda pm=pm: lambda e: e.activation(out=r_w[:, 384:896], in_=pm[:, 0:512], func=AF.Copy))(),
                 reads=[bpm], adds=[brw])
            pm, bpm = mm_chunk(5)
            s.op('dve', (lambda pm=pm: lambda e: e.tensor_copy(
                out=sset['DV'][:, i4, :, 0:64], in_=pm[:, 0:256].rearrange("p (g d) -> p g d", g=4)))(),
                reads=[bpm], adds=[bS])
            s.op('act', (lambda pm=pm: lambda e: e.activation(out=sset['BV'][:, i4, :], in_=pm[:, 256:512],
                                                              func=AF.Copy))(), reads=[bpm], adds=[bS])
            for k in range(8):
                s.op('pe', (lambda k=k: lambda e: e.matmul(
                    TL[0:32, 0:128], lhsT=Wb[:, k, 3072:3104], rhs=h_t[:, k, :],
                    start=(k == 0), stop=(k == 7)))(), reads=[bh, bW], writes=[bTL])
            s.op('act', lambda e: e.activation(out=sset['LR'][0:32, i4 * 128:(i4 + 1) * 128], in_=TL[0:32, 0:128],
                                               func=AF.Copy), reads=[bTL], adds=[bS])
            n_t, bnt = nt.next()
            s.op('pool', lambda e: e.tensor_tensor(out=n_t, in0=r_w, in1=r_w, op=ALU.mult), reads=[brw], writes=[bnt])
            if i4 == n4 - 1:
                stores(sset, bS, t0, n4, 0)
            yield
            s8, bs8 = st8.next()
            s.op('dve', lambda e: e.tensor_reduce(out=s8[:, 0:14], in_=n_t.rearrange("p (h d) -> p h d", d=64),
                                                  axis=AX.X, op=ALU.add), reads=[bnt], writes=[bs8])
            s.op('dve', lambda e: e.tensor_scalar(out=s8[:, 0:14], in0=s8[:, 0:14], scalar1=1.0 / 64, scalar2=EPS,
                                                  op0=ALU.mult, op1=ALU.add), reads=[bs8], writes=[bs8])
            self.rsqrt(s8[:, 0:14], s8[:, 16:30], s8[:, 32:46], s8[:, 48:62], bs8)
            s.op('dve', lambda e: e.tensor_scalar(out=s8[:, 16:20], in0=s8[:, 16:20], scalar1=0.125, scalar2=None,
                                                  op0=ALU.mult), reads=[bs8], writes=[bs8])
            s.op('dve', lambda e: e.tensor_scalar(out=s8[:, 22:26], in0=s8[:, 22:26], scalar1=0.125, scalar2=None,
                                                  op0=ALU.mult), reads=[bs8], writes=[bs8])
            yield
            s.op('dve', lambda e: e.tensor_tensor(
                out=n_t.rearrange("p (h d) -> p h d", d=64), in0=r_w.rearrange("p (h d) -> p h d", d=64),
                in1=bc(s8[:, 16:30].unsqueeze(2), [128, 14, 64]), op=ALU.mult), reads=[brw, bs8], writes=[bnt])
            q2, bq2 = nt2.next()
            s.op('pool', lambda e: e.tensor_tensor(out=q2, in0=n_t[:, 0:384], in1=wc, op=ALU.mult),
                 reads=[bnt, b_nw], writes=[bq2])
            d_t, bdt = dqk.next()
            s.op('pool', lambda e: e.tensor_tensor(out=d_t, in0=n_t[:, 384:896], in1=wd, op=ALU.mult),
                 reads=[bnt, b_nw], writes=[bdt])
            yield
            q_r, bqr = qkr.next()
            if latent:
                r_t, brt = rt.next()
                cst_i = cst[:, i4, :]
                xv = q2.rearrange("p (h a f) -> p h a f", a=2, f=16)
                ov = q_r.rearrange("p (h a f) -> p h a f", a=2, f=16)

                def csb(off):
                    v = cst_i[:, off:off + 32].rearrange("p (a f) -> p a f", a=2)
                    return bc(v.unsqueeze(1), [128, 6, 2, 16])
                x1h = xv[:, :, 0, :].rearrange("p (h a) f -> p h a f", a=2)
                x2h = xv[:, :, 1, :].rearrange("p (h a) f -> p h a f", a=2)
                o1h = ov[:, :, 0, :].rearrange("p (h a) f -> p h a f", a=2)
                o2h = ov[:, :, 1, :].rearrange("p (h a) f -> p h a f", a=2)
                tv = [r_t[:, j, :].rearrange("p (h a f) -> p h a f", a=2, f=16) for j in range(4)]
                cosb, sinb = csb(0), csb(32)
                s.op('pool', lambda e: e.tensor_tensor(out=tv[0], in0=x1h, in1=cosb, op=ALU.mult),
                     reads=[bq2, bcs], adds=[brt])
                s.op('pool', lambda e: e.tensor_tensor(out=tv[1], in0=x2h, in1=sinb, op=ALU.mult),
                     reads=[bq2, bcs], adds=[brt])
                s.op('dve', lambda e: e.tensor_tensor(out=tv[2], in0=x2h, in1=cosb, op=ALU.mult),
                     reads=[bq2, bcs], adds=[brt])
                s.op('dve', lambda e: e.tensor_tensor(out=tv[3], in0=x1h, in1=sinb, op=ALU.mult),
                     reads=[bq2, bcs], adds=[brt])
                s.op('pool', lambda e: e.tensor_tensor(out=o1h, in0=tv[0], in1=tv[1], op=ALU.subtract),
                     reads=[brt], writes=[bqr])
                s.op('dve', lambda e: e.tensor_tensor(out=o2h, in0=tv[2], in1=tv[3], op=ALU.add),
                     reads=[brt], writes=[bqr])
            else:
                s.op('pool', lambda e: e.tensor_copy(out=q_r, in_=q2), reads=[bq2], writes=[bqr])
            k_d, bkd = kd.next()
            s.op('pool', lambda e: e.tensor_copy(
                out=k_d.rearrange("p (g j d) -> p g j d", g=2, j=2),
                in_=bc(q_r[:, 256:384].rearrange("p (g d) -> p g d", g=2).unsqueeze(2), [128, 2, 2, 64])),
                reads=[bqr], writes=[bkd])
            yield
            for j in range(4):
                src = q_r[:, j * 128:(j + 1) * 128] if j < 2 else k_d[:, (j - 2) * 128:(j - 1) * 128]
                s.op('pe', (lambda src=src, j=j: lambda e: e.transpose(
                    out=TQb[:, j * 128:(j + 1) * 128], in_=src, identity=self.ident_b))(),
                    reads=[bqr, bkd, self.b_const], writes=[bTQ])
            for j in range(4):
                s.op('pe', (lambda j=j: lambda e: e.transpose(
                    out=TQb[:, 512 + j * 128:512 + (j + 1) * 128], in_=d_t[:, j * 128:(j + 1) * 128],
                    identity=self.ident_b))(), reads=[bdt, self.b_const], writes=[bTQ])
            s.op('act', lambda e: e.activation(
                out=sset['CT'][:, :, i4 * 128:(i4 + 1) * 128],
                in_=TQb[:, 0:512].rearrange("p (a t) -> p a t", a=4), func=AF.Copy), reads=[bTQ], adds=[sset['bufB']])
            s.op('act', lambda e: e.activation(
                out=sset['DT'][:, :, i4 * 128:(i4 + 1) * 128],
                in_=TQb[:, 512:1024].rearrange("p (a t) -> p a t", a=4), func=AF.Copy), reads=[bTQ], adds=[sset['bufB']])
            if i4 == n4 - 1:
                stores(sset, bS, t0, n4, 1)

        gens = []
        for (t0, n4) in GROUPS:
            sset = stg.next()
            cs_slot = cs.next() if t0 >= 2 else None
            for i4 in range(n4):
                gens.append(tile_body(t0, n4, i4, sset, cs_slot))
        pipelineN(gens, 9)

    def phase_c(self, l):
        return self.attn_phase(l, 'c')

    def phase_d(self, l):
        return self.attn_phase(l, 'd')

    def attn_phase(self, l, kind):
        nc, s = self.nc, self.s
        ar = self.arena
        ar.reset()
        need_ctx = (l < DEPTH - 1)
        db = self.dram_bufs
        isC = (kind == 'c')
        G = self.NAG
        ntp = G['ntypes']
        TT, V1d, vw, ycol = (self.CT, self.CV1, 132, 512) if isC else (self.DT, self.DV1, 264, 768)
        QT = [ar.alloc([128, NT], BF16, "QT%d" % g) for g in range(2)]
        KT = [ar.alloc([128, NT], BF16, "KT%d" % g) for g in range(2)]
        V1 = ar.alloc([128, NTILE, vw], BF16, "V1")
        Gg = ar.alloc([128, NTILE, 256], BF16, "Gg")
        Ys = ar.alloc([128, NTILE, 256], BF16, "Ys")
        bQK = [Buf(), Buf()]
        bV, bG, bM = Buf(), Buf(), Buf()
        bYg = [Buf() for _ in range(NTILE)]
        for g in range(2):
            s.dma('sp', QT[g], TT[g], reads=[db['CT' if isC else 'DT']], adds=[bQK[g]])
            s.dma('sp', KT[g], TT[2 + g], reads=[db['CT' if isC else 'DT']], adds=[bQK[g]])
        self.dma_tm('sp', V1, V1d, 0, NTILE, True, [db['CV1' if isC else 'DV1']], [bV])
        self.dma_tm('sp', Gg, self.Gs[:, ycol:ycol + 256], 0, NTILE, True, [db['Gs']], [bG])
        if isC:
            msk = ar.alloc([128, 2, 128], BF16, "cmsk")
            es = ar.alloc([128, 4], F32, "ces")
            s.dma('sp', msk, self.cmask_d, adds=[bM])
            s.dma('sp', es, self.L[l]['sinkb'], adds=[bM])
            s.op('act', lambda e: e.activation(out=es, in_=es, func=AF.Exp), reads=[bM], writes=[bM])
            mtile = lambda h, typ: msk[:, typ, :]
        else:
            NB = ar.alloc([128, 4 * ntp, 128], BF16, "dNB")
            s.dma('sp', NB, self.L[l]['nab'], writes=[bM])
            for h4 in range(4):
                s.op('act', (lambda h4=h4: lambda e: e.activation(
                    out=NB[:, h4 * ntp:(h4 + 1) * ntp, :], in_=NB[:, h4 * ntp:(h4 + 1) * ntp, :], func=AF.Exp))(),
                    reads=[bM], writes=[bM])
            mtile = lambda h, typ: NB[:, h * ntp + typ, :]
        PT = Rot([(ar.alloc([128, 512], BF16, "PT"), Buf()) for _ in range(8)])
        ST = Rot([self.ps[i] for i in range(6)])
        OP = Rot([self.ps[6], self.ps[7]])
        dn = Rot([(ar.alloc([128, 8], F32, "den"), Buf()) for _ in range(4)])

        def keys_for(qt):
            kts = [(0, None), (1, None)]
            if qt >= 2:
                if isC:
                    if qt - 1 >= 2:
                        kts.append((qt - 1, 0))
                    kts.append((qt, None))
                    if qt + 1 < NTILE:
                        kts.append((qt + 1, 1))
                else:
                    n = qt - 2
                    for m in G['nbrs'][n]:
                        kts.append((m + 2, G['table'][(n, m)]))
            return kts

        def body(h, qts):
            g, j = h // 2, h % 2
            jsl = slice(j * 64, (j + 1) * 64)
            hv = g if isC else h
            nq = len(qts)
            W = 128 * nq
            per_q = [keys_for(q) for q in qts]
            union = sorted({kt for kl in per_q for kt, _ in kl})
            upos = {kt: u for u, kt in enumerate(union)}
            per_bank = 512 // W
            nbank = (len(union) + per_bank - 1) // per_bank
            banks = [ST.next() for _ in range(nbank)]
            pts = [PT.next() for _ in range(nbank)]
            q0 = qts[0] * 128
            for u, kt in enumerate(union):
                bk, bbk = banks[u // per_bank]
                c0 = (u % per_bank) * W
                s.op('pe', (lambda bk=bk, c0=c0, kt=kt: lambda e: e.matmul(
                    bk[:, c0:c0 + W], lhsT=KT[g][jsl, kt * 128:(kt + 1) * 128], rhs=QT[g][jsl, q0:q0 + W],
                    start=True, stop=True))(), reads=[bQK[g]], writes=[bbk])
            for bi in range(nbank):
                ncol = min(per_bank, len(union) - bi * per_bank) * W
                bk, bbk = banks[bi]
                pt, bpt = pts[bi]
                s.op('act', (lambda bk=bk, pt=pt, ncol=ncol: lambda e: e.activation(
                    out=pt[:, 0:ncol], in_=bk[:, 0:ncol], func=AF.Exp))(), reads=[bbk], writes=[bpt])
            nm = 0
            for qi, kl in enumerate(per_q):
                for kt, typ in kl:
                    if typ is None:
                        continue
                    u = upos[kt]
                    pt, bpt = pts[u // per_bank]
                    c0 = (u % per_bank) * W + qi * 128
                    eng = 'dve' if (isC or nm % 5 != 4) else 'pool'
                    nm += 1
                    s.op(eng, (lambda pt=pt, c0=c0, typ=typ: lambda e: e.tensor_tensor(
                        out=pt[:, c0:c0 + 128], in0=pt[:, c0:c0 + 128], in1=mtile(h, typ), op=ALU.mult))(),
                        reads=[bM, bpt], writes=[bpt])
            yield
            o, bo = OP.next()
            for qi, kl in enumerate(per_q):
                for ki, (kt, typ) in enumerate(kl):
                    u = upos[kt]
                    pt, bpt = pts[u // per_bank]
                    c0 = (u % per_bank) * W + qi * 128
                    s.op('pe', (lambda pt=pt, c0=c0, kt=kt, ki=ki, qi=qi, nk=len(kl): lambda e: e.matmul(
                        o[:, qi * 66:qi * 66 + 65], lhsT=pt[:, c0:c0 + 128], rhs=V1[:, kt, hv * 66:hv * 66 + 65],
                        start=(ki == 0), stop=(ki == nk - 1)))(), reads=[bpt, bV], writes=[bo])
            d, bd = dn.next()
            ov = o[:, 0:66 * nq].rearrange("p (q c) -> p q c", c=66)
            if isC:
                s.op('dve', lambda e: e.tensor_tensor(out=d[:, 0:nq], in0=ov[:, :, 64],
                                                      in1=bc(es[:, h:h + 1], [128, nq]), op=ALU.add),
                     reads=[bo, bM], writes=[bd])
                s.op('dve', lambda e: e.reciprocal(out=d[:, 4:4 + nq], in_=d[:, 0:nq]), reads=[bd], writes=[bd])
            else:
                s.op('dve', lambda e: e.reciprocal(out=d[:, 4:4 + nq], in_=ov[:, :, 64]), reads=[bo], writes=[bd])
            for qi, qt in enumerate(qts):
                s.op('dve', (lambda qi=qi, qt=qt: lambda e: e.scalar_tensor_tensor(
                    out=Ys[:, qt, h * 64:(h + 1) * 64], in0=o[:, qi * 66:qi * 66 + 64], scalar=d[:, 4 + qi:5 + qi],
                    in1=Gg[:, qt, h * 64:(h + 1) * 64], op0=ALU.mult, op1=ALU.mult))(),
                    reads=[bo, bd, bG], adds=[bYg[qt // 4]])

        pairs = ([[0, 1]] if need_ctx else []) + [[q, q + 1] for q in range(2, NTILE, 2)]
        gens = []
        for pr in pairs:
            for h in range(4):
                gens.append(body(h, pr))
            qt = pr[-1]
            if qt % 4 == 3 or qt == NTILE - 1:
                g0 = max(pairs[0][0], (qt // 4) * 4)
                gens.append(store_item((lambda g0=g0, qt=qt: lambda: self.dma_tm(
                    'sp', Ys, self.Ymix[:, ycol:ycol + 256], g0, qt + 1, False, [bYg[qt // 4]], [db['Ymix']]))())(2))
        pipeline2(gens)

    def phase_a(self, l):
        nc, s = self.nc, self.s
        ar = self.arena
        ar.reset()
        need_ctx = (l < DEPTH - 1)
        db = self.dram_bufs
        A = ar.alloc([128, NTILE, 512], BF16, "aQKV")
        LRf = ar.alloc([96, NT], F32, "aLRf")
        LRx = ar.alloc([96, NT], BF16, "aLRx")
        Ga = ar.alloc([128, NTILE, 256], BF16, "aG")
        Ost = ar.alloc([128, NTILE, 256], F32, "aOst")
        Ys = ar.alloc([128, NTILE, 256], BF16, "aY")
        tri = ar.alloc([128, 4, 128], F32, "tri")
        blk = ar.alloc([128, 260], F32, "blk")
        Wf = ar.alloc([96, 256], F32, "wdecf")
        Wx = ar.alloc([96, 256], BF16, "wdecx")
        bdec = ar.alloc([1, 256], F32, "bdec")
        bhl = ar.alloc([1, 2, 256], BF16, "bhl")
        ones1 = ar.alloc([1, 128], BF16, "ones1")
        trib = ar.alloc([128, 4, 128], BF16, "trib")
        onb = ar.alloc([128, 64], F32, "onb")
        bA, bLR, bG, bK = Buf(), Buf(), Buf(), Buf()
        bYc = [Buf() for _ in range(NTILE)]
        bOst = [Buf() for _ in range(NTILE)]
        self.dma_tm('sp', A, self.Aqkv, 0, NTILE, True, [db['Aqkv']], [bA])
        for r3 in range(3):
            s.dma('sp', LRf[r3 * 32:(r3 + 1) * 32, :], self.LrT, reads=[db['LrT']], adds=[bLR])
            s.dma('sp', Wf[r3 * 32:(r3 + 1) * 32, :], self.L[l]['wdec'][0:32, :], adds=[bK])
        self.dma_tm('sp', Ga, self.Gs[:, 0:256], 0, NTILE, True, [db['Gs']], [bG])
        s.dma('sp', tri, self.tri_d, adds=[bK])
        s.dma('sp', blk, self.blk_d, adds=[bK])
        s.dma('sp', bdec, self.L[l]['wdec'][32:33, :], adds=[bK])
        s.dma('sp', onb, self.L[l]['onb'], adds=[bK])
        bLX, bWX = Buf(), Buf()
        s.op('pool', lambda e: e.memset(ones1, 1.0), adds=[bWX])
        s.op('act', lambda e: e.activation(out=LRx[0:32, :], in_=LRf[0:32, :], func=AF.Copy), reads=[bLR], adds=[bLX])
        s.op('act', lambda e: e.activation(out=LRx[64:96, :], in_=LRf[64:96, :], func=AF.Copy), reads=[bLR], adds=[bLX])
        s.op('dve', lambda e: e.tensor_copy(out=LRx[32:64, :], in_=LRf[32:64, :]), reads=[bLR], adds=[bLX])
        s.op('dve', lambda e: e.tensor_tensor(out=LRx[32:64, :], in0=LRf[32:64, :], in1=LRx[32:64, :], op=ALU.subtract),
             reads=[bLR, bLX], adds=[bLX])
        s.op('pool', lambda e: e.tensor_copy(out=Wx[0:64, :], in_=Wf[0:64, :]), reads=[bK], adds=[bWX])
        s.op('pool', lambda e: e.tensor_copy(out=Wx[64:96, :], in_=Wf[64:96, :]), reads=[bK], adds=[bWX])
        s.op('pool', lambda e: e.tensor_tensor(out=Wx[64:96, :], in0=Wf[64:96, :], in1=Wx[64:96, :], op=ALU.subtract),
             reads=[bK, bWX], adds=[bWX])
        s.op('pool', lambda e: e.tensor_copy(out=bhl[:, 0, :], in_=bdec), reads=[bK], adds=[bWX])
        s.op('pool', lambda e: e.tensor_tensor(out=bhl[:, 1, :], in0=bdec, in1=bhl[:, 0, :], op=ALU.subtract),
             reads=[bK, bWX], adds=[bWX])
        s.op('pool', lambda e: e.tensor_copy(out=trib, in_=tri), reads=[bK], adds=[bWX])
        blkm = blk[:, 0:256]
        hm = blk[:, 256:260]
        Sf = [ar.alloc([128, 256], F32, "Sf%d" % d) for d in range(2)]
        Sb = [ar.alloc([128, 256], BF16, "Sb%d" % d) for d in range(2)]
        bS = [Buf(), Buf()]
        for d in range(2):
            s.op('pool', (lambda d=d: lambda e: e.memset(Sf[d], 0.0))(), writes=[bS[d]])
            s.op('pool', (lambda d=d: lambda e: e.memset(Sb[d], 0.0))(), adds=[bS[d]])

        def rot(shape, dt, name, n=4):
            return [Rot([(ar.alloc(shape, dt, name), Buf()) for _ in range(n)]) for _ in range(2)]
        gS = rot([128, 128], F32, "gS")
        gH = rot([128, 2, 128], BF16, "gH")
        eT = rot([128, 128], F32, "eT")
        Eq = rot([128, 128], F32, "Eq")
        Ek = rot([128, 128], F32, "Ek")
        Eh = rot([128, 128], F32, "Eh")
        qt_ = rot([128, 128], BF16, "qt")
        kt_ = rot([128, 128], BF16, "kt")
        kh_ = rot([128, 128], BF16, "kh")
        Q4 = rot([128, 512], BF16, "Q4")
        Pm = rot([128, 512], BF16, "Pm")
        fin = Rot([(ar.alloc([128, 3, 256], F32, "fin"), Buf()) for _ in range(3)])
        fst = Rot([(ar.alloc([128, 64], F32, "fst"), Buf()) for _ in range(3)])
        ZG = [self.ps[0], self.ps[1]]
        TRs = Rot([self.ps[2], self.ps[7]])
        AT = [self.ps[3], self.ps[4]]
        OK = [self.ps[5], self.ps[6]]

        order = [list(range(NTILE)), [1, 0] + list(range(NTILE - 1, 1, -1))]
        pos = [{c: i for i, c in enumerate(order[d])} for d in range(2)]

        def step(c, d):
            zg, bzg = ZG[d]
            at, bat = AT[d]
            ok, bok = OK[d]
            first = pos[d][c] < pos[1 - d][c] or (pos[d][c] == pos[1 - d][c] and d == 0)
            cs_ = slice(c * 128, (c + 1) * 128)
            mi_incl, mi_tail = (0, 2) if d == 0 else (1, 3)
            s.op('pe', lambda e: e.matmul(zg[:, 0:128], lhsT=LRx[:, cs_], rhs=Wx[:, d * 128:(d + 1) * 128],
                                          start=True, stop=False), reads=[bLX, bWX], writes=[bzg])
            for hl in range(2):
                s.op('pe', (lambda hl=hl: lambda e: e.matmul(
                    zg[:, 0:128], lhsT=ones1[0:1, :], rhs=bhl[0:1, hl, d * 128:(d + 1) * 128],
                    start=False, stop=(hl == 1)))(), reads=[bWX], writes=[bzg])
            e_t, bet = eT[d].next()
            g_s, bgs = gS[d].next()
            s.op('act', lambda e: e.activation(out=e_t, in_=zg[:, 0:128], func=AF.Exp, scale=-1.0),
                 reads=[bzg], writes=[bet])
            s.op('act', lambda e: e.activation(out=g_s, in_=e_t, func=AF.Ln, bias=1.0), reads=[bet], writes=[bgs])
            g_h, bgh = gH[d].next()
            s.op('act', lambda e: e.activation(out=g_h[:, 0, :], in_=g_s, func=AF.Copy), reads=[bgs], writes=[bgh])
            s.op('pool', lambda e: e.tensor_tensor(out=g_h[:, 1, :], in0=g_s, in1=g_h[:, 0, :], op=ALU.subtract),
                 reads=[bgs, bgh], writes=[bgh])
            yield
            for hl in range(2):
                s.op('pe', (lambda hl=hl: lambda e: e.matmul(
                    zg[:, 128:256], lhsT=g_h[:, hl, :], rhs=trib[:, mi_incl, :], start=(hl == 0), stop=(hl == 1)))(),
                    reads=[bgh, bWX], writes=[bzg])
            for hl in range(2):
                s.op('pe', (lambda hl=hl: lambda e: e.matmul(
                    zg[:, 256:384], lhsT=trib[:, mi_tail, :], rhs=g_h[:, hl, :], start=(hl == 0), stop=(hl == 1)))(),
                    reads=[bgh, bWX], writes=[bzg])
            eq, beq = Eq[d].next()
            ek, bek = Ek[d].next()
            eh, beh = Eh[d].next()
            s.op('act', lambda e: e.activation(out=eq, in_=zg[:, 128:256], func=AF.Exp, scale=-1.0 / 16),
                 reads=[bzg], writes=[beq])
            s.op('act', lambda e: e.activation(out=ek, in_=zg[:, 128:256], func=AF.Exp, scale=1.0 / 16),
                 reads=[bzg], writes=[bek])
            s.op('act', lambda e: e.activation(out=eh, in_=zg[:, 256:384], func=AF.Exp, scale=-1.0 / 16),
                 reads=[bzg], writes=[beh])
            yield
            tr, btr = TRs.next()
            trb = tr.bitcast(BF16)
            s.op('pe', lambda e: e.transpose(out=trb[:, 0:128], in_=A[:, c, 0:128], identity=self.ident_b),
                 reads=[bA, self.b_const], writes=[btr])
            s.op('pe', lambda e: e.transpose(out=trb[:, 128:256], in_=A[:, c, 128:256], identity=self.ident_b),
                 reads=[bA, self.b_const], writes=[btr])
            q_t, bqt = qt_[d].next()
            k_t, bkt = kt_[d].next()
            k_h, bkh = kh_[d].next()
            s.op('pool', lambda e: e.tensor_tensor(out=k_h, in0=A[:, c, 128:256], in1=eh, op=ALU.mult),
                 reads=[bA, beh], writes=[bkh])
            s.op('dve', lambda e: e.scalar_tensor_tensor(out=q_t, in0=trb[:, 0:128], scalar=32.0 ** -0.5, in1=eq,
                                                         op0=ALU.mult, op1=ALU.mult), reads=[btr, beq], writes=[bqt])
            s.op('dve', lambda e: e.tensor_tensor(out=k_t, in0=trb[:, 128:256], in1=ek, op=ALU.mult),
                 reads=[btr, bek], writes=[bkt])
            yield
            q4, bq4 = Q4[d].next()
            s.op('pool', lambda e: e.tensor_tensor(
                out=q4.rearrange("p (h t) -> p h t", h=4), in0=bc(q_t.unsqueeze(1), [128, 4, 128]),
                in1=bc(hm.unsqueeze(2), [128, 4, 128]), op=ALU.mult), reads=[bqt, bK], writes=[bq4])
            yield
            s.op('pe', lambda e: e.matmul(at[:, 0:512], lhsT=k_t, rhs=q4, start=True, stop=True),
                 reads=[bkt, bq4], writes=[bat])
            p_m, bpm = Pm[d].next()
            s.op('dve', lambda e: e.tensor_tensor(
                out=p_m.rearrange("p (h t) -> p h t", h=4), in0=at[:, 0:512].rearrange("p (h t) -> p h t", h=4),
                in1=bc(tri[:, mi_incl, :].unsqueeze(1), [128, 4, 128]), op=ALU.mult), reads=[bat, bK], writes=[bpm])
            yield
            s.op('pe', lambda e: e.matmul(ok[:, 0:256], lhsT=q_t, rhs=Sb[d], start=True, stop=False),
                 reads=[bqt, bS[d]], writes=[bok])
            for h in range(4):
                s.op('pe', (lambda h=h: lambda e: e.matmul(
                    ok[:, h * 64:(h + 1) * 64], lhsT=p_m[:, h * 128:(h + 1) * 128],
                    rhs=A[:, c, 256 + h * 64:256 + (h + 1) * 64], start=False, stop=(h == 3)))(),
                    reads=[bpm, bA], writes=[bok])
            s.op('pe', lambda e: e.matmul(ok[:, 256:512], lhsT=k_h, rhs=A[:, c, 256:512], start=True, stop=True),
                 reads=[bkh, bA], writes=[bok])
            last = 127 if d == 0 else 0
            s.op('dve', lambda e: e.scalar_tensor_tensor(
                out=Sf[d], in0=Sf[d], scalar=eq[:, last:last + 1], in1=ok[:, 256:512], op0=ALU.mult, op1=ALU.add),
                reads=[bok, beq, bS[d]], writes=[bS[d]])
            s.op('dve', lambda e: e.tensor_tensor(out=Sb[d], in0=Sf[d], in1=blkm, op=ALU.mult),
                 reads=[bK, bS[d]], writes=[bS[d]])
            if first:
                s.op('act', lambda e: e.activation(out=Ost[:, c, :], in_=ok[:, 0:256], func=AF.Copy),
                     reads=[bok], writes=[bOst[c]])
                yield
                return
            f, bf = fin.next()
            st_, bst = fst.next()
            s.op('dve', lambda e: e.tensor_tensor(out=f[:, 0, :], in0=ok[:, 0:256], in1=Ost[:, c, :], op=ALU.add),
                 reads=[bok, bOst[c]], writes=[bf])
            yield
            if need_ctx or c >= 2:
                s.op('act', lambda e: e.activation(out=f[:, 1, :], in_=f[:, 0, :], func=AF.Square),
                     reads=[bf], writes=[bf])
                s.op('dve', lambda e: e.tensor_reduce(
                    out=st_[:, 0:4], in_=f[:, 1, :].rearrange("p (h d) -> p h d", d=64), axis=AX.X, op=ALU.add),
                    reads=[bf], writes=[bst])
                s.op('dve', lambda e: e.tensor_scalar(out=st_[:, 0:4], in0=st_[:, 0:4], scalar1=1.0 / 64, scalar2=EPS,
                                                      op0=ALU.mult, op1=ALU.add), reads=[bst], writes=[bst])
                s.op('act', lambda e: e.activation(out=st_[:, 32:36], in_=st_[:, 0:4], func=AF.Ln),
                     reads=[bst], writes=[bst])
                s.op('act', lambda e: e.activation(out=st_[:, 16:20], in_=st_[:, 32:36], func=AF.Exp, scale=-0.5),
                     reads=[bst], writes=[bst])
                s.op('pool', lambda e: e.tensor_tensor(
                    out=f[:, 2, :].rearrange("p (h d) -> p h d", d=64),
                    in0=Ga[:, c, :].rearrange("p (h d) -> p h d", d=64),
                    in1=bc(onb.unsqueeze(1), [128, 4, 64]), op=ALU.mult), reads=[bG, bK], writes=[bf])
                s.op('pool', lambda e: e.tensor_tensor(
                    out=f[:, 1, :].rearrange("p (h d) -> p h d", d=64),
                    in0=f[:, 0, :].rearrange("p (h d) -> p h d", d=64),
                    in1=bc(st_[:, 16:20].unsqueeze(2), [128, 4, 64]), op=ALU.mult), reads=[bf, bst], writes=[bf])
                s.op('pool', lambda e: e.tensor_tensor(out=Ys[:, c, :], in0=f[:, 1, :], in1=f[:, 2, :], op=ALU.mult),
                     reads=[bf], writes=[bYc[c]])

        def ystore(c0, c1):
            return store_item(lambda: self.dma_tm('sp', Ys, self.Ymix[:, 0:256], c0, c1, False,
                                                  [bYc[c] for c in range(c0, c1)], [db['Ymix']]))(7)
        gens = []
        for i in range(NTILE):
            gens.append(step(order[0][i], 0))
            gens.append(step(order[1][i], 1))
            if i == 1 and need_ctx:
                gens.append(ystore(0, 2))
            if i >= 19 and i % 2 == 1:
                gens.append(ystore(i - 1, i + 1))
                gens.append(ystore(35 - i, 37 - i))
        pipelineN(gens, 7)

    def phase_b(self, l):
        nc, s = self.nc, self.s
        ar = self.arena
        ar.reset()
        need_ctx = (l < DEPTH - 1)
        db = self.dram_bufs
        X0 = ar.alloc([64, 64, 256], BF16, "fX0")
        x3_off = ar.off
        X0p = ar.alloc([64, 128, 128], BF16, "fX0p")
        X1 = ar.alloc([128, 128, 128], BF16, "fX1")
        M3 = ar.alloc([128, 64, 2, 128], BF16, "fM3")
        D1 = ar.alloc([64, 128], BF16, "fD1")
        CH = ar.alloc([128, 8, 128], BF16, "fCH")
        RT = ar.alloc([128, 2, NT], BF16, "fRT")
        Gb = ar.alloc([128, NTILE, 256], BF16, "fG")
        Ys = ar.alloc([128, NTILE, 256], BF16, "fY")
        fwf = ar.alloc([128, 2, 256], F32, "fwf")
        fwb = ar.alloc([128, 2, 256], BF16, "fwb")
        bX0, bX1, bX3, bK, bRT, bG, bY, bFW, bX0p = Buf(), Buf(), Buf(), Buf(), Buf(), Buf(), Buf(), Buf(), Buf()
        bv_l = self.Bv[NC_:NT, :].rearrange("(r c) k -> r c k", c=64)
        for q in range(4):
            s.dma('sp', X0[:, q * 16:(q + 1) * 16, :], bv_l[:, q * 16:(q + 1) * 16, :], reads=[db['Bv']], adds=[bX0])
        s.dma('sp', D1, self.fD1_d, adds=[bK])
        for q in range(4):
            s.dma('sp', M3[:, q * 16:(q + 1) * 16], self.fM3_d[:, q * 16:(q + 1) * 16], adds=[bK])
        s.dma('sp', CH, self.fCH_d, adds=[bK])
        s.dma('sp', fwf, self.L[l]['fw'].rearrange("(a p) n -> p a n", p=128), adds=[bFW])
        s.op('pool', lambda e: e.tensor_copy(out=fwb, in_=fwf), reads=[bFW], writes=[bFW])
        self.dma_tm('sp', Gb, self.Gs[:, 256:512], 0, NTILE, True, [db['Gs']], [bG])
        PS = Rot([self.ps[i] for i in range(8)])
        bYg = [Buf() for _ in range(NTILE)]
        evr = [0]

        def evac(out_ap, in_ap, rb, wb, add=False, extra=()):
            eng = 'act' if evr[0] % 2 == 0 else 'dve'
            evr[0] += 1
            kw = dict(adds=[wb] + list(extra)) if add else dict(writes=[wb])
            if eng == 'act':
                s.op('act', lambda e: e.activation(out=out_ap, in_=in_ap, func=AF.Copy), reads=[rb], **kw)
            else:
                s.op('dve', lambda e: e.tensor_copy(out=out_ap, in_=in_ap), reads=[rb], **kw)

        X0v = X0.rearrange("r c (q t) -> r q t c", t=2)
        X0pv = X0p.rearrange("r q (t c) -> r q t c", t=2)
        for qi, eng in enumerate(('dve', 'act', 'dve', 'act')):
            sl = slice(qi * 32, (qi + 1) * 32)
            if eng == 'act':
                s.op('act', (lambda sl=sl: lambda e: e.activation(out=X0pv[:, sl], in_=X0v[:, sl], func=AF.Copy))(),
                     reads=[bX0], adds=[bX0p])
            else:
                s.op(eng, (lambda sl=sl: lambda e: e.tensor_copy(out=X0pv[:, sl], in_=X0v[:, sl]))(),
                     reads=[bX0], adds=[bX0p])
        for q4 in range(32):
            bk, bbk = PS.next()
            for i in range(4):
                chp = q4 * 4 + i
                s.op('pe', (lambda chp=chp, i=i, bk=bk: lambda e: e.matmul(
                    bk[:, i * 128:(i + 1) * 128], lhsT=X0p[:, chp, :], rhs=D1, start=True, stop=True))(),
                    reads=[bX0p, bK], writes=[bbk])
            evac(X1[:, q4 * 4:(q4 + 1) * 4, :], bk[:, 0:512].rearrange("p (a n) -> p a n", a=4), bbk, bX1, add=True)
        X3 = ar.view([128, 64, 2, 128], BF16, x3_off - 64 * 256 * 2)
        assert x3_off - 64 * 256 * 2 == self.arena.base
        X1v = X1.rearrange("p q (k z) -> p k z q", z=2)
        for k4 in range(16):
            bkA, bbA = PS.next()
            bkB, bbB = PS.next()
            for i in range(4):
                k1 = k4 * 4 + i
                for z in range(2):
                    for ch2, (bk, bbk) in enumerate(((bkA, bbA), (bkB, bbB))):
                        ps_ = slice(ch2 * 64, (ch2 + 1) * 64)
                        s.op('pe', (lambda k1=k1, z=z, ps_=ps_, bk=bk, i=i: lambda e: e.matmul(
                            bk[:, i * 128:(i + 1) * 128], lhsT=X1v[ps_, k1, z, :], rhs=M3[ps_, k1, z, :],
                            start=(z == 0), stop=(z == 1)))(), reads=[bX1, bK], writes=[bbk])
            for ch2, (bk, bbk) in enumerate(((bkA, bbA), (bkB, bbB))):
                evac(X3[:, k4 * 4:(k4 + 1) * 4, ch2, :], bk[:, 0:512].rearrange("p (a n) -> p a n", a=4),
                     bbk, bX3, add=True, extra=[bX0])
        X3v = X3.rearrange("p k t (j z) -> p t z k j", z=2)
        RTl = RT[:, :, NC_:NT].rearrange("p a (j k) -> p a k j", k=64)
        for mc in range(2):
            for kb in range(8):
                bk, bbk = PS.next()
                n = 0
                for ch2 in range(2):
                    for z in range(2):
                        s.op('pe', (lambda mc=mc, kb=kb, ch2=ch2, z=z, n=n, bk=bk: lambda e: e.matmul(
                            bk[:, 0:512], lhsT=CH[:, mc * 4 + ch2 * 2 + z, :],
                            rhs=X3v[:, ch2, z, kb * 8:(kb + 1) * 8, :], start=(n == 0), stop=(n == 3)))(),
                            reads=[bX3, bK], writes=[bbk])
                        n += 1
                evac(RTl[:, mc, kb * 8:(kb + 1) * 8, :], bk[:, 0:512].rearrange("p (k j) -> p k j", k=8),
                     bbk, bRT, add=True)
        if need_ctx:
            Vc = ar.alloc([128, 2, 256], BF16, "fVc")
            DC = ar.alloc([128, 2, 512], BF16, "fDC")
            CHc = ar.alloc([128, 2, 128], BF16, "fCHc")
            X3c = ar.alloc([128, 2, 512], BF16, "fX3c")
            bC, bX3c = Buf(), Buf()
            s.dma('sp', Vc, self.Bv[0:NC_, :].rearrange("(i p) k -> p i k", p=128), reads=[db['Bv']], adds=[bC])
            s.dma('sp', DC, self.fDC_d, adds=[bC])
            s.dma('sp', CHc, self.fCHc_d, adds=[bC])
            for cc in range(2):
                bk, bbk = PS.next()
                for i in range(2):
                    s.op('pe', (lambda cc=cc, i=i, bk=bk: lambda e: e.matmul(
                        bk[:, 0:512], lhsT=Vc[:, i, cc * 128:(cc + 1) * 128], rhs=DC[:, i, :],
                        start=(i == 0), stop=(i == 1)))(), reads=[bC], writes=[bbk])
                evac(X3c[:, cc, :], bk[:, 0:512], bbk, bX3c, add=True)
            X3cv = X3c.rearrange("p a (k z) -> p a z k", z=2)
            for cc in range(2):
                bk, bbk = PS.next()
                for z in range(2):
                    s.op('pe', (lambda cc=cc, z=z, bk=bk: lambda e: e.matmul(
                        bk[:, 0:256], lhsT=CHc[:, z, :], rhs=X3cv[:, cc, z, :], start=(z == 0), stop=(z == 1)))(),
                        reads=[bX3c, bC], writes=[bbk])
                evac(RT[:, cc, 0:NC_], bk[:, 0:256], bbk, bRT, add=True)
        t0 = 0 if need_ctx else 2
        for ti in range(t0, NTILE):
            bk, bbk = PS.next()
            for mc in range(2):
                s.op('pe', (lambda ti=ti, mc=mc, bk=bk: lambda e: e.matmul(
                    bk[:, 0:256], lhsT=RT[:, mc, ti * 128:(ti + 1) * 128], rhs=fwb[:, mc, :],
                    start=(mc == 0), stop=(mc == 1)))(), reads=[bRT, bFW], writes=[bbk])
            s.op('dve', (lambda ti=ti, bk=bk: lambda e: e.tensor_tensor(
                out=Ys[:, ti, :], in0=bk[:, 0:256], in1=Gb[:, ti, :], op=ALU.mult))(), reads=[bbk, bG],
                adds=[bYg[ti // 4]])
            if ti % 4 == 3 or ti == NTILE - 1:
                self.dma_tm('sp', Ys, self.Ymix[:, 256:512], max(t0, (ti // 4) * 4), ti + 1, False,
                            [bYg[ti // 4]], [db['Ymix']])

    def phase_o(self, l):
        nc, s = self.nc, self.s
        ar = self.arena
        ar.reset()
        last = (l == DEPTH - 1)
        db = self.dram_bufs
        modT, bmod = self.modT[l], self.b_mod[l]
        nmod = 1 if last else 2
        gcol = ar.alloc([128, 8, 2, 128], F32, "gcol")
        GB = ar.alloc([128, 2, D], F32, "GB")
        Wo = [ar.alloc([128, 8, D], BF16, "Wo%d" % i) for i in range(nmod)]
        bgc, bGB, bWo = Buf(), Buf(), Buf()
        for i in range(nmod):
            s.op('dve', (lambda i=i: lambda e: e.tensor_copy(
                out=gcol[:, :, i, :], in_=bc(modT[:, 16:24, i:i + 1], [128, 8, 128])))(), reads=[bmod], adds=[bgc])
        for i in range(nmod):
            for hf in range(2):
                bk, bbk = self.ps[i * 2 + hf]
                for kk in range(4):
                    k = hf * 4 + kk
                    s.op('pe', (lambda i=i, k=k, kk=kk, bk=bk: lambda e: e.matmul(
                        bk[:, kk * 128:(kk + 1) * 128], lhsT=gcol[:, k, i, :], rhs=self.ident_f,
                        start=True, stop=True))(), reads=[bgc, self.b_const], writes=[bbk])
                s.op('act', (lambda i=i, hf=hf, bk=bk: lambda e: e.activation(
                    out=GB[:, i, hf * 512:(hf + 1) * 512], in_=bk[:, 0:512], func=AF.Copy))(),
                    reads=[bbk], adds=[bGB])
        wst = Rot([(ar.alloc([128, D], F32, "wst"), Buf()) for _ in range(4)])
        for mk in range(8):
            w, bw = wst.next()
            s.dma('sp', w, self.L[l]['wo'][mk * 128:(mk + 1) * 128, :], writes=[bw])
            for i in range(nmod):
                eng = 'dve' if i == 0 else 'pool'
                s.op(eng, (lambda i=i, mk=mk, w=w: lambda e: e.tensor_tensor(
                    out=Wo[i][:, mk, :], in0=w, in1=GB[:, i, :], op=ALU.mult))(), reads=[bw, bGB], adds=[bWo])
        yt = Rot([(ar.alloc([128, 2, D], BF16, "yt"), Buf()) for _ in range(4)])
        xt = Rot([(ar.alloc([128, 2, D], F32, "xt"), Buf()) for _ in range(4)])
        yT = Rot([(ar.alloc([128, 8, 128], BF16, "yT"), Buf()) for _ in range(4)])
        xo = Rot([(ar.alloc([128, 2, D], F32, "xo"), Buf()) for _ in range(4)])
        TPs = Rot([self.ps[4], self.ps[5]])
        MMs = Rot([self.ps[0], self.ps[1], self.ps[2], self.ps[3], self.ps[6], self.ps[7]])
        src = self.xin if l == 0 else self.X1
        bsrc = Buf() if l == 0 else db['X1']
        t0 = 2 if last else 0

        def tile2(ti):
            u0 = ti * 128
            wi = 0 if ti >= 2 else 1
            y_t, byt = yt.next()
            x_t, bxt = xt.next()
            tm = lambda dr: dr[u0:u0 + 256, :].rearrange("(i p) c -> p i c", p=128)
            s.dma('sp', y_t, tm(self.Ymix), reads=[db['Ymix']], writes=[byt])
            s.dma('sp', x_t, tm(src), reads=[bsrc], writes=[bxt])
            yTs = []
            for i in range(2):
                tp, btp = TPs.next()
                tpb = tp.bitcast(BF16)
                for k in range(8):
                    s.op('pe', (lambda k=k, i=i, tpb=tpb: lambda e: e.transpose(
                        out=tpb[:, k * 128:(k + 1) * 128], in_=y_t[:, i, k * 128:(k + 1) * 128],
                        identity=self.ident_b))(), reads=[byt, self.b_const], writes=[btp])
                y_T, byT = yT.next()
                s.op('act', (lambda y_T=y_T, tpb=tpb: lambda e: e.activation(
                    out=y_T.rearrange("p k t -> p (k t)"), in_=tpb[:, 0:1024], func=AF.Copy))(),
                    reads=[btp], writes=[byT])
                yTs.append((y_T, byT))
            yield
            x_o, bxo = xo.next()
            for i in range(2):
                y_T, byT = yTs[i]
                for nc_ in range(2):
                    bk, bbk = MMs.next()
                    for k in range(8):
                        s.op('pe', (lambda k=k, nc_=nc_, bk=bk, y_T=y_T: lambda e: e.matmul(
                            bk[:, 0:512], lhsT=y_T[:, k, :], rhs=Wo[wi][:, k, nc_ * 512:(nc_ + 1) * 512],
                            start=(k == 0), stop=(k == 7)))(), reads=[byT, bWo], writes=[bbk])
                    s.op('dve', (lambda nc_=nc_, bk=bk, i=i: lambda e: e.tensor_tensor(
                        out=x_o[:, i, nc_ * 512:(nc_ + 1) * 512], in0=bk[:, 0:512],
                        in1=x_t[:, i, nc_ * 512:(nc_ + 1) * 512], op=ALU.add))(), reads=[bbk, bxt], adds=[bxo])
            if last:
                s.dma('pool', self.out[u0 - NC_:u0 - NC_ + 256, :].rearrange("(i p) c -> p i c", p=128), x_o,
                      reads=[bxo], adds=[self.b_out])
            else:
                s.dma('pool', self.X1[u0:u0 + 256, :].rearrange("(i p) c -> p i c", p=128), x_o,
                      reads=[bxo], adds=[db['X1']])

        pipeline2([tile2(ti) for ti in range(t0, NTILE, 2)])

    def rsqrt1(self, v, y, t1, buf):
        s = self.s
        I32 = mybir.dt.int32
        vi, yi, t1i = v.bitcast(I32), y.bitcast(I32), t1.bitcast(I32)
        s.op('dve', lambda e: e.tensor_single_scalar(out=t1i, in_=vi, scalar=1, op=ALU.arith_shift_right),
             reads=[buf], writes=[buf])
        s.op('dve', lambda e: e.tensor_scalar(out=yi, in0=t1i, scalar1=-1.0, scalar2=1597463007.0,
                                              op0=ALU.mult, op1=ALU.add), reads=[buf], writes=[buf])
        for _ in range(2):
            s.op('dve', lambda e: e.scalar_tensor_tensor(out=t1, in0=y, scalar=v, in1=y, op0=ALU.mult, op1=ALU.mult),
                 reads=[buf], writes=[buf])
            s.op('dve', lambda e: e.tensor_scalar(out=t1, in0=t1, scalar1=-0.5, scalar2=1.5,
                                                  op0=ALU.mult, op1=ALU.add), reads=[buf], writes=[buf])
            s.op('dve', lambda e: e.tensor_tensor(out=y, in0=y, in1=t1, op=ALU.mult), reads=[buf], writes=[buf])

    def rsqrt(self, v, y, t1, t2, buf):
        s = self.s
        I32 = mybir.dt.int32
        vi, yi, t1i = v.bitcast(I32), y.bitcast(I32), t1.bitcast(I32)
        s.op('dve', lambda e: e.tensor_single_scalar(out=t1i, in_=vi, scalar=1, op=ALU.arith_shift_right),
             reads=[buf], writes=[buf])
        s.op('dve', lambda e: e.tensor_scalar(out=yi, in0=t1i, scalar1=-1.0, scalar2=1597463007.0,
                                              op0=ALU.mult, op1=ALU.add), reads=[buf], writes=[buf])
        for _ in range(2):
            s.op('dve', lambda e: e.tensor_tensor(out=t1, in0=v, in1=y, op=ALU.mult), reads=[buf], writes=[buf])
            s.op('dve', lambda e: e.tensor_tensor(out=t2, in0=t1, in1=y, op=ALU.mult), reads=[buf], writes=[buf])
            s.op('dve', lambda e: e.tensor_scalar(out=t2, in0=t2, scalar1=-0.5, scalar2=1.5,
                                                  op0=ALU.mult, op1=ALU.add), reads=[buf], writes=[buf])
            s.op('dve', lambda e: e.tensor_tensor(out=y, in0=y, in1=t2, op=ALU.mult), reads=[buf], writes=[buf])

    def _norm_heads(self, pm, bpm, nh, wt, b_w, sq, nt, st8, out_f32, rot_out=None, out=None):
        s = self.s
        w = nh * 64
        sq_t, bsq = sq.next()
        s.op('act', lambda e: e.activation(out=sq_t[:, 0:w], in_=pm[:, 0:w], func=AF.Square),
             reads=[bpm], writes=[bsq])
        s8, bs8 = st8.next()
        s.op('dve', lambda e: e.tensor_reduce(out=s8[:, 0:nh], in_=sq_t[:, 0:w].rearrange("p (h d) -> p h d", d=64),
                                              axis=AX.X, op=ALU.add), reads=[bsq], writes=[bs8])
        s.op('dve', lambda e: e.tensor_scalar(out=s8[:, 0:nh], in0=s8[:, 0:nh], scalar1=1.0 / 64, scalar2=EPS,
                                              op0=ALU.mult, op1=ALU.add), reads=[bs8], writes=[bs8])
        s.op('dve', lambda e: e.tensor_scalar(out=s8[:, 8:8 + nh], in0=s8[:, 0:nh], scalar1=-0.5, scalar2=None,
                                              op0=ALU.pow), reads=[bs8], writes=[bs8])
        n_t, bnt = nt.next()
        s.op('dve', lambda e: e.tensor_tensor(
            out=n_t[:, 0:w].rearrange("p (h d) -> p h d", d=64), in0=pm[:, 0:w].rearrange("p (h d) -> p h d", d=64),
            in1=bc(s8[:, 8:8 + nh].unsqueeze(2), [128, nh, 64]), op=ALU.mult), reads=[bpm, bs8], writes=[bnt])
        if out_f32:
            o, bo = rot_out.next()
        else:
            o, bo = out
        s.op('pool', lambda e: e.tensor_tensor(out=o[:, 0:w], in0=n_t[:, 0:w], in1=wt[:, 0:w], op=ALU.mult),
             reads=[bnt, b_w], writes=[bo])
        self._last_norm = (o, bo)

    @property
    def xin_l(self):
        return self.xin


def _col_perm():
    off = {}
    o = 0
    for name, w in (("a_q", 128), ("a_k", 128), ("a_v", 256), ("a_g", 256), ("a_lr", 32), ("b_v", 256),
                    ("b_g", 256), ("c_q", 256), ("c_k", 128), ("c_v", 128), ("c_g", 256), ("d_q", 256),
                    ("d_k", 256), ("d_v", 256), ("d_g", 256)):
        off[name] = (o, w)
        o += w
    order = ["a_q", "a_k", "a_v", "a_g", "b_g", "c_g", "d_g", "c_q", "c_k", "c_v", "d_q", "d_k", "d_v", "b_v", "a_lr"]
    perm = []
    for nm in order:
        a, w = off[nm]
        perm.extend(range(a, a + w))
    return np.array(perm)


def _rope_table():
    t = np.arange(NL)
    row = (t // 64).astype(np.float32)
    col = (t % 64).astype(np.float32)
    inv = (10000.0 ** (-np.arange(0, 32, 2, dtype=np.float32) / 32)).astype(np.float32)
    ang = np.stack([row[:, None] * inv, col[:, None] * inv], axis=1)
    cs = np.concatenate([np.cos(ang).reshape(NL, 32), np.sin(ang).reshape(NL, 32)], axis=1)
    return cs.astype(np.float32)


def na_geometry():
    r = np.arange(64)
    row_start = np.clip(r - 4, 0, 56)
    col_start = np.clip(r - 8, 0, 48)
    kk = np.arange(128)
    ka, kc = (kk // 64)[:, None], (kk % 64)[:, None]
    qa, qc = (kk // 64)[None, :], (kk % 64)[None, :]
    vcol = (kc >= col_start[qc]) & (kc < col_start[qc] + 16)
    dx = np.clip(kc - qc, -15, 15) + 15
    sigs, table, nbrs, mats = {}, {}, [], []
    for n in range(32):
        nb = []
        rr = 2 * n + qa
        rs = row_start[rr]
        for m in range(32):
            a = 2 * m + ka
            valid = (a >= rs) & (a < rs + 8) & vcol
            if not valid.any():
                continue
            dy = np.where(valid, a - rr + 7, 0)
            sig = (valid.tobytes(), dy.tobytes())
            if sig not in sigs:
                sigs[sig] = len(mats)
                mats.append((valid, dy))
            table[(n, m)] = sigs[sig]
            nb.append(m)
        nbrs.append(nb)
    return dict(ntypes=len(mats), table=table, nbrs=nbrs, mats=mats, dx=dx)


def na_bias_mats(rel_bias, G):
    out = np.empty((128, 4 * G['ntypes'], 128), dtype=np.float32)
    for h in range(4):
        for t, (valid, dy) in enumerate(G['mats']):
            out[:, h * G['ntypes'] + t, :] = np.where(valid, rel_bias[h][dy, G['dx']], NEG)
    return out.astype(ml_dtypes.bfloat16)


def fnet_consts():
    bf = ml_dtypes.bfloat16
    r = np.arange(64, dtype=np.float64)
    k1 = np.arange(64, dtype=np.float64)
    a = 2 * np.pi * np.outer(r, k1) / 64.0
    D1 = np.stack([np.cos(a), -np.sin(a)], axis=2).reshape(64, 128)
    c = np.arange(64, dtype=np.float64)[:, None, None]
    kk1 = np.arange(64, dtype=np.float64)[None, :, None]
    k2 = np.arange(64, dtype=np.float64)[None, None, :]
    th = 2 * np.pi * c * (kk1 + 64 * k2) / 4096.0
    m3r, m3i = np.cos(th) / 512.0, -np.sin(th) / 512.0
    ra = np.stack([m3r, m3i], axis=3)
    rb = np.stack([-m3i, m3r], axis=3)
    M3 = np.stack([ra, rb], axis=2).reshape(64, 64, 2, 128)
    M3 = np.concatenate([M3, M3], axis=0)
    j = np.arange(64, dtype=np.float64)
    ph = 2 * np.pi * np.outer(j, j) / 64.0
    C, S = np.cos(ph), np.sin(ph)
    CH = np.zeros((128, 2, 2, 2, 2, 64))
    for chp in range(128):
        g, jj = chp // 32, chp % 32
        for ch2 in range(2):
            CH[chp, g // 2, ch2, 0, g % 2, :] = C[2 * jj + ch2]
            CH[chp, g // 2, ch2, 1, g % 2, :] = S[2 * jj + ch2]
    CH = CH.reshape(128, 8, 128)
    t = np.arange(256, dtype=np.float64)
    ac = 2 * np.pi * np.outer(t, t) / 256.0
    DC = np.stack([np.cos(ac), -np.sin(ac)], axis=2).reshape(2, 128, 512).transpose(1, 0, 2) / 128.0
    CHc = np.zeros((128, 2, 2, 64))
    for p_ in range(128):
        CHc[p_, 0, p_ // 64, :] = C[p_ % 64]
        CHc[p_, 1, p_ // 64, :] = S[p_ % 64]
    CHc = CHc.reshape(128, 2, 128)
    return dict(fD1=D1.astype(bf), fM3=M3.astype(bf), fCH=CH.astype(bf), fDC=np.ascontiguousarray(DC).astype(bf),
                fCHc=CHc.astype(bf))


_FC = {}


def make_core_inputs(b, inp, depth=DEPTH):
    f = lambda a: np.ascontiguousarray(np.asarray(a, dtype=np.float32))
    m = {}
    m["xin"] = f(np.concatenate([np.asarray(inp["ctx"][b]), np.asarray(inp["x"][b])], axis=0))
    cv = np.stack([np.asarray(inp["c"][b]).reshape(8, 128).T, np.asarray(inp["c_ctx"]).reshape(8, 128).T], axis=2)
    m["cvec"] = f(cv)
    m["ident_f"] = np.eye(128, dtype=np.float32)
    m["ident_b"] = np.eye(128).astype(ml_dtypes.bfloat16)
    m["cs_tab"] = _rope_table()
    jj, ii = np.arange(128)[:, None], np.arange(128)[None, :]
    m["cmask"] = np.stack([(ii <= jj), (jj <= ii)], axis=1).astype(ml_dtypes.bfloat16)
    G = na_geometry()
    if not _FC:
        _FC.update(fnet_consts())
    m.update(_FC)
    ss, tt = np.arange(128)[:, None], np.arange(128)[None, :]
    m["tri"] = np.stack([ss <= tt, ss >= tt, ss > tt, ss < tt], axis=1).astype(np.float32)
    hd = np.arange(128)[:, None] // 32
    m["blkmask"] = np.concatenate([(hd == (np.arange(256)[None, :] // 64)), (hd == np.arange(4)[None, :])],
                                  axis=1).astype(np.float32)
    perm = _col_perm()
    for l in range(depth):
        m["ada_w%d" % l] = f(inp["ada_w"][l])
        m["ada_brow%d" % l] = f(np.broadcast_to(np.asarray(inp["ada_b"][l])[None, :], (2, 3 * D)))
        m["norm_wT%d" % l] = f(np.asarray(inp["norm_w"][l]).reshape(8, 128).T)
        m["wp%d" % l] = f(np.asarray(inp["w_in"][l])[:, perm])
        qn, kn = np.asarray(inp["swa_q_norm"][l]), np.asarray(inp["swa_k_norm"][l])
        m["wc%d" % l] = f(np.broadcast_to(np.concatenate([np.tile(qn, 4), np.tile(kn, 2)])[None, :], (128, 384)))
        qn, kn = np.asarray(inp["na_q_norm"][l]), np.asarray(inp["na_k_norm"][l])
        m["wd%d" % l] = f(np.broadcast_to(np.concatenate([np.tile(qn, 4), np.tile(kn, 4)])[None, :], (128, 512)))
        wdec = np.zeros((33, 256), np.float32)
        wdec[0:16, 0:128] = np.asarray(inp["gla_dec_w"][l][0])
        wdec[16:32, 128:256] = np.asarray(inp["gla_dec_w"][l][1])
        wdec[32, :] = np.asarray(inp["gla_dec_b"][l]).reshape(256)
        m["wdec%d" % l] = wdec
        m["fw%d" % l] = f(inp["fnet_w"][l])
        m["wo%d" % l] = f(inp["w_out"][l])
        m["onb%d" % l] = f(np.broadcast_to(np.asarray(inp["gla_out_norm"][l])[None, :], (128, 64)))
        m["sinkb%d" % l] = f(np.broadcast_to(np.asarray(inp["swa_sink"][l])[None, :], (128, 4)))
        m["nab%d" % l] = na_bias_mats(np.asarray(inp["na_rel_bias"][l], dtype=np.float32), G)
    return m


_CACHE = {}


def kernel(**inputs):
    if "nc" not in _CACHE:
        _CACHE["nc"] = Builder().build()
    nc = _CACHE["nc"]
    ncores = int(os.environ.get("K_NCORES", "8"))
    in_maps = [make_core_inputs(i % 4, inputs) for i in range(ncores)]
    res = run_bass_kernel_spmd(nc, in_maps, core_ids=list(range(ncores)))
    out = np.stack([np.asarray(res.results[b]["out"], dtype=np.float32) for b in range(4)], axis=0)
    return out
```
